# Optimizing a Trainium2 kernel written in Bass

```python
import math
import jax
import jax.numpy as jnp
from jax import lax
import numpy as np

D_MODEL = 1024
BATCH = 32
SEQ = 256
DEPTH = 2
DEC_BATCH = 8
DEC_SEQ = 1024
PAST_LEN = 256

GRID_W = 64
HEAD_DIM = 64
ROPE_BASE = 10000.0
EPS = 1e-6
GN_EPS = 1e-5
RET_HEADS = 8
RET_DK = 64
RET_DV = 128
RET_CHUNK = 128
GQA_HEADS = 8
GQA_KV_HEADS = 2
Q_BLOCK = 128
NA_HEADS = 16
NA_KH = 8
NA_KW = 16
NA_WIDTH = NA_HEADS * HEAD_DIM
D_FF = 2816
CONV_WIDTH = 3
N_EVEN = (DEPTH + 1) // 2
N_ODD = DEPTH // 2
EVEN_SPLITS = (RET_HEADS * RET_DK, RET_HEADS * RET_DK, RET_HEADS * RET_DV, RET_HEADS * RET_DV,
               GQA_HEADS * HEAD_DIM, GQA_KV_HEADS * HEAD_DIM, GQA_KV_HEADS * HEAD_DIM)
EVEN_IN = 3840
EVEN_OUT_IN = 1536

kernel_name = 'hybrid_diffusion_retention_gqa_natten_step'

F32 = jnp.float32


def rms_norm(x, g):
    x32 = x.astype(F32)
    y = x32 * lax.rsqrt(jnp.mean(x32 * x32, axis=-1, keepdims=True) + EPS)
    return (y * g.astype(F32)).astype(x.dtype)


def to_heads(x, n, d):
    b, t, _ = x.shape
    return x.reshape(b, t, n, d).transpose(0, 2, 1, 3)


def from_heads(x):
    b, h, t, d = x.shape
    return x.transpose(0, 2, 1, 3).reshape(b, t, h * d)


def axial_angles(n, head_dim):
    t = jnp.arange(n)
    row = (t // GRID_W).astype(F32)
    col = (t % GRID_W).astype(F32)
    half = head_dim // 2
    inv = ROPE_BASE ** (-jnp.arange(0, half, 2, dtype=F32) / half)
    return row[:, None] * inv, col[:, None] * inv


def _rotate(x, ang):
    x1, x2 = jnp.split(x, 2, axis=-1)
    cos = jnp.cos(ang).astype(x.dtype)
    sin = jnp.sin(ang).astype(x.dtype)
    return jnp.concatenate([x1 * cos - x2 * sin, x2 * cos + x1 * sin], axis=-1)


def apply_axial_rope(x, ang):
    ang_r, ang_c = ang
    half = x.shape[-1] // 2
    return jnp.concatenate([_rotate(x[..., :half], ang_r), _rotate(x[..., half:], ang_c)], axis=-1)


def retention_chunkwise(q, k, v, decay_logit, s0):
    b, h, n, _ = q.shape
    dv = v.shape[-1]
    c = RET_CHUNK
    nc = n // c
    log_g = jax.nn.log_sigmoid(decay_logit.astype(F32))
    pos = jnp.arange(c, dtype=F32)
    diff = pos[:, None] - pos[None, :]
    inner_decay = jnp.where(diff >= 0, jnp.exp(log_g[:, None, None] * jnp.maximum(diff, 0.0)),
                            0.0).astype(q.dtype)
    q_decay = jnp.exp(log_g[:, None] * (pos + 1.0))[..., None].astype(q.dtype)
    k_decay = jnp.exp(log_g[:, None] * (c - 1.0 - pos))[..., None].astype(q.dtype)
    chunk_decay = jnp.exp(log_g * c)[:, None, None].astype(q.dtype)

    def chunks(x):
        return x.reshape(b, h, nc, c, x.shape[-1]).transpose(2, 0, 1, 3, 4)

    def step(s, inp):
        qc, kc, vc = inp
        scores = jnp.einsum('bhid,bhjd->bhij', qc, kc) * inner_decay
        o = (jnp.einsum('bhij,bhje->bhie', scores, vc)
             + jnp.einsum('bhid,bhde->bhie', qc * q_decay, s))
        s = s * chunk_decay + jnp.einsum('bhjd,bhje->bhde', kc * k_decay, vc)
        return s, o

    s_fin, o = lax.scan(step, s0.astype(q.dtype), (chunks(q), chunks(k), chunks(v)))
    return o.transpose(1, 2, 0, 3, 4).reshape(b, h, n, dv), s_fin


def retention_group_norm(o, gain):
    o32 = o.astype(F32)
    mu = jnp.mean(o32, axis=-1, keepdims=True)
    var = jnp.mean(jnp.square(o32 - mu), axis=-1, keepdims=True)
    y = (o32 - mu) * lax.rsqrt(var + GN_EPS)
    return (from_heads(y) * gain.astype(F32)).astype(o.dtype)


def attention_blocked(q, k, v):
    b, h, n, d = q.shape
    kv = k.shape[1]
    g = h // kv
    nb = n // Q_BLOCK
    qb = q.reshape(b, kv, g, nb, Q_BLOCK, d).transpose(3, 0, 1, 2, 4, 5)
    scale = d ** -0.5

    def block(qi):
        s = jnp.einsum('bkgqd,bkmd->bkgqm', qi, k).astype(F32) * scale
        p = jax.nn.softmax(s, axis=-1).astype(v.dtype)
        return jnp.einsum('bkgqm,bkmd->bkgqd', p, v)

    o = lax.map(block, qb)
    return o.transpose(1, 2, 3, 0, 4, 5).reshape(b, h, n, d)


def na_tables(rows):
    kh = min(NA_KH, rows)
    kw = NA_KW
    r = np.arange(rows)
    cidx = np.arange(GRID_W)
    rs = np.clip(r - kh // 2, 0, rows - kh)
    cs = np.clip(cidx - kw // 2, 0, GRID_W - kw)
    key_r = rs[:, None] + np.arange(kh)
    key_c = cs[:, None] + np.arange(kw)
    idx = (key_r[:, None, :, None] * GRID_W + key_c[None, :, None, :]).reshape(rows, GRID_W, kh * kw)
    rel_r = key_r - r[:, None] + NA_KH - 1
    rel_c = key_c - cidx[:, None] + NA_KW - 1
    return (jnp.asarray(idx, jnp.int32), jnp.asarray(rel_r, jnp.int32), jnp.asarray(rel_c, jnp.int32))


def neighbourhood_attention(q, k, v, ctx_k, ctx_v, rpb):
    b, h, n, d = q.shape
    rows = n // GRID_W
    idx, rel_r, rel_c = na_tables(rows)
    n_win = idx.shape[-1]
    scale = d ** -0.5
    q_rows = q.reshape(b, h, rows, GRID_W, d).transpose(2, 0, 1, 3, 4)

    def row_block(inp):
        qi, idx_i, rel_r_i = inp
        kg = jnp.take(k, idx_i, axis=2)
        vg = jnp.take(v, idx_i, axis=2)
        bias = rpb[:, rel_r_i[None, :, None], rel_c[:, None, :]].reshape(h, GRID_W, n_win)
        s_win = jnp.einsum('bhwd,bhwkd->bhwk', qi, kg).astype(F32) * scale + bias.astype(F32)
        s_ctx = jnp.einsum('bhwd,bhmd->bhwm', qi, ctx_k).astype(F32) * scale
        p = jax.nn.softmax(jnp.concatenate([s_win, s_ctx], axis=-1), axis=-1).astype(v.dtype)
        return (jnp.einsum('bhwk,bhwkd->bhwd', p[..., :n_win], vg)
                + jnp.einsum('bhwm,bhmd->bhwd', p[..., n_win:], ctx_v))

    o = lax.map(row_block, (q_rows, idx, rel_r))
    return o.transpose(1, 2, 0, 3, 4).reshape(b, h, n, d)


def even_mixer(h, w_in, w_out, decay_f, decay_b, gn, q_gain, k_gain, s0_f, s0_b, ctx_k, ctx_v, ang):
    split_idx = [int(i) for i in np.cumsum(EVEN_SPLITS)[:-1]]
    qr, kr, vr, gr, qa, ka, va = jnp.split(h @ w_in, split_idx, axis=-1)
    qr = to_heads(qr, RET_HEADS, RET_DK)
    kr = to_heads(kr, RET_HEADS, RET_DK) * (RET_DK ** -0.5)
    vr = to_heads(vr, RET_HEADS, RET_DV)
    qa = rms_norm(to_heads(qa, GQA_HEADS, HEAD_DIM), q_gain)
    ka = rms_norm(to_heads(ka, GQA_KV_HEADS, HEAD_DIM), k_gain)
    va = to_heads(va, GQA_KV_HEADS, HEAD_DIM)
    if ang is None:
        keys, vals = ka, va
    else:
        qr = apply_axial_rope(qr, ang)
        kr = apply_axial_rope(kr, ang)
        qa = apply_axial_rope(qa, ang)
        keys = jnp.concatenate([apply_axial_rope(ka, ang), ctx_k], axis=2)
        vals = jnp.concatenate([va, ctx_v], axis=2)
    o_f, s_f = retention_chunkwise(qr, kr, vr, decay_f, s0_f)
    o_b, s_b = retention_chunkwise(qr[:, :, ::-1], kr[:, :, ::-1], vr[:, :, ::-1], decay_b, s0_b)
    ret = retention_group_norm(o_f + o_b[:, :, ::-1], gn) * jax.nn.silu(gr)
    att = from_heads(attention_blocked(qa, keys, vals))
    out = jnp.concatenate([ret, att], axis=-1) @ w_out
    return out, s_f, s_b, ka, va


def odd_mixer(h, w_in, w_out, rpb, ctx_k, ctx_v):
    q, k, v = jnp.split(h @ w_in, 3, axis=-1)
    q = to_heads(q, NA_HEADS, HEAD_DIM)
    k = to_heads(k, NA_HEADS, HEAD_DIM)
    v = to_heads(v, NA_HEADS, HEAD_DIM)
    if ctx_k is None:
        o = attention_blocked(q, k, v)
    else:
        o = neighbourhood_attention(q, k, v, ctx_k, ctx_v, rpb)
    return from_heads(o) @ w_out, k, v


def conv_ffn(h, w_up, conv_w, conv_b, w_down):
    u = h @ w_up
    up = jnp.pad(u, ((0, 0), (1, 1), (0, 0)))
    u = up[:, :-2] * conv_w[0] + up[:, 1:-1] * conv_w[1] + up[:, 2:] * conv_w[2] + conv_b
    a, g = jnp.split(u, 2, axis=-1)
    return (jax.nn.silu(a) * g) @ w_down


def ada_params(cond, w, b):
    m = jax.nn.silu(cond) @ w + b
    return jnp.split(m[:, None, :], 6, axis=-1)


def setup_inputs(seed: int = 0) -> dict:
    key = jax.random.key(seed)
    ks = iter(jax.random.split(key, 40))

    def nrm(shape, scale):
        return jax.random.normal(next(ks), shape, F32) * scale

    def gain(shape):
        return 1.0 + nrm(shape, 0.05)

    gam = 1.0 - 2.0 ** (-5.0 - jnp.arange(RET_HEADS, dtype=F32))
    decay_logit = jnp.log(gam) - jnp.log(1.0 - gam)
    conv_base = jnp.array([0.25, 0.5, 0.25], F32)[:, None]
    return {
        'x_prompt': nrm((BATCH, SEQ, D_MODEL), 1.0),
        'x_sample': nrm((DEC_BATCH, DEC_SEQ, D_MODEL), 1.0),
        'state_ret_fwd': nrm((DEC_BATCH, N_EVEN, RET_HEADS, RET_DK, RET_DV), 0.5),
        'state_ret_bwd': nrm((DEC_BATCH, N_EVEN, RET_HEADS, RET_DK, RET_DV), 0.5),
        'cache_gqa_k': nrm((DEC_BATCH, N_EVEN, GQA_KV_HEADS, PAST_LEN, HEAD_DIM), 1.0),
        'cache_gqa_v': nrm((DEC_BATCH, N_EVEN, GQA_KV_HEADS, PAST_LEN, HEAD_DIM), 1.0),
        'cache_na_k': nrm((DEC_BATCH, N_ODD, NA_HEADS, PAST_LEN, HEAD_DIM), 1.0),
        'cache_na_v': nrm((DEC_BATCH, N_ODD, NA_HEADS, PAST_LEN, HEAD_DIM), 1.0),
        'c': nrm((DEC_BATCH, D_MODEL), 1.0),
        'c_ctx': nrm((D_MODEL,), 1.0),
        'ada_w': nrm((DEPTH, D_MODEL, 6 * D_MODEL), 0.5 * D_MODEL ** -0.5),
        'ada_b': nrm((DEPTH, 6 * D_MODEL), 0.01),
        'norm_mix': gain((DEPTH, D_MODEL)),
        'norm_ffn': gain((DEPTH, D_MODEL)),
        'norm_final': gain((D_MODEL,)),
        'even_w_in': nrm((N_EVEN, D_MODEL, EVEN_IN), D_MODEL ** -0.5),
        'even_w_out': nrm((N_EVEN, EVEN_OUT_IN, D_MODEL), EVEN_OUT_IN ** -0.5),
        'ret_decay_fwd': decay_logit[None, :] + nrm((N_EVEN, RET_HEADS), 0.05),
        'ret_decay_bwd': decay_logit[None, :] + nrm((N_EVEN, RET_HEADS), 0.05),
        'ret_gn': gain((N_EVEN, RET_HEADS * RET_DV)),
        'gqa_q_norm': gain((N_EVEN, HEAD_DIM)),
        'gqa_k_norm': gain((N_EVEN, HEAD_DIM)),
        'odd_w_in': nrm((N_ODD, D_MODEL, 3 * NA_WIDTH), D_MODEL ** -0.5),
        'odd_w_out': nrm((N_ODD, NA_WIDTH, D_MODEL), NA_WIDTH ** -0.5),
        'na_rpb': nrm((N_ODD, NA_HEADS, 2 * NA_KH - 1, 2 * NA_KW - 1), 0.5),
        'ffn_w_up': nrm((DEPTH, D_MODEL, 2 * D_FF), D_MODEL ** -0.5),
        'ffn_conv_w': conv_base[None] + nrm((DEPTH, CONV_WIDTH, 2 * D_FF), 0.3),
        'ffn_conv_b': nrm((DEPTH, 2 * D_FF), 0.01),
        'ffn_w_down': nrm((DEPTH, D_FF, D_MODEL), D_FF ** -0.5),
    }


def reference(x_prompt, x_sample, state_ret_fwd, state_ret_bwd, cache_gqa_k, cache_gqa_v,
              cache_na_k, cache_na_v, c, c_ctx, ada_w, ada_b, norm_mix, norm_ffn, norm_final,
              even_w_in, even_w_out, ret_decay_fwd, ret_decay_bwd, ret_gn, gqa_q_norm, gqa_k_norm,
              odd_w_in, odd_w_out, na_rpb, ffn_w_up, ffn_conv_w, ffn_conv_b, ffn_w_down):
    xc = x_prompt
    xs = x_sample
    ang = axial_angles(x_sample.shape[1], HEAD_DIM)
    sf_list, sb_list, gk_list, gv_list, nk_list, nv_list = [], [], [], [], [], []
    for l in range(DEPTH):
        mc = ada_params(c_ctx[None, :], ada_w[l], ada_b[l])
        ms = ada_params(c, ada_w[l], ada_b[l])
        hc = rms_norm(xc, norm_mix[l]) * (1.0 + mc[1]) + mc[0]
        hs = rms_norm(xs, norm_mix[l]) * (1.0 + ms[1]) + ms[0]
        if l % 2 == 0:
            e = l // 2
            zeros = jnp.zeros((xc.shape[0], RET_HEADS, RET_DK, RET_DV), xc.dtype)
            oc, s_f, s_b, kc, vc = even_mixer(hc, even_w_in[e], even_w_out[e], ret_decay_fwd[e],
                                              ret_decay_bwd[e], ret_gn[e], gqa_q_norm[e], gqa_k_norm[e],
                                              zeros, zeros, None, None, None)
            os_, _, _, _, _ = even_mixer(hs, even_w_in[e], even_w_out[e], ret_decay_fwd[e],
                                         ret_decay_bwd[e], ret_gn[e], gqa_q_norm[e], gqa_k_norm[e],
                                         state_ret_fwd[:, e], state_ret_bwd[:, e],
                                         cache_gqa_k[:, e], cache_gqa_v[:, e], ang)
            sf_list.append(s_f)
            sb_list.append(s_b)
            gk_list.append(kc)
            gv_list.append(vc)
        else:
            o = l // 2
            oc, kc, vc = odd_mixer(hc, odd_w_in[o], odd_w_out[o], na_rpb[o], None, None)
            os_, _, _ = odd_mixer(hs, odd_w_in[o], odd_w_out[o], na_rpb[o], cache_na_k[:, o], cache_na_v[:, o])
            nk_list.append(kc)
            nv_list.append(vc)
        xc = xc + mc[2] * oc
        xs = xs + ms[2] * os_
        xc = xc + mc[5] * conv_ffn(rms_norm(xc, norm_ffn[l]) * (1.0 + mc[4]) + mc[3],
                                   ffn_w_up[l], ffn_conv_w[l], ffn_conv_b[l], ffn_w_down[l])
        xs = xs + ms[5] * conv_ffn(rms_norm(xs, norm_ffn[l]) * (1.0 + ms[4]) + ms[3],
                                   ffn_w_up[l], ffn_conv_w[l], ffn_conv_b[l], ffn_w_down[l])
    y_prompt = rms_norm(xc, norm_final)
    y_sample = rms_norm(xs, norm_final)
    new_state_ret_fwd = jnp.stack(sf_list, axis=1)
    new_state_ret_bwd = jnp.stack(sb_list, axis=1)
    new_cache_gqa_k = jnp.stack(gk_list, axis=1)
    new_cache_gqa_v = jnp.stack(gv_list, axis=1)
    new_cache_na_k = jnp.stack(nk_list, axis=1)
    new_cache_na_v = jnp.stack(nv_list, axis=1)
    return (y_prompt, y_sample, new_state_ret_fwd, new_state_ret_bwd, new_cache_gqa_k, new_cache_gqa_v, new_cache_na_k, new_cache_na_v)
```

```python
import contextlib
import numpy as np
import concourse.bass as bass
import concourse.mybir as mybir
from concourse.bass_utils import run_bass_kernel_spmd

F32 = mybir.dt.float32
BF16 = mybir.dt.bfloat16
AF = mybir.ActivationFunctionType
ALU = mybir.AluOpType
AX = mybir.AxisListType

ENGS = ("pe", "act", "dve", "pool", "sp")


class TT:
    __slots__ = ("name", "w", "r", "sem", "semcnt", "excl")

    def __init__(self, name):
        self.name = name
        self.excl = False
        self.w = None
        self.r = []
        self.sem = None
        self.semcnt = 0


class Op:
    __slots__ = ("eng", "fn", "deps", "sig", "cnt", "dma_tile", "idx")


class Sched:
    def __init__(self, nc, same_engine_sync=True):
        self.nc = nc
        self.ops = []
        self.same = same_engine_sync
        self.n_tiles = 0
        self.fence_deps = set()
        self.fence_start = 0

    def tile(self, name=None):
        self.n_tiles += 1
        return TT(name or f"t{self.n_tiles}")

    def _add(self, eng, fn, reads, writes, dma_tile=None):
        op = Op()
        op.eng = eng
        op.fn = fn
        op.idx = len(self.ops)
        op.sig = dma_tile is not None
        op.cnt = 0
        op.dma_tile = dma_tile
        deps = set()
        for t in reads:
            if t.w is not None:
                deps.add(t.w)
            if t.excl:
                for r in t.r:
                    if self.ops[r].eng != eng:
                        deps.add(r)
        for t in writes:
            if t.w is not None:
                deps.add(t.w)
            for r in t.r:
                deps.add(r)
        deps |= self.fence_deps
        deps.discard(op.idx)
        op.deps = deps
        for t in reads:
            t.r.append(op.idx)
        for t in writes:
            t.w = op.idx
            t.r = []
        self.ops.append(op)
        return op

    def fence(self):
        lasts = {}
        for op in self.ops:
            if op.dma_tile is None:
                lasts[op.eng] = op.idx
        nd = set(lasts.values())
        for op in self.ops[self.fence_start:]:
            if op.dma_tile is not None:
                nd.add(op.idx)
        self.fence_deps = self.fence_deps | nd
        self.fence_start = len(self.ops)

    def op(self, eng, fn, reads=(), writes=()):
        return self._add(eng, fn, list(reads), list(writes))

    def dma(self, q, out, in_, reads=(), writes=(), tile=None):
        assert tile is not None
        return self._add(q, lambda e: e.dma_start(out=out, in_=in_), list(reads), list(writes), dma_tile=tile)

    def emit(self):
        nc = self.nc
        ops = self.ops
        for op in ops:
            nd = set()
            for d in op.deps:
                p = ops[d]
                if p.dma_tile is None and p.eng == op.eng:
                    if p.eng == "pe" or not self.same:
                        continue
                nd.add(d)
            op.deps = nd
            for d in nd:
                ops[d].sig = True
        esem = {e: nc.alloc_semaphore(f"s_{e}") for e in ENGS}
        ecnt = {e: 0 for e in ENGS}
        for op in ops:
            if op.dma_tile is not None:
                t = op.dma_tile
                if t.sem is None:
                    self.n_dsem = getattr(self, "n_dsem", 0) + 1
                    t.sem = nc.alloc_semaphore(f"d{self.n_dsem}_{t.name}")
                t.semcnt += 16
                op.cnt = t.semcnt
            elif op.sig:
                ecnt[op.eng] += 1
                op.cnt = ecnt[op.eng]
        waited = {e: {} for e in ENGS}

        def emit_engine(ename, eobj):
            wd = waited[ename]
            for op in ops:
                if op.eng != ename:
                    continue
                need = {}
                for d in op.deps:
                    p = ops[d]
                    sem = p.dma_tile.sem if p.dma_tile is not None else esem[p.eng]
                    key = id(sem)
                    if key not in need or need[key][1] < p.cnt:
                        need[key] = (sem, p.cnt)
                for key, (sem, v) in need.items():
                    if wd.get(key, 0) >= v:
                        continue
                    eobj.wait_ge(sem, v)
                    wd[key] = v
                ins = op.fn(eobj)
                if op.dma_tile is not None:
                    ins.then_inc(op.dma_tile.sem, 16)
                elif op.sig:
                    ins.then_inc(esem[ename], 1)

        with nc.Block() as block:
            @block.tensor
            def _(e):
                emit_engine("pe", e)

            @block.scalar
            def _(e):
                emit_engine("act", e)

            @block.vector
            def _(e):
                emit_engine("dve", e)

            @block.gpsimd
            def _(e):
                emit_engine("pool", e)

            @block.sync
            def _(e):
                emit_engine("sp", e)


D = 1024
NT = 2048
DFF = 2816
NFC = 22
EPS = 1e-6
GN_EPS = 1e-5
GROUPS = [(0, 512, 0, 2, 256), (512, 512, 0, 2, 256), (1024, 1024, 1, 1, 1024)]
FFN_PHASES = [(0, 6), (6, 12), (12, 17), (17, 22)]

CFG_DEFAULT = dict(even=True, odd=True, ffn=True, dbg=False)


class Ctx:
    pass


def build_program(cfg, plan=None):
    nc = bass.Bass("TRN2", target_bir_lowering=False)
    S = Sched(nc)
    K = Ctx()
    K.nc, K.S, K.cfg = nc, S, cfg
    K.outs = []

    K.dram = {}

    def din(name, shape, dt=F32):
        a = nc.dram_tensor(name, list(shape), dt, kind="ExternalInput").ap()
        K.dram[name] = a
        return a

    def dout(name, shape):
        return nc.dram_tensor(name, list(shape), F32, kind="ExternalOutput").ap()

    def sb(name, shape, dt=F32):
        return nc.alloc_sbuf_tensor(name, list(shape), dt)

    K.uid = [0]

    def sbs(stack, name, shape, dt=F32):
        K.uid[0] += 1
        return stack.enter_context(nc.sbuf_tensor(f"{name}_{K.uid[0]}", list(shape), dt))

    K.din, K.dout, K.sb, K.sbs = din, dout, sb, sbs

    def MM(out, lhsT, rhs, start, stop, R, W):
        S.op("pe", lambda e: e.matmul(out, lhsT=lhsT, rhs=rhs, start=start, stop=stop), R, W)

    def TR(out, in_, ident, R, W):
        S.op("pe", lambda e: e.transpose(out, in_, ident), R, W)

    def ACT(out, in_, func, R, W, scale=1.0, bias=None):
        if bias is None:
            S.op("act", lambda e: e.activation(out=out, in_=in_, func=func, scale=scale), R, W)
        else:
            S.op("act", lambda e: e.activation(out=out, in_=in_, func=func, scale=scale, bias=bias), R, W)

    def TTO(eng, out, in0, in1, op, R, W):
        S.op(eng, lambda e: e.tensor_tensor(out=out, in0=in0, in1=in1, op=op), R, W)

    def STT(eng, out, in0, scalar, in1, op0, op1, R, W):
        S.op(eng, lambda e: e.scalar_tensor_tensor(out=out, in0=in0, scalar=scalar, in1=in1, op0=op0, op1=op1), R, W)

    def TS(eng, out, in0, s1, s2, op0, op1, R, W):
        if s2 is None:
            S.op(eng, lambda e: e.tensor_scalar(out=out, in0=in0, scalar1=s1, scalar2=None, op0=op0), R, W)
        else:
            S.op(eng, lambda e: e.tensor_scalar(out=out, in0=in0, scalar1=s1, scalar2=s2, op0=op0, op1=op1), R, W)

    def CP(eng, out, in_, R, W):
        if eng == "act":
            S.op("act", lambda e: e.copy(out=out, in_=in_), R, W)
        else:
            S.op(eng, lambda e: e.tensor_copy(out=out, in_=in_), R, W)

    def MSET(eng, ap, val, W):
        S.op(eng, lambda e: e.memset(ap, val), [], W)

    def DMA(q, out, in_, R, W, tile):
        S.dma(q, out, in_, R, W, tile)

    def RCP(out, in_, R, out_t):
        ACT(out, in_, AF.Ln, R, [out_t])
        ACT(out, out, AF.Exp, [out_t], [out_t], scale=-1.0)

    K.RCP = RCP
    K.MM, K.TR, K.ACT, K.TTO, K.STT, K.TS, K.CP, K.MSET, K.DMA = MM, TR, ACT, TTO, STT, TS, CP, MSET, DMA

    K.pd = [nc.alloc_psum_tensor(f"pd{i}", [128, 1024], F32) for i in range(4)]
    K.pt = [S.tile(f"pb{i}") for i in range(8)]
    for t_ in K.pt:
        t_.excl = True
    K.rr = [0]

    def bank(i):
        d, h = divmod(i, 2)
        return K.pd[d], h * 512, K.pt[i]

    def next_bank():
        i = K.rr[0] % 8
        K.rr[0] += 1
        return bank(i)

    def next_dbank():
        if K.rr[0] % 2:
            K.rr[0] += 1
        i = K.rr[0] % 8
        K.rr[0] += 2
        return K.pd[i // 2], [K.pt[i], K.pt[i + 1]]

    K.bank, K.next_bank, K.next_dbank = bank, next_bank, next_dbank
    K.bank_alloc = next_bank

    x_tok = din("x_tok", [NT, D])
    consts = din("consts", [128, 128 + 16])
    smallp = din("smallp", [128, 2 * 48 + 5 * 8])
    ada_w = din("ada_w", [2, 128, 8, 6 * D])
    if cfg["even"]:
        din("even_w_in", [128, 8, 4 * 1024 + 512 + 512 + 384])
        din("even_w_out", [128, 12, D])
    if cfg["ffn"]:
        din("w_up", [2, 128, 8, 2 * DFF])
        din("w_dn", [2, 128, NFC, D])
    if cfg["odd"]:
        din("odd_w_in", [128, 8, 3072])
        din("odd_w_out", [128, 8, D])
    y_out = dout("y_out", [NT, D])

    xT = sb("xT", [128, 8, NT], F32)
    K.xT = xT
    K.xT_t = [S.tile(f"xT{b}") for b in range(4)]
    hT = sb("hT", [128, 8, NT], BF16)
    K.hT = hT
    K.hT_t = [S.tile(f"hT{b}") for b in range(4)]
    cst = sb("cst", [128, 128 + 16], F32)
    cst_t = S.tile("cst")
    K.ident = cst[:, 0:128]
    smp = sb("smp", [128, 2 * 48 + 40], F32)
    smp_t = S.tile("smp")
    ones_bf = sb("ones_bf", [128, 128], BF16)
    ones_t = S.tile("ones")
    neghalf = sb("neghalf", [128, 512], F32)
    nh_t = S.tile("nh")
    K.ones_bf, K.ones_t, K.neghalf, K.nh_t, K.cst_t = ones_bf, ones_t, neghalf, nh_t, cst_t

    DMA("sp", cst[:], consts, [], [cst_t], cst_t)
    DMA("sp", smp[:], smallp, [], [smp_t], smp_t)
    MSET("pool", ones_bf[:], 1.0, [ones_t])
    MSET("pool", neghalf[:], -0.5, [nh_t])
    epsc = sb("epsc", [128, 2], F32)
    epsc_t = S.tile("epsc")
    MSET("pool", epsc[:, 0:1], EPS, [epsc_t])
    MSET("pool", epsc[:, 1:2], GN_EPS, [epsc_t])
    K.epsc, K.epsc_t = epsc, epsc_t

    scT = sb("scT", [128, 16], BF16)
    scT_t = S.tile("scT")
    ACT(scT[:], cst[:, 128:144], AF.Silu, [cst_t], [scT_t])
    mT = sb("mT", [128, 2, 48, 2], F32)
    mT_t = S.tile("mT")
    wraw = [sb(f"wa{i}", [128, 4096], BF16) for i in range(3)]
    wa = [w[:, :].rearrange("p (k n) -> p k n", k=8) for w in wraw]
    wa_t = [S.tile(f"wa{i}") for i in range(3)]
    K.wraw, K.wslot_t = wraw, wa_t
    K.wplan = []
    K.wpos = [0]
    K.wissued = [0]

    def wview(sidx, k, n):
        return wraw[sidx][:, 0:k * n].rearrange("p (k n) -> p k n", k=k)

    def wissue(q, name, idx, k, n):
        sidx = q % 3
        DMA("pool", wview(sidx, k, n), K.dram[name][idx], [], [wa_t[sidx]], wa_t[sidx])

    def wnext(name, idx, k, n, live_prev=0):
        p = K.wpos[0]
        K.wpos[0] += 1
        if plan is None:
            K.wplan.append((name, idx, k, n))
            wissue(p, name, idx, k, n)
        else:
            assert plan[p] == (name, idx, k, n), (p, plan[p], name, idx, k, n)
            upto = min(len(plan) - 1, p + 2 - live_prev)
            while K.wissued[0] <= upto:
                q = K.wissued[0]
                wissue(q, *plan[q])
                K.wissued[0] += 1
        return wview(p % 3, k, n), wa_t[p % 3]

    K.wnext = wnext
    mT_tl = [S.tile(f"mT{l}") for l in range(2)]
    mod = sb("mod", [128, 2, 6, 8, 2], F32)
    mod_t = [S.tile(f"mod{l}") for l in range(2)]
    K.mod, K.mod_t = mod, mod_t
    K.gfin = smp[:, 96 + 32:96 + 40]
    K.smp_t = smp_t
    K.ada_pending = [(1, blk) for blk in range(12)]

    def ada_block(l, blk):
        wv, wv_t = wnext("ada_w", (l, slice(None), slice(None), slice(blk * 512, (blk + 1) * 512)), 8, 512)
        pd, off, ptile = K.bank_alloc()
        for oc in range(4):
            for k in range(8):
                MM(pd[:, off + oc * 2: off + oc * 2 + 2], wv[:, k, oc * 128:(oc + 1) * 128],
                   scT[:, k * 2:(k + 1) * 2], k == 0, k == 7, [wv_t, scT_t], [ptile])
        TTO("dve", mT[:, l, blk * 4:(blk + 1) * 4, :],
            pd[:, off:off + 8].rearrange("p (c g) -> p c g", g=2),
            smp[:, l * 48 + blk * 4: l * 48 + blk * 4 + 4].unsqueeze(2).broadcast_to([128, 4, 2]),
            ALU.add, [ptile, smp_t], [mT_tl[l]])

    def ada_finish(l):
        for (dst, jscale, jshift, jgate, gi) in [(0, 1, 0, 2, l), (3, 4, 3, 5, 2 + l)]:
            gn = smp[:, 96 + gi * 8: 96 + gi * 8 + 8].unsqueeze(2).broadcast_to([128, 8, 2])
            STT("dve", mod[:, l, dst, :, :], mT[:, l, jscale * 8:(jscale + 1) * 8, :], 1.0, gn, ALU.add, ALU.mult,
                [mT_tl[l], smp_t], [mod_t[l]])
            CP("dve", mod[:, l, dst + 1, :, :], mT[:, l, jshift * 8:(jshift + 1) * 8, :], [mT_tl[l]], [mod_t[l]])
            CP("dve", mod[:, l, dst + 2, :, :], mT[:, l, jgate * 8:(jgate + 1) * 8, :], [mT_tl[l]], [mod_t[l]])

    def ada_some(n):
        for _ in range(n):
            if K.ada_pending:
                ada_block(*K.ada_pending.pop(0))

    K.ada_some = ada_some

    K.nrr = [0]
    K.t1rr = [0]
    K.nb = None

    class NormBufs:
        def __init__(self, st):
            self.sq = [K.sbs(st, f"sq{i}", [128, 8, 512], BF16) for i in range(2)]
            self.sq_t = [[S.tile(f"sq{i}a"), S.tile(f"sq{i}b")] for i in range(2)]
            self.rs = [K.sbs(st, f"rs{i}", [128, 512], F32) for i in range(2)]
            self.rs_t = [S.tile(f"rs{i}") for i in range(2)]
            self.t1 = [K.sbs(st, f"t1_{i}", [128, 512], F32) for i in range(3)]
            self.t1_t = [S.tile(f"t1_{i}") for i in range(3)]

    def rstd_block(tb):
        i = K.nrr[0] % 2
        K.nrr[0] += 1
        nb = K.nb
        sq, sq_t, rs, rs_t = nb.sq, nb.sq_t, nb.rs, nb.rs_t
        sl = slice(tb * 512, (tb + 1) * 512)
        for hf in range(2):
            ACT(sq[i][:, 4 * hf:4 * hf + 4, :], xT[:, 4 * hf:4 * hf + 4, sl], AF.Square, [K.xT_t[tb]], [sq_t[i][hf]])
        pd, off, ptile = next_bank()
        for c in range(8):
            MM(pd[:, off:off + 512], ones_bf[:], sq[i][:, c, :], c == 0, c == 7, [ones_t, sq_t[i][c // 4]], [ptile])
        ACT(rs[i][:], pd[:, off:off + 512], AF.Ln, [ptile, K.epsc_t], [rs_t[i]], scale=1.0 / D, bias=K.epsc[:, 0:1])
        ACT(rs[i][:], rs[i][:], AF.Exp, [rs_t[i]], [rs_t[i]], scale=-0.5)
        return rs[i], rs_t[i]

    def norm_mod(l, which, tbs):
        with contextlib.ExitStack() as st:
            K.nb = NormBufs(st)
            norm_mod_body(l, which, tbs)
        S.fence()

    def norm_mod_body(l, which, tbs):
        t1, t1_t = K.nb.t1, K.nb.t1_t

        def s2(tb, rr_):
            r, r_t = rr_
            g = 0 if tb < 2 else 1
            sl = slice(tb * 512, (tb + 1) * 512)
            for c in range(8):
                j = K.t1rr[0] % 3
                K.t1rr[0] += 1
                if c < 3:
                    STT("dve", t1[j][:], xT[:, c, sl], mod[:, l, which, c, g:g + 1], r[:], ALU.mult, ALU.mult,
                        [K.xT_t[tb], mod_t[l], r_t], [t1_t[j]])
                    ACT(hT[:, c, sl], t1[j][:], AF.Identity, [t1_t[j], mod_t[l]], [K.hT_t[tb]],
                        bias=mod[:, l, which + 1, c, g:g + 1])
                else:
                    TTO("dve", t1[j][:], xT[:, c, sl], r[:], ALU.mult, [K.xT_t[tb], r_t], [t1_t[j]])
                    TS("dve", hT[:, c, sl], t1[j][:], mod[:, l, which, c, g:g + 1], mod[:, l, which + 1, c, g:g + 1],
                       ALU.mult, ALU.add, [t1_t[j], mod_t[l]], [K.hT_t[tb]])

        pipeline(list(tbs), rstd_block, s2, 1)

    K.norm_mod = norm_mod
    K.rstd_block = rstd_block

    if cfg["ffn"]:
        convp = din("convp", [128, 2, 4, 2 * NFC])
        cvp = sb("cvp", [128, 2, 4, 2 * NFC], F32)
        cvp_t = S.tile("cvp")
        DMA("sp", cvp[:], convp, [], [cvp_t], cvp_t)
        K.ffrr = [0, 0, 0]

    def ffn(l):
        with contextlib.ExitStack() as st:
            K.nb = NormBufs(st)
            norm_mod_body(l, 3, [0, 1, 2, 3])
            ffn_body(l, st)
        S.fence()

    def ffn_body(l, st):
        actT = K.sbs(st, "actT", [128, 6, NT], BF16)
        actT_t = [[S.tile(f"actT{c}_{b}") for b in range(3)] for c in range(6)]
        acc = [[K.sbs(st, f"acc{h}{i}", [128, 1024], F32) for i in range(2)] for h in range(2)]
        acc_t = [[S.tile(f"acc{h}{i}") for i in range(2)] for h in range(2)]
        sil = [K.sbs(st, f"sil{i}", [128, 1024], F32) for i in range(2)]
        sil_t = [S.tile(f"sil{i}") for i in range(2)]
        for (c0, c1) in FFN_PHASES:
            for c in range(c0, c1):
                wuv, wuv_t = wnext("w_up", (l, slice(None), slice(None), slice(c * 256, (c + 1) * 256)), 8, 256)
                for gi, (t0, T, g, nseq, slen) in enumerate(GROUPS):
                    accs = []
                    for h in range(2):
                        pdt, ptl = next_dbank()
                        for sb_ in range(T // 512):
                            for k in range(8):
                                MM(pdt[:, sb_ * 512:(sb_ + 1) * 512], wuv[:, k, h * 128:(h + 1) * 128],
                                   hT[:, k, t0 + sb_ * 512: t0 + (sb_ + 1) * 512], k == 0, k == 7,
                                   [wuv_t, K.hT_t[(t0 // 512) + sb_]], [ptl[sb_]])
                        pts = ptl[:T // 512]
                        i = K.ffrr[1] % 2
                        if h == 1:
                            K.ffrr[1] += 1
                        a, a_t = acc[h][i], acc_t[h][i]
                        ci = c * 2 + h
                        U = pdt[:, 0:T]
                        ACT(a[:, 0:T], U, AF.Identity, pts + [cvp_t], [a_t],
                            scale=cvp[:, l, 1, ci:ci + 1], bias=cvp[:, l, 3, ci:ci + 1])
                        U3 = U.rearrange("p (s t) -> p s t", s=nseq)
                        a3 = a[:, 0:T].rearrange("p (s t) -> p s t", s=nseq)
                        STT("dve", a3[:, :, 1:slen], U3[:, :, 0:slen - 1], cvp[:, l, 0, ci:ci + 1], a3[:, :, 1:slen],
                            ALU.mult, ALU.add, pts + [cvp_t, a_t], [a_t])
                        STT("dve", a3[:, :, 0:slen - 1], U3[:, :, 1:slen], cvp[:, l, 2, ci:ci + 1], a3[:, :, 0:slen - 1],
                            ALU.mult, ALU.add, pts + [cvp_t, a_t], [a_t])
                        accs.append((a, a_t))
                    i2 = K.ffrr[2] % 2
                    K.ffrr[2] += 1
                    ACT(sil[i2][:, 0:T], accs[0][0][:, 0:T], AF.Silu, [accs[0][1]], [sil_t[i2]])
                    TTO("pool", actT[:, c - c0, t0:t0 + T], sil[i2][:, 0:T], accs[1][0][:, 0:T], ALU.mult,
                        [sil_t[i2], accs[1][1]], [actT_t[c - c0][gi]])
            npc = c1 - c0
            for dh in range(2):
                wdv, wdv_t = wnext("w_dn", (l, slice(None), slice(c0, c1), slice(dh * 512, (dh + 1) * 512)), npc, 512)
                for dc in range(4):
                    dmc = dh * 4 + dc
                    for tb in range(4):
                        g = 0 if tb < 2 else 1
                        gi = tb if tb < 2 else 2
                        pd, off, ptile = next_bank()
                        for cc in range(npc):
                            MM(pd[:, off:off + 512], wdv[:, cc, dc * 128:(dc + 1) * 128],
                               actT[:, cc, tb * 512:(tb + 1) * 512], cc == 0, cc == npc - 1,
                               [wdv_t, actT_t[cc][gi]], [ptile])
                        STT("dve", xT[:, dmc, tb * 512:(tb + 1) * 512], pd[:, off:off + 512],
                            mod[:, l, 5, dmc, g:g + 1], xT[:, dmc, tb * 512:(tb + 1) * 512], ALU.mult, ALU.add,
                            [ptile, mod_t[l], K.xT_t[tb]], [K.xT_t[tb]])

    K.ffn = ffn

    xst = contextlib.ExitStack()
    NXS = 8
    xs = [K.sbs(xst, f"xs{i}", [128, D], F32) for i in range(NXS)]
    xs_t = [S.tile(f"xs{i}") for i in range(NXS)]
    K.nb = NormBufs(xst)
    for tt in range(16):
        s = tt % NXS
        DMA("act" if tt % 2 else "sp", xs[s][:], x_tok[tt * 128:(tt + 1) * 128, :], [], [xs_t[s]], xs_t[s])
        for half in range(2):
            pd, off, ptile = next_bank()
            for j in range(4):
                c = half * 4 + j
                TR(pd[:, off + j * 128: off + (j + 1) * 128], xs[s][:, c * 128:(c + 1) * 128], K.ident,
                   [xs_t[s], cst_t], [ptile])
            eng = "act" if (tt + half) % 2 else "dve"
            CP(eng, xT[:, half * 4:(half + 1) * 4, tt * 128:(tt + 1) * 128],
               pd[:, off:off + 512].rearrange("p (c t) -> p c t", c=4), [ptile], [K.xT_t[tt // 4]])
        if tt < 12:
            ada_block(0, tt)
    ada_finish(0)
    K.first_norm = False
    if cfg["even"]:
        norm_mod_body(0, 0, [0, 1, 2, 3])
        K.first_norm = True
    xst.close()
    S.fence()

    for l in range(2):
        if l == 1:
            ada_some(12)
            ada_finish(1)
        if l == 0 and cfg["even"]:
            even_mixer(K, l)
        if l == 1 and cfg["odd"]:
            odd_mixer(K, l)
        if cfg["ffn"]:
            ffn(l)

    S.fence()
    fst = contextlib.ExitStack()
    K.nb = NormBufs(fst)
    yT = [K.sbs(fst, f"yT{i}", [128, 8, 512], F32) for i in range(2)]
    yT_t = [S.tile(f"yT{i}") for i in range(2)]
    ys = [K.sbs(fst, f"ys{i}", [128, D], F32) for i in range(2)]
    ys_t = [S.tile(f"ys{i}") for i in range(2)]

    def fin_s2(tb, rr_):
        r, r_t = rr_
        yb = tb % 2
        sl = slice(tb * 512, (tb + 1) * 512)
        for c in range(8):
            STT("dve", yT[yb][:, c, :], xT[:, c, sl], K.gfin[:, c:c + 1], r[:], ALU.mult, ALU.mult,
                [K.xT_t[tb], smp_t, r_t], [yT_t[yb]])
        for q in range(4):
            tt = tb * 4 + q
            s_ = tt % 2
            for half in range(2):
                pd, off, ptile = next_bank()
                for j in range(4):
                    c = half * 4 + j
                    TR(pd[:, off + j * 128: off + (j + 1) * 128], yT[yb][:, c, q * 128:(q + 1) * 128], K.ident,
                       [yT_t[yb], cst_t], [ptile])
                eng = "act" if half else "dve"
                CP(eng, ys[s_][:, half * 512:(half + 1) * 512], pd[:, off:off + 512], [ptile], [ys_t[s_]])
            DMA("sp", y_out[tt * 128:(tt + 1) * 128, :], ys[s_][:], [ys_t[s_]], [], ys_t[s_])
            if ys_t[s_] not in K.outs:
                K.outs.append(ys_t[s_])

    pipeline([0, 1, 2, 3], rstd_block, fin_s2, 1)

    if plan is None:
        return K.wplan
    assert K.wpos[0] == len(plan)
    S.op("sp", lambda e: e.nop(), [], K.outs)
    S.emit()
    return nc


def even_mixer(K, l):
    nc, S = K.nc, K.S
    MM, TR, ACT, TTO, STT, TS, CP, MSET, DMA = K.MM, K.TR, K.ACT, K.TTO, K.STT, K.TS, K.CP, K.MSET, K.DMA
    RCP = K.RCP
    hT, hT_t, xT, xT_t = K.hT, K.hT_t, K.xT, K.xT_t
    NCOL = 4 * 1024 + 512 + 512 + 384
    evsmall_d = K.din("evsmall", [128, 24])
    evrow_d = K.din("evrow", [128, 84])
    evtab_d = K.din("evtab", [128, 3 * 1024 + 256])
    cgk = K.din("cache_gqa_k", [2, 256, 64])
    cgv = K.din("cache_gqa_v", [2, 256, 64])
    s0f_d = K.din("state_ret_fwd", [8, 64, 128])
    s0b_d = K.din("state_ret_bwd", [8, 64, 128])
    sf_out = K.dout("sf_out", [4, 8, 64, 128])
    sb_out = K.dout("sb_out", [4, 8, 64, 128])
    gk_out = K.dout("gk_out", [4, 2, 256, 64])
    gv_out = K.dout("gv_out", [4, 2, 256, 64])

    if not K.first_norm:
        K.norm_mod(l, 0, [0, 1, 2, 3])
    st = contextlib.ExitStack()
    sbs = lambda n, sh, dt=F32: K.sbs(st, n, sh, dt)
    NH = 1024
    evs = sbs("evs", [128, 24], F32)
    evs_t = S.tile("evs")
    evr = sbs("evr", [128, 84], F32)
    evr_t = S.tile("evr")
    lgp = sbs("lgp", [128, 4, 4], F32)
    lgp_t = S.tile("lgp")
    lgr = sbs("lgr", [128, 16], F32)
    lgr_t = S.tile("lgr")
    dk = sbs("dk", [128, 2, 2, 8], F32)
    dk_t = S.tile("dk")
    scs = sbs("scs", [128, 4, 2], F32)
    scs_t = S.tile("scs")
    tab = sbs("tab", [128, 1024 + 256], F32)
    tab_t = S.tile("tab")
    POS = tab[:, 0:1024]
    RELUD = tab[:, 1024:1152]
    bones = sbs("bones", [128, 128], BF16)
    bones_t = S.tile("bones")
    rr = dict(b=0)
    K.side = []

    def side_pop(n=1):
        for _ in range(n):
            if K.side:
                K.side.pop(0)[1]()

    def side_flush(tag=None):
        if tag is None:
            n = len(K.side)
        else:
            idx = [i for i, (t_, _) in enumerate(K.side) if t_ == tag]
            n = idx[-1] + 1 if idx else 0
        for _ in range(n):
            K.side.pop(0)[1]()

    K.side_pop, K.side_flush = side_pop, side_flush

    def rot(key, n):
        i = rr.get(key, 0) % n
        rr[key] = rr.get(key, 0) + 1
        return i

    K.ev_nb = 3

    def rbank():
        return K.bank(4 + rot("b", K.ev_nb))

    K.bank_alloc = rbank

    DMA("sp", evs[:], evsmall_d, [], [evs_t], evs_t)
    DMA("sp", evr[:], evrow_d, [], [evr_t], evr_t)
    DMA("sp", tab[:], evtab_d[:, 0:1280], [], [tab_t], tab_t)
    MSET("pool", bones[:], 0.0, [bones_t])
    MSET("pool", bones[0:64, 0:64], 1.0, [bones_t])
    MSET("pool", bones[64:128, 64:128], 1.0, [bones_t])
    ACT(lgp[:, :, 0:2], evs[:, 0:8].rearrange("p (a b) -> p a b", b=2), AF.Exp, [evs_t], [lgp_t], scale=-1.0)
    ACT(lgp[:, :, 2:4], lgp[:, :, 0:2], AF.Ln, [lgp_t], [lgp_t], bias=1.0)
    TS("dve", lgp[:, :, 0:2], lgp[:, :, 2:4], -1.0, None, ALU.mult, None, [lgp_t], [lgp_t])
    ACT(lgr[:], evr[:, 0:16], AF.Exp, [evr_t], [lgr_t], scale=-1.0)
    ACT(lgr[:], lgr[:], AF.Ln, [lgr_t], [lgr_t], bias=1.0)
    TS("dve", lgr[:], lgr[:], -1.0, None, ALU.mult, None, [lgr_t], [lgr_t])
    lgs = sbs("lgs", [128, 8], F32)
    lgs_t = S.tile("lgs")
    TTO("dve", lgs[:], lgr[:, 0:8], lgr[:, 8:16], ALU.add, [lgr_t], [lgs_t])
    cdg = sbs("cdg", [128, 2, 128], F32)
    cdg_t = S.tile("cdg")
    for d_ in range(2):
        for t in range(2):
            ACT(dk[:, d_, t, :], lgr[:, d_ * 8:(d_ + 1) * 8], AF.Exp, [lgr_t, evr_t], [dk_t],
                scale=evr[:, 80 + d_ * 2 + t: 80 + d_ * 2 + t + 1])
    ACT(scs[:, :, 0:1], lgp[:, :, 0:1], AF.Exp, [lgp_t], [scs_t], scale=513.0)
    ACT(scs[:, :, 1:2], lgp[:, :, 1:2], AF.Exp, [lgp_t], [scs_t], scale=512.0)

    def load_w(c0, ncols, live_prev=0):
        return K.wnext("even_w_in", (slice(None), slice(None), slice(c0, c0 + ncols)), 8, ncols, live_prev)

    def proj_fm(w, w_t, col, tb):
        pd, off, ptile = rbank()
        for k in range(8):
            MM(pd[:, off:off + 512], w[:, k, col:col + 128], hT[:, k, tb * 512:(tb + 1) * 512], k == 0, k == 7,
               [w_t, hT_t[tb]], [ptile])
        return pd, off, ptile

    for half in range(2):
        tb0 = 2 * half
        T0 = 1024 * half
        rope = half == 1
        K.ev_nb = 3 if half == 0 else 4
        nseq, slen = (4, 256) if half == 0 else (1, 1024)
        pos0 = 384 if half == 0 else 0
        hst = contextlib.ExitStack()
        hs = lambda n, sh, dt=F32: K.sbs(hst, n, sh, dt)
        concT = hs("concT", [128, 12, NH], BF16)
        concT_t = [S.tile(f"conc{c}") for c in range(12)]
        tp = [hs(f"tp{i}", [128, 512], F32) for i in range(3)]
        tp_t = [S.tile(f"tp{i}") for i in range(3)]
        rp, rp_t, gtm, gtm_t = tp, tp_t, tp, tp_t
        pT = [hs(f"pT{i}", [128, 512], BF16) for i in range(4)]
        pT_t = [S.tile(f"pT{i}") for i in range(4)]
        if rope:
            cs = hs("cs", [128, 2048], F32)
            cs_t = S.tile("cs")
            DMA("sp", cs[:], evtab_d[:, 1280:3328], [], [cs_t], cs_t)
            COS, SIN = cs[:, 0:1024], cs[:, 1024:2048]
        else:
            cs_t = tab_t
            COS = SIN = None
        rst = contextlib.ExitStack()
        rs_ = lambda n, sh, dt=F32: K.sbs(rst, n, sh, dt)
        G = [rs_(f"G{i}", [128, 512], F32) for i in range(2)]
        G_t = [S.tile(f"G{i}") for i in range(2)]
        qk4 = [rs_(f"qk4_{i}", [128, NH], BF16) for i in range(2)]
        qk4 += [rs_(f"qk4_{i}", [128, 2, NH], BF16) for i in range(2, 4)]
        qk4_t = [S.tile(f"qk4_{i}") for i in range(4)]
        for i_ in (2, 3):
            MSET("pool", qk4[i_][64:128, 0, :], 0.0, [qk4_t[i_]])
            MSET("pool", qk4[i_][0:64, 1, :], 0.0, [qk4_t[i_]])
        vtok = rs_("vtok", [128, 8, 256], BF16)
        vtok_t = [S.tile(f"vtok{t}") for t in range(8)]
        gate = rs_("gate", [128, 2, NH], BF16)
        gate_t = [S.tile(f"gate{i}") for i in range(2)]
        obf = [rs_("obf0", [128, 2, 512], BF16)] * 2
        obf_t = [S.tile("obf0")] * 2
        osb = [rs_("osb0", [128, 512], F32)] * 2
        osb_t = [S.tile("osb0")] * 2
        gnb = rs_("gnb", [128, 512], F32)
        gnb_t = S.tile("gnb")
        if half == 0:
            kd = [rs_(f"kd{i}", [128, 8, 128], BF16) for i in range(2)]
            kd_t = [S.tile(f"kd{i}") for i in range(2)]
            stg = rs_("stg", [128, 512], F32)
            stg_t = S.tile("stg")
        else:
            s0 = rs_("s0", [128, 2, 128], F32)
            s0_t = S.tile("s0")
            s0s = rs_("s0s", [128, 2, 2, 128], BF16)
            s0s_t = S.tile("s0s")
            MSET("pool", s0s[64:128, 0, :, :], 0.0, [s0s_t])
            MSET("pool", s0s[0:64, 1, :, :], 0.0, [s0s_t])
        for hp in range(4):
            K.ada_some(1)
            wA, wA_t = load_w(hp * 1024, 512)
            if True:
                for x in range(2):
                    ACT(cdg[:, x, :], RELUD, AF.Exp, [tab_t, lgs_t], [cdg_t], scale=lgs[:, 2 * hp + x:2 * hp + x + 1])
                    TTO("pool", cdg[:, x, :], cdg[:, x, :], K.ident, ALU.add, [cdg_t, K.cst_t], [cdg_t])
            gsel = [0, 3, 2, 1]
            for bi in range(2):
                tb = tb0 + bi
                sl = slice(bi * 512, (bi + 1) * 512)
                for which in range(2):
                    K.side_pop(1)
                    pd, off, ptile = proj_fm(wA, wA_t, which * 128, tb)
                    src, src_t = pd[:, off:off + 512], ptile
                    if rope:
                        pd2, off2, ptile2 = proj_fm(wA, wA_t, 256 + which * 128, tb)
                        i1, i2 = rot("tp", 3), rot("tp", 3)
                        TTO("dve", rp[i1][:], pd[:, off:off + 512], COS[:, sl], ALU.mult, [ptile, cs_t], [rp_t[i1]])
                        TTO("dve", rp[i2][:], pd2[:, off2:off2 + 512], SIN[:, sl], ALU.mult, [ptile2, cs_t], [rp_t[i2]])
                        TTO("pool", rp[i1][:], rp[i1][:], rp[i2][:], ALU.add, [rp_t[i1], rp_t[i2]], [rp_t[i1]])
                        src, src_t = rp[i1][:], rp_t[i1]
                    for dr in range(2):
                        oi = which * 2 + dr
                        gi = rot("G", 2)
                        ps = slice(pos0 + (bi * 512 if half == 1 else 0), pos0 + (bi * 512 if half == 1 else 0) + 512)
                        if half == 0:
                            for s2 in range(2):
                                ACT(G[gi][:, s2 * 256:(s2 + 1) * 256], POS[:, 384:640], AF.Exp, [tab_t, lgp_t], [G_t[gi]],
                                    scale=lgp[:, hp, gsel[oi]:gsel[oi] + 1])
                        else:
                            ACT(G[gi][:, 0:512], POS[:, ps], AF.Exp, [tab_t, lgp_t], [G_t[gi]],
                                scale=lgp[:, hp, gsel[oi]:gsel[oi] + 1])
                        if which == 0:
                            STT("dve", qk4[oi][:, sl], src, 1.0, G[gi][:, 0:512], ALU.mult, ALU.mult,
                                [src_t, G_t[gi]], [qk4_t[oi]])
                        else:
                            for x_ in range(2):
                                ph = slice(64 * x_, 64 * x_ + 64)
                                STT("dve", qk4[oi][ph, x_, sl], src[ph, :], 0.125, G[gi][ph, 0:512], ALU.mult, ALU.mult,
                                    [src_t, G_t[gi]], [qk4_t[oi]])
            if half == 0:
                for tt in range(8):
                    K.side_pop(1)
                    tb = tb0 + tt // 4
                    tsl = slice(T0 + tt * 128, T0 + (tt + 1) * 128)
                    pd, off, ptile = rbank()
                    for k in range(8):
                        MM(pd[:, off:off + 128], hT[:, k, tsl], wA[:, k, 128:256], k == 0, k == 7, [wA_t, hT_t[tb]], [ptile])
                    for d_ in range(2):
                        STT("dve", kd[d_][:, tt, :].rearrange("p (h d) -> p h d", h=2),
                            pd[:, off:off + 128].rearrange("p (h d) -> p h d", h=2), 0.125,
                            dk[:, d_, tt % 2, 2 * hp:2 * hp + 2].unsqueeze(2).broadcast_to([128, 2, 64]),
                            ALU.mult, ALU.mult, [ptile, dk_t], [kd_t[d_]])
            wB, wB_t = load_w(hp * 1024 + 512, 512)
            def gates(x):
                for bi in range(2):
                    tb = tb0 + bi
                    sl = slice(bi * 512, (bi + 1) * 512)
                    pd, off, ptile = proj_fm(wB, wB_t, x * 128, tb)
                    ACT(gate[:, x, sl], pd[:, off:off + 512], AF.Silu, [ptile], [gate_t[x]])

            K.side_flush(0)
            gates(0)
            for tt in range(8):
                K.side_pop(1)
                tb = tb0 + tt // 4
                tsl = slice(T0 + tt * 128, T0 + (tt + 1) * 128)
                pd, off, ptile = rbank()
                for k in range(8):
                    MM(pd[:, off:off + 256], hT[:, k, tsl], wB[:, k, 256:512], k == 0, k == 7, [wB_t, hT_t[tb]], [ptile])
                CP("act", vtok[:, tt, :], pd[:, off:off + 256], [ptile], [vtok_t[tt]])
            K.side_flush()
            gates(1)
            if half == 1:
                for d_, src_d in ((0, s0f_d), (1, s0b_d)):
                    DMA("sp", s0[:, d_, :], src_d[2 * hp:2 * hp + 2].rearrange("h k v -> (h k) v"), [], [s0_t], s0_t)
                    for x_ in range(2):
                        ph = slice(64 * x_, 64 * x_ + 64)
                        TS("dve", s0s[ph, x_, d_, :], s0[ph, d_, :], scs[ph, hp, d_:d_ + 1], None, ALU.mult, None,
                           [s0_t, scs_t], [s0s_t])
            for x in range(2):
                b = 64 * x
                h = 2 * hp + x
                opd = K.pd[x]
                optl = [K.pt[2 * x], K.pt[2 * x + 1]]
                ntl = slen // 128
                blk = min(slen, 512)
                jobs = [(sq_, ib, jt) for sq_ in range(nseq) for ib in range(slen // blk) for jt in range(ntl)]
                K.side_flush(x)

                def s1(jb, b=b, x=x):
                    sq_, ib, jt = jb
                    s_off = sq_ * slen
                    c_lo = s_off + ib * blk
                    jsl = slice(s_off + jt * 128, s_off + (jt + 1) * 128)
                    pd, off, ptile = rbank()
                    its = range(ib * (blk // 128), (ib + 1) * (blk // 128))
                    fw = [it for it in its if it >= jt]
                    bw = [it for it in its if it < jt]
                    j = rot("pT", 4)
                    if bw:
                        lo, hi = s_off + bw[0] * 128, s_off + (bw[-1] + 1) * 128
                        MM(pd[:, off + lo - c_lo: off + hi - c_lo], qk4[3][:, x, jsl], qk4[1][:, lo:hi],
                           True, True, [qk4_t[3], qk4_t[1]], [ptile])
                    if fw:
                        lo, hi = s_off + fw[0] * 128, s_off + (fw[-1] + 1) * 128
                        MM(pd[:, off + lo - c_lo: off + hi - c_lo], qk4[2][:, x, jsl], qk4[0][:, lo:hi],
                           True, True, [qk4_t[2], qk4_t[0]], [ptile])
                    dlo = None
                    if jt in its:
                        dlo = s_off + jt * 128 - c_lo
                        TTO("dve", pT[j][:, dlo:dlo + 128], pd[:, off + dlo:off + dlo + 128], cdg[:, x, :], ALU.mult,
                            [ptile, cdg_t], [pT_t[j]])
                    segs = [(0, blk)] if dlo is None else [(0, dlo), (dlo + 128, blk)]
                    ceng = "act" if (jt % 2 == 0) else "dve"
                    for (lo, hi) in segs:
                        if hi > lo:
                            CP(ceng, pT[j][:, lo:hi], pd[:, off + lo:off + hi], [ptile], [pT_t[j]])
                    return j

                def s2(jb, j, b=b, x=x, opd=opd, optl=optl):
                    sq_, ib, jt = jb
                    s_off = sq_ * slen
                    c_lo = s_off + ib * blk
                    ob = c_lo // 512
                    first = jt == 0
                    if half == 1 and jt == 0:
                        for d_ in range(2):
                            MM(opd[:, c_lo:c_lo + blk], s0s[:, x, d_, :], qk4[d_][:, c_lo:c_lo + blk],
                               d_ == 0, False, [s0s_t, qk4_t[d_]], [optl[ob]])
                        first = False
                    tt = sq_ * ntl + jt
                    MM(opd[:, c_lo:c_lo + blk], vtok[:, tt, x * 128:(x + 1) * 128], pT[j][:, 0:blk],
                       first, jt == ntl - 1, [vtok_t[tt], pT_t[j]], [optl[ob]])
                    if half == 0 and jt == ntl - 1:
                        for d_ in range(2):
                            spd, soff, sptile = K.bank(7)
                            for jt2 in range(2):
                                tt2 = sq_ * 2 + jt2
                                MM(spd[64 * d_:64 * d_ + 64, soff + sq_ * 128: soff + (sq_ + 1) * 128],
                                   kd[d_][:, tt2, x * 64:(x + 1) * 64], vtok[:, tt2, x * 128:(x + 1) * 128], jt2 == 0, jt2 == 1,
                                   [kd_t[d_], vtok_t[tt2]], [sptile])

                pipeline(jobs, s1, s2, 3, K.side)
                if half == 0:
                    spd, soff, sptile = K.bank(7)
                    CP("dve", stg[:], spd[:, soff:soff + 512], [sptile], [stg_t])
                    for d_, dst in ((0, sf_out), (1, sb_out)):
                        DMA("sp", dst[:, h, :, :].rearrange("s k v -> k s v"),
                            stg[64 * d_:64 * d_ + 64, :].rearrange("k (s v) -> k s v", s=4), [stg_t], [], stg_t)
                    if stg_t not in K.outs:
                        K.outs.append(stg_t)
                def gn_steps(x=x, h=h, opd=opd, optl=optl):
                    steps = []
                    for bi in range(2):
                        sl = slice(bi * 512, (bi + 1) * 512)

                        def st1(bi=bi, sl=sl):
                            CP("dve", osb[0][:], opd[:, sl], [optl[bi]], [osb_t[0]])
                            CP("dve", obf[0][:, 0, :], opd[:, sl], [optl[bi]], [obf_t[0]])
                            ACT(obf[0][:, 1, :], opd[:, sl], AF.Square, [optl[bi]], [obf_t[0]])

                        def st2():
                            pd1, off1, pt1 = rbank()
                            MM(pd1[:, off1:off1 + 512], K.ones_bf[:], obf[0][:, 0, :], True, True, [K.ones_t, obf_t[0]], [pt1])
                            pd2, off2, pt2 = rbank()
                            MM(pd2[:, off2:off2 + 512], K.ones_bf[:], obf[0][:, 1, :], True, True, [K.ones_t, obf_t[0]], [pt2])
                            ACT(gnb[:], pd1[:, off1:off1 + 512], AF.Square, [pt1], [gnb_t], scale=1.0 / 128.0)
                            STT("dve", gnb[:], pd2[:, off2:off2 + 512], 1.0 / 128.0, gnb[:], ALU.mult, ALU.subtract,
                                [pt2, gnb_t], [gnb_t])
                            STT("dve", osb[0][:], pd1[:, off1:off1 + 512], -1.0 / 128.0, osb[0][:], ALU.mult, ALU.add,
                                [pt1, osb_t[0]], [osb_t[0]])

                        def st3():
                            ACT(gnb[:], gnb[:], AF.Ln, [gnb_t, K.epsc_t], [gnb_t], bias=K.epsc[:, 1:2])
                            ACT(gnb[:], gnb[:], AF.Exp, [gnb_t], [gnb_t], scale=-0.5)

                        def st4(sl=sl):
                            TTO("pool", gnb[:], gnb[:], gate[:, x, sl], ALU.mult, [gnb_t, gate_t[x]], [gnb_t])
                            STT("dve", concT[:, h, sl], osb[0][:], evs[:, 8 + h:9 + h], gnb[:], ALU.mult, ALU.mult,
                                [osb_t[0], evs_t, gnb_t], [concT_t[h]])

                        steps += [st1, st2, st3, st4]
                    return steps

                K.side.extend([(x, f_) for f_ in gn_steps()])
        K.side_flush()
        rst.close()
        S.fence()
        gst = contextlib.ExitStack()
        gs_ = lambda n, sh, dt=F32: K.sbs(gst, n, sh, dt)
        qa = gs_("qa", [128, 4, NH], BF16)
        qa_t = [S.tile(f"qa{c}") for c in range(4)]
        ka = gs_("ka", [128, 2, NH], BF16)
        ka_t = S.tile("ka")
        MSET("pool", ka[64:128, 0, :], 0.0, [ka_t])
        MSET("pool", ka[0:64, 1, :], 0.0, [ka_t])
        vaug = gs_("vaug", [128, 8, 256], BF16)
        vaug_t = [S.tile(f"vaug{t}") for t in range(8)]
        MSET("pool", vaug[:, :, 64:192], 1.0, vaug_t)
        rden = [gs_(f"rden{i}", [128, 512], F32) for i in range(2)]
        rden_t = [S.tile(f"rden{i}") for i in range(2)]
        if half == 0:
            kvs = [gs_(f"kvs{i}", [128, 256], F32) for i in range(2)]
            kvs_t = [S.tile(f"kvs{i}") for i in range(2)]
            sm = [gs_(f"sm{i}", [128, 4], F32) for i in range(2)]
            sm_t = [S.tile(f"sm{i}") for i in range(2)]
        else:
            vaugc = gs_("vaugc", [128, 2, 256], BF16)
            vaugc_t = S.tile("vaugc")
            MSET("pool", vaugc[:, :, 64:192], 1.0, [vaugc_t])
            kcr = gs_("kcr", [128, 2, 2, 64], F32)
            kcr_t = S.tile("kcr")
            kcT = gs_("kcT", [128, 2, 256], BF16)
            kcT_t = S.tile("kcT")
            MSET("pool", kcT[64:128, 0, :], 0.0, [kcT_t])
            MSET("pool", kcT[0:64, 1, :], 0.0, [kcT_t])
        K.ada_some(2)
        wC, wC_t = load_w(4096, 512)
        wD, wD_t = (load_w(4096 + 512, 512, 1) if rope else (None, None))

        def qk_fm(w, w_t, col, wsw, wsw_t, colsw, gcol, dst, dst_t, tb, sl, padded=False):
            def fin(fn):
                if not padded:
                    fn(dst[:, sl], slice(0, 128))
                else:
                    for x_ in range(2):
                        fn(dst[64 * x_:64 * x_ + 64, x_, sl], slice(64 * x_, 64 * x_ + 64))

            pd, off, ptile = proj_fm(w, w_t, col, tb)
            j = rot("pT", 4)
            ACT(pT[j][:], pd[:, off:off + 512], AF.Square, [ptile], [pT_t[j]])
            pds, offs, pts = rbank()
            MM(pds[:, offs:offs + 512], bones[:], pT[j][:], True, True, [bones_t, pT_t[j]], [pts])
            i0 = rot("tp", 3)
            ACT(rp[i0][:], pds[:, offs:offs + 512], AF.Ln, [pts, K.epsc_t], [rp_t[i0]], scale=1.0 / 64.0, bias=K.epsc[:, 0:1])
            ACT(rp[i0][:], rp[i0][:], AF.Exp, [rp_t[i0]], [rp_t[i0]], scale=-0.5)
            if not rope:
                fin(lambda o_, ph: STT("dve", o_, pd[ph, off:off + 512], evs[ph, gcol:gcol + 1], rp[i0][ph, :], ALU.mult, ALU.mult,
                                       [ptile, evs_t, rp_t[i0]], [dst_t]))
            else:
                pd2, off2, ptile2 = proj_fm(wsw, wsw_t, colsw, tb)
                i1, i2 = rot("tp", 3), rot("tp", 3)
                STT("dve", gtm[i1][:], pd[:, off:off + 512], evs[:, gcol:gcol + 1], COS[:, sl], ALU.mult, ALU.mult,
                    [ptile, evs_t, cs_t], [gtm_t[i1]])
                STT("dve", gtm[i2][:], pd2[:, off2:off2 + 512], evs[:, gcol + 1:gcol + 2], SIN[:, sl], ALU.mult, ALU.mult,
                    [ptile2, evs_t, cs_t], [gtm_t[i2]])
                TTO("pool", gtm[i1][:], gtm[i1][:], gtm[i2][:], ALU.add, [gtm_t[i1], gtm_t[i2]], [gtm_t[i1]])
                fin(lambda o_, ph: TTO("pool", o_, gtm[i1][ph, :], rp[i0][ph, :], ALU.mult, [gtm_t[i1], rp_t[i0]], [dst_t]))

        for bi in range(2):
            tb = tb0 + bi
            sl = slice(bi * 512, (bi + 1) * 512)
            for c in range(4):
                qk_fm(wC, wC_t, c * 128, wD, wD_t, c * 128, 16, qa[:, c, :], qa_t[c], tb, sl)
        wE, wE_t = load_w(4096 + 1024, 384)
        for bi in range(2):
            tb = tb0 + bi
            sl = slice(bi * 512, (bi + 1) * 512)
            qk_fm(wE, wE_t, 128, wE, wE_t, 0, 18, ka, ka_t, tb, sl, padded=True)
        for tt in range(8):
            tb = tb0 + tt // 4
            tsl = slice(T0 + tt * 128, T0 + (tt + 1) * 128)
            pd, off, ptile = rbank()
            for k in range(8):
                MM(pd[:, off:off + 256], hT[:, k, tsl], wE[:, k, 128:384], k == 0, k == 7, [wE_t, hT_t[tb]], [ptile])
            CP("act", vaug[:, tt, 0:64], pd[:, off + 128:off + 192], [ptile], [vaug_t[tt]])
            CP("act", vaug[:, tt, 192:256], pd[:, off + 192:off + 256], [ptile], [vaug_t[tt]])
            if half == 0:
                j = rot("kvs", 2)
                i1 = rot("tp", 3)
                si = rot("sm", 2)
                ACT(rp[i1][:, 0:128], pd[:, off:off + 128], AF.Square, [ptile], [rp_t[i1]])
                S.op("dve", lambda e, o=sm[si][:, 0:2], i_=rp[i1][:, 0:128].rearrange("p (h d) -> p h d", h=2):
                     e.tensor_reduce(out=o, in_=i_, axis=AX.X, op=ALU.add), [rp_t[i1]], [sm_t[si]])
                ACT(sm[si][:, 0:2], sm[si][:, 0:2], AF.Ln, [sm_t[si], K.epsc_t], [sm_t[si]], scale=1.0 / 64.0, bias=K.epsc[:, 0:1])
                ACT(sm[si][:, 0:2], sm[si][:, 0:2], AF.Exp, [sm_t[si]], [sm_t[si]], scale=-0.5)
                for kv in range(2):
                    STT("dve", kvs[j][:, kv * 64:(kv + 1) * 64], pd[:, off + kv * 64: off + (kv + 1) * 64],
                        sm[si][:, kv:kv + 1], evr[:, 16:80], ALU.mult, ALU.mult, [ptile, sm_t[si], evr_t], [kvs_t[j]])
                CP("dve", kvs[j][:, 128:256], pd[:, off + 128:off + 256], [ptile], [kvs_t[j]])
                sq_, t0 = tt // 2, (tt % 2) * 128
                DMA("sp", gk_out[sq_, :, t0:t0 + 128, :].rearrange("h t d -> t h d"),
                    kvs[j][:, 0:128].rearrange("p (h d) -> p h d", h=2), [kvs_t[j]], [], kvs_t[j])
                DMA("sp", gv_out[sq_, :, t0:t0 + 128, :].rearrange("h t d -> t h d"),
                    kvs[j][:, 128:256].rearrange("p (h d) -> p h d", h=2), [kvs_t[j]], [], kvs_t[j])
                if kvs_t[j] not in K.outs:
                    K.outs.append(kvs_t[j])
        if half == 1:
            for x in range(2):
                DMA("sp", kcr[:, :, x, :], cgk[x].rearrange("(t p) d -> p t d", p=128), [], [kcr_t], kcr_t)
                DMA("pool", bass.AP(tensor=vaugc, offset=192 * x, ap=[[512, 128], [256, 2], [1, 64]]),
                    cgv[x].rearrange("(t p) d -> p t d", p=128), [], [vaugc_t], vaugc_t)
            pd, off, ptile = rbank()
            for t in range(2):
                TR(pd[:, off + t * 128: off + (t + 1) * 128], kcr[:, t, :, :].rearrange("p h d -> p (h d)"), K.ident,
                   [kcr_t, K.cst_t], [ptile])
            CP("dve", kcT[0:64, 0, :], pd[0:64, off:off + 256], [ptile], [kcT_t])
            CP("dve", kcT[64:128, 1, :], pd[64:128, off:off + 256], [ptile], [kcT_t])
        blk = min(slen, 512)
        jobs = []
        for c in range(4):
            for x in range(2):
                grp = []
                for sq_ in range(nseq):
                    for ib in range(slen // blk):
                        keys = [("s", sq_ * (slen // 128) + jt) for jt in range(slen // 128)]
                        if half == 1:
                            keys += [("c", 0), ("c", 1)]
                        for ki, (kind, tt) in enumerate(keys):
                            grp.append([c, x, sq_ * slen + ib * blk, kind, tt, ki == 0, ki == len(keys) - 1, False])
                grp[-1][-1] = True
                jobs += [tuple(g) for g in grp]

        def s1(jb):
            c, x, c_lo, kind, tt, isfirst, islast, isend = jb
            b = 64 * x
            pd, off, ptile = rbank()
            if kind == "s":
                MM(pd[:, off:off + blk], ka[:, x, tt * 128:(tt + 1) * 128], qa[:, c, c_lo:c_lo + blk],
                   True, True, [ka_t, qa_t[c]], [ptile])
            else:
                MM(pd[:, off:off + blk], kcT[:, x, tt * 128:(tt + 1) * 128], qa[:, c, c_lo:c_lo + blk],
                   True, True, [kcT_t, qa_t[c]], [ptile])
            j = rot("pT", 4)
            ACT(pT[j][:, 0:blk], pd[:, off:off + blk], AF.Exp, [ptile], [pT_t[j]], scale=0.125)
            return j

        def s2(jb, j):
            c, x, c_lo, kind, tt, isfirst, islast, isend = jb
            opd = K.pd[x]
            optl = [K.pt[2 * x], K.pt[2 * x + 1]]
            ob = c_lo // 512
            if kind == "s":
                va, va_t = vaug[:, tt, x * 128:(x + 1) * 128], vaug_t[tt]
            else:
                va, va_t = vaugc[:, tt, x * 128:(x + 1) * 128], vaugc_t
            MM(opd[:, c_lo:c_lo + blk], va, pT[j][:, 0:blk], isfirst, islast, [va_t, pT_t[j]], [optl[ob]])
            if isend:
                jr = rot("rden", 2)
                dn = slice(64, 128) if x == 0 else slice(0, 64)
                obp = slice(0, 64) if x == 0 else slice(64, 128)
                for bi in range(2):
                    sl = slice(bi * 512, (bi + 1) * 512)
                    RCP(rden[jr][obp, :], opd[dn, sl], [optl[bi]], rden_t[jr])
                    TTO("dve", concT[obp, 8 + c, sl], opd[obp, sl], rden[jr][obp, :], ALU.mult, [optl[bi], rden_t[jr]],
                        [concT_t[8 + c]])

        pipeline(jobs, s1, s2, 3)
        gst.close()
        for dq in range(4):
            w, w_t = K.wnext("even_w_out", (slice(None), slice(None), slice(dq * 256, (dq + 1) * 256)), 12, 256)
            for dc in range(2):
                dmc = dq * 2 + dc
                for bi in range(2):
                    tb = tb0 + bi
                    pd, off, ptile = rbank()
                    for c in range(12):
                        MM(pd[:, off:off + 512], w[:, c, dc * 128:(dc + 1) * 128], concT[:, c, bi * 512:(bi + 1) * 512],
                           c == 0, c == 11, [w_t, concT_t[c]], [ptile])
                    STT("dve", xT[:, dmc, tb * 512:(tb + 1) * 512], pd[:, off:off + 512], K.mod[:, l, 2, dmc, half:half + 1],
                        xT[:, dmc, tb * 512:(tb + 1) * 512], ALU.mult, ALU.add, [ptile, K.mod_t[l], xT_t[tb]], [xT_t[tb]])
        hst.close()
        S.fence()
    K.bank_alloc = K.next_bank
    st.close()
    S.fence()


def pipeline(jobs, stage1, stage2, depth=2, side=None):
    pend = []
    for jb in jobs:
        pend.append((jb, stage1(jb)))
        if len(pend) > depth:
            j0, h0 = pend.pop(0)
            stage2(j0, h0)
            if side:
                side.pop(0)[1]()
    for j0, h0 in pend:
        stage2(j0, h0)
        if side:
            side.pop(0)[1]()


def na_query_rows(kr):
    rows = [r for r in range(16) if min(max(r - 4, 0), 8) <= kr <= min(max(r - 4, 0), 8) + 7]
    return rows[0], rows[-1]


def odd_mixer(K, l):
    nc, S = K.nc, K.S
    MM, TR, ACT, TTO, STT, TS, CP, MSET, DMA = K.MM, K.TR, K.ACT, K.TTO, K.STT, K.TS, K.CP, K.MSET, K.DMA
    RCP = K.RCP
    hT, hT_t, xT, xT_t = K.hT, K.hT_t, K.xT, K.xT_t
    rpbx = K.din("rpbx", [16, 128, 15, 64])
    cmask_d = K.din("colmask", [128, 64])
    ck = K.din("cache_na_k", [16, 256, 64])
    cv = K.din("cache_na_v", [16, 256, 64])
    nk_out = K.dout("nk_out", [4, 16, 256, 64])
    nv_out = K.dout("nv_out", [4, 16, 256, 64])

    K.norm_mod(l, 0, [0, 1, 2, 3])
    st = contextlib.ExitStack()
    sbs = lambda n, sh, dt=F32: K.sbs(st, n, sh, dt)
    concT = sbs("concT", [128, 8, NT], BF16)
    concT_t = [[S.tile(f"conc{c}_{b}") for b in range(5)] for c in range(8)]
    qT = sbs("qT", [128, NT], BF16)
    kT = sbs("kT", [128, 2, NT], BF16)
    qT_t = [S.tile(f"qT{b}") for b in range(4)]
    kT_t = [S.tile(f"kT{b}") for b in range(4)]
    vaug = sbs("vaug", [128, 16, 256], BF16)
    vaug_t = [S.tile(f"vaug{t}") for t in range(16)]
    vaugc = sbs("vaugc", [128, 2, 256], BF16)
    vaugc_t = S.tile("vaugc")
    kcr = sbs("kcr", [128, 2, 2, 64], F32)
    kcr_t = S.tile("kcr")
    kcT = sbs("kcT", [128, 2, 256], BF16)
    kcT_t = S.tile("kcT")
    cmask = sbs("cmask", [128, 64], F32)
    cmask_t = S.tile("cmask")
    bt = [sbs(f"bt{i}", [128, 16, 64], F32) for i in range(2)]
    bt_t = [S.tile(f"bt{i}") for i in range(2)]
    pT = [sbs(f"pT{i}", [128, 512], BF16) for i in range(4)]
    pT_t = [S.tile(f"pT{i}") for i in range(4)]
    tmp = [sbs(f"tmp{i}", [128, 512], F32) for i in range(2)]
    tmp_t = [S.tile(f"tmp{i}") for i in range(2)]
    rden = [sbs("rden0", [128, 512], F32)] * 2
    rden_t = [S.tile("rden0")] * 2
    kvs = [sbs(f"kvs{i}", [128, 256], F32) for i in range(2)]
    kvs_t = [S.tile(f"kvs{i}") for i in range(2)]
    rr = dict(p=0, t=0, r=0, k=0, b=4)

    def rot(key, n):
        i = rr.get(key, 0) % n
        rr[key] = rr.get(key, 0) + 1
        return i

    def rbank():
        i = 4 + rot("b", 4)
        return K.bank(i)

    DMA("sp", cmask[:], cmask_d, [], [cmask_t], cmask_t)
    MSET("pool", kT[64:128, 0, :], 0.0, kT_t)
    MSET("pool", kT[0:64, 1, :], 0.0, kT_t)
    MSET("pool", kcT[64:128, 0, :], 0.0, [kcT_t])
    MSET("pool", kcT[0:64, 1, :], 0.0, [kcT_t])
    MSET("pool", vaug[:, :, 64:192], 1.0, vaug_t)
    MSET("pool", vaugc[:, :, 64:192], 1.0, [vaugc_t])

    for hp in range(8):
        w, w_t = K.wnext("odd_w_in", (slice(None), slice(None), slice(hp * 384, (hp + 1) * 384)), 8, 384)
        for x in range(2):
            DMA("sp", bt[x][0:64, 0:15, :], rpbx[2 * hp + x, 0:64], [], [bt_t[x]], bt_t[x])
            DMA("sp", bt[x][64:128, 1:16, :], rpbx[2 * hp + x, 64:128], [], [bt_t[x]], bt_t[x])
            TTO("pool", bt[x][0:64, 0:15, :], bt[x][0:64, 0:15, :], cmask[0:64, :].unsqueeze(1).broadcast_to([64, 15, 64]),
                ALU.add, [bt_t[x], cmask_t], [bt_t[x]])
            TTO("pool", bt[x][64:128, 1:16, :], bt[x][64:128, 1:16, :],
                cmask[64:128, :].unsqueeze(1).broadcast_to([64, 15, 64]), ALU.add, [bt_t[x], cmask_t], [bt_t[x]])
        for x in range(2):
            DMA("sp", kcr[:, :, x, :], ck[2 * hp + x].rearrange("(t p) d -> p t d", p=128), [], [kcr_t], kcr_t)
            DMA("pool", bass.AP(tensor=vaugc, offset=192 * x, ap=[[512, 128], [256, 2], [1, 64]]),
                cv[2 * hp + x].rearrange("(t p) d -> p t d", p=128), [], [vaugc_t], vaugc_t)
        if K.cfg.get("ostop", 9) <= 1:
            continue
        for tb in range(4):
            sl = slice(tb * 512, (tb + 1) * 512)
            for which, dst, dst_t in ((0, qT, qT_t), (1, kT, kT_t)):
                pd, off, ptile = rbank()
                for k in range(8):
                    MM(pd[:, off:off + 512], w[:, k, which * 128:(which + 1) * 128], hT[:, k, sl], k == 0, k == 7,
                       [w_t, hT_t[tb]], [ptile])
                if which == 0:
                    ACT(dst[:, sl], pd[:, off:off + 512], AF.Copy, [ptile], [dst_t[tb]], scale=0.125)
                else:
                    CP("dve", dst[0:64, 0, sl], pd[0:64, off:off + 512], [ptile], [dst_t[tb]])
                    CP("dve", dst[64:128, 1, sl], pd[64:128, off:off + 512], [ptile], [dst_t[tb]])
        if K.cfg.get("ostop", 9) <= 2:
            continue
        for tt in range(16):
            tb = tt // 4
            pd, off, ptile = rbank()
            c0 = 128 if tt < 8 else 256
            ncol = 256 if tt < 8 else 128
            for k in range(8):
                MM(pd[:, off:off + ncol], hT[:, k, tt * 128:(tt + 1) * 128], w[:, k, c0:384], k == 0, k == 7,
                   [w_t, hT_t[tb]], [ptile])
            vo = off + ncol - 128
            CP("act", vaug[:, tt, 0:64], pd[:, vo:vo + 64], [ptile], [vaug_t[tt]])
            CP("act", vaug[:, tt, 192:256], pd[:, vo + 64:vo + 128], [ptile], [vaug_t[tt]])
            nd = K.cfg.get("nodma", 0)
            if tt < 8 and nd != 1:
                j = rot("k", 2)
                CP(K.cfg.get("kveng", "dve") if isinstance(K.cfg.get("kveng", "dve"), str) else ("act" if K.cfg["kveng"] else "dve"), kvs[j][:], pd[:, off:off + 256], [ptile], [kvs_t[j]])
                if nd == 2:
                    continue
                sq_, t0 = tt // 2, (tt % 2) * 128
                DMA("sp", nk_out[sq_, 2 * hp:2 * hp + 2, t0:t0 + 128, :].rearrange("h t d -> t h d"),
                    kvs[j][:, 0:128].rearrange("p (h d) -> p h d", h=2), [kvs_t[j]], [], kvs_t[j])
                DMA("sp", nv_out[sq_, 2 * hp:2 * hp + 2, t0:t0 + 128, :].rearrange("h t d -> t h d"),
                    kvs[j][:, 128:256].rearrange("p (h d) -> p h d", h=2), [kvs_t[j]], [], kvs_t[j])
                if kvs_t[j] not in K.outs:
                    K.outs.append(kvs_t[j])
        if K.cfg.get("ostop", 9) <= 3:
            continue
        pd, off, ptile = rbank()
        for t in range(2):
            TR(pd[:, off + t * 128: off + (t + 1) * 128], kcr[:, t, :, :].rearrange("p h d -> p (h d)"), K.ident,
               [kcr_t, K.cst_t], [ptile])
        CP("dve", kcT[0:64, 0, :], pd[0:64, off:off + 256], [ptile], [kcT_t])
        CP("dve", kcT[64:128, 1, :], pd[64:128, off:off + 256], [ptile], [kcT_t])
        jobs = []
        for sq_ in range(4):
            for x in range(2):
                jobs.append(("p", sq_, x))
        for x in range(2):
            nj = []
            for kt in range(2):
                for qb in range(2):
                    nj.append(["c", x, kt, qb, qb * 512, (qb + 1) * 512])
            for m in range(8):
                a0, b0 = na_query_rows(2 * m)
                a1, b1 = na_query_rows(2 * m + 1)
                a, bq = min(a0, a1), max(b0, b1)
                lo, hi = a * 64, (bq + 1) * 64
                if lo < 512:
                    nj.append(["w", x, m, 0, lo, min(hi, 512)])
                if hi > 512:
                    nj.append(["w", x, m, 1, max(lo, 512), hi])
            last, first = {}, {}
            for ji, jb in enumerate(nj):
                last[jb[3]] = ji
                first.setdefault(jb[3], ji)
            for ji, jb in enumerate(nj):
                jobs.append(tuple(jb) + (ji == first[jb[3]], ji == last[jb[3]], ji == len(nj) - 1))

        def s1(jb):
            if jb[0] == "p":
                _, sq_, x = jb
                b = 64 * x
                t0 = sq_ * 256
                tb = sq_ // 2
                pd, off, ptile = rbank()
                for kt in range(2):
                    MM(pd[:, off + kt * 256: off + (kt + 1) * 256], kT[:, x, t0 + kt * 128: t0 + (kt + 1) * 128],
                       qT[:, t0:t0 + 256], True, True, [kT_t[tb], qT_t[tb]], [ptile])
                j = rot("p", 4)
                ACT(pT[j][:], pd[:, off:off + 512], AF.Exp, [ptile], [pT_t[j]])
                return j
            kind, x, ka, qb, lo, hi = jb[:6]
            b = 64 * x
            n = hi - lo
            pd, off, ptile = rbank()
            j = rot("p", 4)
            if kind == "c":
                MM(pd[:, off:off + 512], kcT[:, x, ka * 128:(ka + 1) * 128], qT[:, 1024 + lo:1024 + hi],
                   True, True, [kcT_t, qT_t[2 + qb]], [ptile])
                ACT(pT[j][:], pd[:, off:off + 512], AF.Exp, [ptile], [pT_t[j]])
            else:
                m = ka
                s0 = 7 - 2 * m + lo // 64
                MM(pd[:, off:off + n], kT[:, x, 1024 + m * 128:1024 + (m + 1) * 128],
                   qT[:, 1024 + lo:1024 + hi], True, True, [kT_t[2 + m // 4], qT_t[2 + qb]], [ptile])
                i2 = rot("t", 2)
                TTO("dve", tmp[i2][:, 0:n], pd[:, off:off + n],
                    bt[x][:, s0:s0 + n // 64, :].rearrange("p s c -> p (s c)"), ALU.add,
                    [ptile, bt_t[x]], [tmp_t[i2]])
                for par in range(2):
                    a_, b_ = na_query_rows(2 * m + par)
                    for r in range(lo // 64, hi // 64):
                        if r < a_ or r > b_:
                            c_ = (r - lo // 64) * 64
                            MSET("dve", tmp[i2][64 * par:64 * par + 64, c_:c_ + 64], -30000.0, [tmp_t[i2]])
                ACT(pT[j][:, 0:n], tmp[i2][:, 0:n], AF.Exp, [tmp_t[i2]], [pT_t[j]])
            return j

        def s2(jb, j):
            if jb[0] == "p":
                _, sq_, x = jb
                t0 = sq_ * 256
                opd, ooff, optile = K.bank(sq_ % 4)
                for kt in range(2):
                    tt = sq_ * 2 + kt
                    MM(opd[:, ooff + x * 256: ooff + (x + 1) * 256], vaug[:, tt, x * 128:(x + 1) * 128],
                       pT[j][:, kt * 256:(kt + 1) * 256], kt == 0, kt == 1, [vaug_t[tt], pT_t[j]], [optile])
                if x == 1:
                    jr = rot("r", 2)
                    RCP(rden[jr][0:64, 0:256], opd[64:128, ooff:ooff + 256], [optile], rden_t[jr])
                    RCP(rden[jr][64:128, 0:256], opd[0:64, ooff + 256:ooff + 512], [optile], rden_t[jr])
                    TTO("dve", concT[0:64, hp, t0:t0 + 256], opd[0:64, ooff:ooff + 256], rden[jr][0:64, 0:256], ALU.mult,
                        [optile, rden_t[jr]], [concT_t[hp][sq_]])
                    TTO("dve", concT[64:128, hp, t0:t0 + 256], opd[64:128, ooff + 256:ooff + 512], rden[jr][64:128, 0:256],
                        ALU.mult, [optile, rden_t[jr]], [concT_t[hp][sq_]])
                return
            kind, x, ka, qb, lo, hi, isfirst, islast, isend = jb
            n = hi - lo
            opd = K.pd[x]
            optl = [K.pt[2 * x], K.pt[2 * x + 1]]
            if kind == "c":
                MM(opd[:, lo:hi], vaugc[:, ka, x * 128:(x + 1) * 128], pT[j][:], isfirst, islast,
                   [vaugc_t, pT_t[j]], [optl[qb]])
            else:
                tt = 8 + ka
                MM(opd[:, lo:hi], vaug[:, tt, x * 128:(x + 1) * 128], pT[j][:, 0:n],
                   isfirst, islast, [vaug_t[tt], pT_t[j]], [optl[qb]])
            if isend:
                jr = rot("r", 2)
                for qb2 in range(2):
                    sl = slice(qb2 * 512, (qb2 + 1) * 512)
                    dn = slice(64, 128) if x == 0 else slice(0, 64)
                    ob = slice(0, 64) if x == 0 else slice(64, 128)
                    RCP(rden[jr][ob, :], opd[dn, sl], [optl[qb2]], rden_t[jr])
                    TTO("dve", concT[ob, hp, 1024 + qb2 * 512:1024 + (qb2 + 1) * 512], opd[ob, sl], rden[jr][ob, :],
                        ALU.mult, [optl[qb2], rden_t[jr]], [concT_t[hp][4]])

        pipeline(jobs, s1, s2, 3)
    for dh in range(2 if K.cfg.get("ostop", 9) > 5 else 0):
        w, w_t = K.wnext("odd_w_out", (slice(None), slice(None), slice(dh * 512, (dh + 1) * 512)), 8, 512)
        for dc in range(4):
            dmc = dh * 4 + dc
            for tb in range(4):
                g = 0 if tb < 2 else 1
                pd, off, ptile = rbank()
                for c in range(8):
                    rd = [concT_t[c][2 * tb], concT_t[c][2 * tb + 1]] if tb < 2 else [concT_t[c][4]]
                    MM(pd[:, off:off + 512], w[:, c, dc * 128:(dc + 1) * 128], concT[:, c, tb * 512:(tb + 1) * 512],
                       c == 0, c == 7, [w_t] + rd, [ptile])
                STT("dve", xT[:, dmc, tb * 512:(tb + 1) * 512], pd[:, off:off + 512], K.mod[:, l, 2, dmc, g:g + 1],
                    xT[:, dmc, tb * 512:(tb + 1) * 512], ALU.mult, ALU.add, [ptile, K.mod_t[l], xT_t[tb]], [xT_t[tb]])
    st.close()
    S.fence()


def fm(v):
    v = np.asarray(v, np.float32)
    r = v.reshape(*v.shape[:-1], v.shape[-1] // 128, 128)
    r = np.moveaxis(r, -1, 0)
    return np.ascontiguousarray(r)


def wl(W):
    Kd, N = W.shape
    return np.ascontiguousarray(W.reshape(Kd // 128, 128, N).transpose(1, 0, 2))


_PROG = {}


def prep_shared(inp, cfg):
    sh = {}
    sh["ada_w"] = np.stack([wl(inp["ada_w"][l]) for l in range(2)])
    ada_b = np.stack([inp["ada_b"][l].reshape(48, 128).T for l in range(2)], 1).reshape(128, 96)
    gains = np.stack([inp["norm_mix"][0], inp["norm_mix"][1], inp["norm_ffn"][0], inp["norm_ffn"][1],
                      inp["norm_final"]])
    sh["smallp"] = np.ascontiguousarray(np.concatenate([ada_b, fm(gains).reshape(128, 40)], 1), np.float32)
    if cfg["ffn"]:
        perm = np.concatenate([np.concatenate([np.arange(c * 128, (c + 1) * 128),
                                               DFF + np.arange(c * 128, (c + 1) * 128)]) for c in range(NFC)])
        sh["w_up"] = np.stack([wl(inp["ffn_w_up"][l][:, perm]) for l in range(2)])
        sh["w_dn"] = np.stack([wl(inp["ffn_w_down"][l]) for l in range(2)])
        cp = np.zeros((128, 2, 4, 2 * NFC), np.float32)
        for l in range(2):
            for j in range(4):
                v = inp["ffn_conv_w"][l][j] if j < 3 else inp["ffn_conv_b"][l]
                vp = v[perm].reshape(2 * NFC, 128).T
                cp[:, l, j, :] = vp
        sh["convp"] = cp
    if cfg["even"]:
        Wi = np.asarray(inp["even_w_in"][0], np.float32)
        QR, KR, VR, GR, QA, KA, VA = 0, 512, 1024, 2048, 3072, 3584, 3712

        def swp(base, h):
            d = np.arange(64)
            sw = np.where((d % 32) < 16, d + 16, d - 16)
            return base + h * 64 + sw

        cols = []
        for hp in range(4):
            h0, h1 = 2 * hp, 2 * hp + 1
            cols += [QR + h0 * 64 + np.arange(64), QR + h1 * 64 + np.arange(64)]
            cols += [KR + h0 * 64 + np.arange(64), KR + h1 * 64 + np.arange(64)]
            cols += [swp(QR, h0), swp(QR, h1), swp(KR, h0), swp(KR, h1)]
            cols += [GR + h0 * 128 + np.arange(128), GR + h1 * 128 + np.arange(128)]
            cols += [VR + h0 * 128 + np.arange(128), VR + h1 * 128 + np.arange(128)]
        for c in range(4):
            cols += [QA + c * 64 + np.arange(64), QA + (c + 4) * 64 + np.arange(64)]
        for c in range(4):
            cols += [swp(QA, c), swp(QA, c + 4)]
        cols += [swp(KA, 0), swp(KA, 1), KA + np.arange(128), VA + np.arange(128)]
        cols = np.concatenate(cols)
        assert cols.shape[0] == 4 * 1024 + 512 + 512 + 384
        sh["even_w_in"] = wl(Wi[:, cols])
        Wo = np.asarray(inp["even_w_out"][0], np.float32)
        rows = [np.arange(1024)]
        for c in range(4):
            rows += [1024 + c * 64 + np.arange(64), 1024 + (c + 4) * 64 + np.arange(64)]
        sh["even_w_out"] = wl(Wo[np.concatenate(rows)])
        es = np.zeros((128, 24), np.float32)
        df, db = np.asarray(inp["ret_decay_fwd"][0]), np.asarray(inp["ret_decay_bwd"][0])
        for hp in range(4):
            for x in range(2):
                es[64 * x:64 * x + 64, hp * 2 + 0] = df[2 * hp + x]
                es[64 * x:64 * x + 64, hp * 2 + 1] = db[2 * hp + x]
        es[:, 8:16] = np.asarray(inp["ret_gn"][0]).reshape(8, 128).T
        d = np.arange(64)
        sw = np.where((d % 32) < 16, d + 16, d - 16)
        gq, gk = np.asarray(inp["gqa_q_norm"][0]), np.asarray(inp["gqa_k_norm"][0])
        es[:, 16] = np.concatenate([gq, gq]); es[:, 17] = np.concatenate([gq[sw], gq[sw]])
        es[:, 18] = np.concatenate([gk, gk]); es[:, 19] = np.concatenate([gk[sw], gk[sw]])
        sh["evsmall"] = es
        er = np.zeros((128, 84), np.float32)
        er[:, 0:8] = df[None, :]; er[:, 8:16] = db[None, :]
        er[:, 16:80] = gk[None, :]
        p = np.arange(128)
        er[:, 80] = 255 - p; er[:, 81] = 255 - (128 + p); er[:, 82] = p; er[:, 83] = 128 + p
        sh["evrow"] = er
        tabs = np.zeros((128, 3 * 1024 + 256), np.float32)
        tabs[:, 0:1024] = (np.arange(1024) - 512)[None, :]
        t = np.arange(1024)
        row = (t // 64).astype(np.float32); col = (t % 64).astype(np.float32)
        inv = (10000.0 ** (-np.arange(0, 32, 2, dtype=np.float32) / 32.0)).astype(np.float32)
        ang_r = (row[:, None] * inv[None, :]).astype(np.float32)
        ang_c = (col[:, None] * inv[None, :]).astype(np.float32)
        cosd = np.zeros((64, 1024), np.float32); sind = np.zeros((64, 1024), np.float32)
        for dd in range(64):
            ang = ang_r if dd < 32 else ang_c
            i = dd % 16
            cosd[dd] = np.cos(ang[:, i])
            sind[dd] = np.sin(ang[:, i]) * (-1.0 if (dd % 32) < 16 else 1.0)
        tabs[:, 1280:2304] = np.concatenate([cosd, cosd], 0)
        tabs[:, 2304:3328] = np.concatenate([sind, sind], 0)
        jj = np.arange(128)[:, None]; ii = np.arange(128)[None, :]
        tabs[:, 1024:1152] = np.maximum(jj - ii, 0).astype(np.float32)
        sh["evtab"] = tabs
    if cfg["odd"]:
        Wi = inp["odd_w_in"][0]
        cols = []
        for hp in range(8):
            for part in range(3):
                cols.append(part * 1024 + hp * 128 + np.arange(128))
        sh["odd_w_in"] = wl(Wi[:, np.concatenate(cols)])
        sh["odd_w_out"] = wl(inp["odd_w_out"][0])
        rpb = np.asarray(inp["na_rpb"][0], np.float32)
        kc = np.arange(64)[:, None]
        cc = np.arange(64)[None, :]
        relc = kc - cc + 15
        okc = (relc >= 0) & (relc <= 30)
        relc_c = np.clip(relc, 0, 30)
        rx = np.zeros((16, 128, 15, 64), np.float32)
        for s_ in range(15):
            g = np.where(okc[None], rpb[:, 14 - s_, :][:, relc_c], np.float32(0.0))
            rx[:, 0:64, s_, :] = g
            rx[:, 64:128, s_, :] = g
        sh["rpbx"] = rx
        cs = np.clip(np.arange(64) - 8, 0, 48)[None, :]
        win = (kc >= cs) & (kc < cs + 16)
        cm = np.where(win, 0.0, -30000.0).astype(np.float32)
        sh["colmask"] = np.ascontiguousarray(np.concatenate([cm, cm], 0))
    return sh


def prep_core(inp, i, cfg):
    m = {}
    xp = np.asarray(inp["x_prompt"][4 * i:4 * i + 4], np.float32).reshape(1024, D)
    xsm = np.asarray(inp["x_sample"][i], np.float32)
    m["x_tok"] = np.ascontiguousarray(np.concatenate([xp, xsm], 0))
    cond = np.stack([inp["c_ctx"], inp["c"][i]], -1)
    condT = cond.reshape(8, 128, 2).transpose(1, 0, 2).reshape(128, 16)
    m["consts"] = np.ascontiguousarray(np.concatenate([np.eye(128, dtype=np.float32), condT], 1), np.float32)
    if cfg["even"]:
        m["cache_gqa_k"] = np.ascontiguousarray(inp["cache_gqa_k"][i, 0], np.float32)
        m["cache_gqa_v"] = np.ascontiguousarray(inp["cache_gqa_v"][i, 0], np.float32)
        m["state_ret_fwd"] = np.ascontiguousarray(inp["state_ret_fwd"][i, 0], np.float32)
        m["state_ret_bwd"] = np.ascontiguousarray(inp["state_ret_bwd"][i, 0], np.float32)
    if cfg["odd"]:
        m["cache_na_k"] = np.ascontiguousarray(inp["cache_na_k"][i, 0], np.float32)
        m["cache_na_v"] = np.ascontiguousarray(inp["cache_na_v"][i, 0], np.float32)
    return m


def run(inp, cfg):
    key = tuple(sorted(cfg.items()))
    if key not in _PROG:
        plan = build_program(cfg, None)
        _PROG[key] = build_program(cfg, plan)
    nc = _PROG[key]
    inp = {k: np.asarray(v) for k, v in inp.items()}
    sh = prep_shared(inp, cfg)
    in_maps = []
    for i in range(8):
        m = dict(sh)
        m.update(prep_core(inp, i, cfg))
        in_maps.append(m)
    ncores = cfg.get("ncores", 8)
    res = run_bass_kernel_spmd(nc, in_maps[:ncores], core_ids=list(range(ncores)))
    return res.results


def kernel(**inputs):
    cfg = dict(CFG_DEFAULT)
    r = run(inputs, cfg)
    y = np.stack([r[i]["y_out"] for i in range(8)])
    y_prompt = np.ascontiguousarray(y[:, :1024].reshape(32, 256, D))
    y_sample = np.ascontiguousarray(y[:, 1024:])

    def cat(name):
        return np.ascontiguousarray(np.concatenate([r[i][name] for i in range(8)], 0)[:, None])

    return (y_prompt, y_sample, cat("sf_out"), cat("sb_out"), cat("gk_out"), cat("gv_out"),
            cat("nk_out"), cat("nv_out"))
```

```python
import contextlib
import numpy as np
import concourse.bass as bass
import concourse.mybir as mybir
from concourse.bass_utils import run_bass_kernel_spmd

F32 = mybir.dt.float32
BF16 = mybir.dt.bfloat16
AF = mybir.ActivationFunctionType
ALU = mybir.AluOpType
AX = mybir.AxisListType

ENGS = ("pe", "act", "dve", "pool", "sp")


class TT:
    __slots__ = ("name", "w", "r", "sem", "semcnt", "excl")

    def __init__(self, name):
        self.name = name
        self.excl = False
        self.w = None
        self.r = []
        self.sem = None
        self.semcnt = 0


class Op:
    __slots__ = ("eng", "fn", "deps", "sig", "cnt", "dma_tile", "idx")


class Sched:
    def __init__(self, nc, same_engine_sync=True):
        self.nc = nc
        self.ops = []
        self.same = same_engine_sync
        self.n_tiles = 0
        self.fence_deps = set()
        self.fence_start = 0

    def tile(self, name=None):
        self.n_tiles += 1
        return TT(name or f"t{self.n_tiles}")

    def _add(self, eng, fn, reads, writes, dma_tile=None):
        op = Op()
        op.eng = eng
        op.fn = fn
        op.idx = len(self.ops)
        op.sig = dma_tile is not None
        op.cnt = 0
        op.dma_tile = dma_tile
        deps = set()
        for t in reads:
            if t.w is not None:
                deps.add(t.w)
            if t.excl:
                for r in t.r:
                    if self.ops[r].eng != eng:
                        deps.add(r)
        for t in writes:
            if t.w is not None:
                deps.add(t.w)
            for r in t.r:
                deps.add(r)
        deps |= self.fence_deps
        deps.discard(op.idx)
        op.deps = deps
        for t in reads:
            t.r.append(op.idx)
        for t in writes:
            t.w = op.idx
            t.r = []
        self.ops.append(op)
        return op

    def fence(self):
        lasts = {}
        for op in self.ops:
            if op.dma_tile is None:
                lasts[op.eng] = op.idx
        nd = set(lasts.values())
        for op in self.ops[self.fence_start:]:
            if op.dma_tile is not None:
                nd.add(op.idx)
        self.fence_deps = self.fence_deps | nd
        self.fence_start = len(self.ops)

    def op(self, eng, fn, reads=(), writes=()):
        return self._add(eng, fn, list(reads), list(writes))

    def dma(self, q, out, in_, reads=(), writes=(), tile=None):
        assert tile is not None
        return self._add(q, lambda e: e.dma_start(out=out, in_=in_), list(reads), list(writes), dma_tile=tile)

    def emit(self):
        nc = self.nc
        ops = self.ops
        for op in ops:
            nd = set()
            for d in op.deps:
                p = ops[d]
                if p.dma_tile is None and p.eng == op.eng:
                    if p.eng == "pe" or not self.same:
                        continue
                nd.add(d)
            op.deps = nd
            for d in nd:
                ops[d].sig = True
        esem = {e: nc.alloc_semaphore(f"s_{e}") for e in ENGS}
        ecnt = {e: 0 for e in ENGS}
        for op in ops:
            if op.dma_tile is not None:
                t = op.dma_tile
                if t.sem is None:
                    self.n_dsem = getattr(self, "n_dsem", 0) + 1
                    t.sem = nc.alloc_semaphore(f"d{self.n_dsem}_{t.name}")
                t.semcnt += 16
                op.cnt = t.semcnt
            elif op.sig:
                ecnt[op.eng] += 1
                op.cnt = ecnt[op.eng]
        waited = {e: {} for e in ENGS}

        def emit_engine(ename, eobj):
            wd = waited[ename]
            for op in ops:
                if op.eng != ename:
                    continue
                need = {}
                for d in op.deps:
                    p = ops[d]
                    sem = p.dma_tile.sem if p.dma_tile is not None else esem[p.eng]
                    key = id(sem)
                    if key not in need or need[key][1] < p.cnt:
                        need[key] = (sem, p.cnt)
                for key, (sem, v) in need.items():
                    if wd.get(key, 0) >= v:
                        continue
                    eobj.wait_ge(sem, v)
                    wd[key] = v
                ins = op.fn(eobj)
                if op.dma_tile is not None:
                    ins.then_inc(op.dma_tile.sem, 16)
                elif op.sig:
                    ins.then_inc(esem[ename], 1)

        with nc.Block() as block:
            @block.tensor
            def _(e):
                emit_engine("pe", e)

            @block.scalar
            def _(e):
                emit_engine("act", e)

            @block.vector
            def _(e):
                emit_engine("dve", e)

            @block.gpsimd
            def _(e):
                emit_engine("pool", e)

            @block.sync
            def _(e):
                emit_engine("sp", e)


D = 1024
NT = 2048
DFF = 2816
NFC = 22
EPS = 1e-6
GN_EPS = 1e-5
GROUPS = [(0, 512, 0, 2, 256), (512, 512, 0, 2, 256), (1024, 1024, 1, 1, 1024)]
FFN_PHASES = [(0, 6), (6, 12), (12, 17), (17, 22)]

CFG_DEFAULT = dict(even=True, odd=True, ffn=True, dbg=False)


class Ctx:
    pass


def build_program(cfg, plan=None):
    nc = bass.Bass("TRN2", target_bir_lowering=False)
    S = Sched(nc)
    K = Ctx()
    K.nc, K.S, K.cfg = nc, S, cfg
    K.outs = []

    K.dram = {}

    def din(name, shape, dt=F32):
        a = nc.dram_tensor(name, list(shape), dt, kind="ExternalInput").ap()
        K.dram[name] = a
        return a

    def dout(name, shape):
        return nc.dram_tensor(name, list(shape), F32, kind="ExternalOutput").ap()

    def sb(name, shape, dt=F32):
        return nc.alloc_sbuf_tensor(name, list(shape), dt)

    K.uid = [0]

    def sbs(stack, name, shape, dt=F32):
        K.uid[0] += 1
        return stack.enter_context(nc.sbuf_tensor(f"{name}_{K.uid[0]}", list(shape), dt))

    K.din, K.dout, K.sb, K.sbs = din, dout, sb, sbs

    def MM(out, lhsT, rhs, start, stop, R, W):
        S.op("pe", lambda e: e.matmul(out, lhsT=lhsT, rhs=rhs, start=start, stop=stop), R, W)

    def TR(out, in_, ident, R, W):
        S.op("pe", lambda e: e.transpose(out, in_, ident), R, W)

    def ACT(out, in_, func, R, W, scale=1.0, bias=None):
        if bias is None:
            S.op("act", lambda e: e.activation(out=out, in_=in_, func=func, scale=scale), R, W)
        else:
            S.op("act", lambda e: e.activation(out=out, in_=in_, func=func, scale=scale, bias=bias), R, W)

    def TTO(eng, out, in0, in1, op, R, W):
        S.op(eng, lambda e: e.tensor_tensor(out=out, in0=in0, in1=in1, op=op), R, W)

    def STT(eng, out, in0, scalar, in1, op0, op1, R, W):
        S.op(eng, lambda e: e.scalar_tensor_tensor(out=out, in0=in0, scalar=scalar, in1=in1, op0=op0, op1=op1), R, W)

    def TS(eng, out, in0, s1, s2, op0, op1, R, W):
        if s2 is None:
            S.op(eng, lambda e: e.tensor_scalar(out=out, in0=in0, scalar1=s1, scalar2=None, op0=op0), R, W)
        else:
            S.op(eng, lambda e: e.tensor_scalar(out=out, in0=in0, scalar1=s1, scalar2=s2, op0=op0, op1=op1), R, W)

    def CP(eng, out, in_, R, W):
        if eng == "act":
            S.op("act", lambda e: e.copy(out=out, in_=in_), R, W)
        else:
            S.op(eng, lambda e: e.tensor_copy(out=out, in_=in_), R, W)

    def MSET(eng, ap, val, W):
        S.op(eng, lambda e: e.memset(ap, val), [], W)

    def DMA(q, out, in_, R, W, tile):
        S.dma(q, out, in_, R, W, tile)

    def RCP(out, in_, R, out_t):
        ACT(out, in_, AF.Ln, R, [out_t])
        ACT(out, out, AF.Exp, [out_t], [out_t], scale=-1.0)

    K.RCP = RCP
    K.MM, K.TR, K.ACT, K.TTO, K.STT, K.TS, K.CP, K.MSET, K.DMA = MM, TR, ACT, TTO, STT, TS, CP, MSET, DMA

    K.pd = [nc.alloc_psum_tensor(f"pd{i}", [128, 1024], F32) for i in range(4)]
    K.pt = [S.tile(f"pb{i}") for i in range(8)]
    for t_ in K.pt:
        t_.excl = True
    K.rr = [0]

    def bank(i):
        d, h = divmod(i, 2)
        return K.pd[d], h * 512, K.pt[i]

    def next_bank():
        i = K.rr[0] % 8
        K.rr[0] += 1
        return bank(i)

    def next_dbank():
        if K.rr[0] % 2:
            K.rr[0] += 1
        i = K.rr[0] % 8
        K.rr[0] += 2
        return K.pd[i // 2], [K.pt[i], K.pt[i + 1]]

    K.bank, K.next_bank, K.next_dbank = bank, next_bank, next_dbank
    K.bank_alloc = next_bank

    x_tok = din("x_tok", [NT, D])
    consts = din("consts", [128, 128 + 16])
    smallp = din("smallp", [128, 2 * 48 + 5 * 8])
    ada_w = din("ada_w", [2, 128, 8, 6 * D])
    if cfg["even"]:
        din("even_w_in", [128, 8, 4 * 1024 + 512 + 512 + 384])
        din("even_w_out", [128, 12, D])
    if cfg["ffn"]:
        din("w_up", [2, 128, 8, 2 * DFF])
        din("w_dn", [2, 128, NFC, D])
    if cfg["odd"]:
        din("odd_w_in", [128, 8, 3072])
        din("odd_w_out", [128, 8, D])
    y_out = dout("y_out", [NT, D])

    xT = sb("xT", [128, 8, NT], F32)
    K.xT = xT
    K.xT_t = [S.tile(f"xT{b}") for b in range(4)]
    hT = sb("hT", [128, 8, NT], BF16)
    K.hT = hT
    K.hT_t = [S.tile(f"hT{b}") for b in range(4)]
    cst = sb("cst", [128, 128 + 16], F32)
    cst_t = S.tile("cst")
    K.ident = cst[:, 0:128]
    smp = sb("smp", [128, 2 * 48 + 40], F32)
    smp_t = S.tile("smp")
    ones_bf = sb("ones_bf", [128, 128], BF16)
    ones_t = S.tile("ones")
    neghalf = sb("neghalf", [128, 512], F32)
    nh_t = S.tile("nh")
    K.ones_bf, K.ones_t, K.neghalf, K.nh_t, K.cst_t = ones_bf, ones_t, neghalf, nh_t, cst_t

    DMA("sp", cst[:], consts, [], [cst_t], cst_t)
    DMA("sp", smp[:], smallp, [], [smp_t], smp_t)
    MSET("pool", ones_bf[:], 1.0, [ones_t])
    MSET("pool", neghalf[:], -0.5, [nh_t])
    epsc = sb("epsc", [128, 2], F32)
    epsc_t = S.tile("epsc")
    MSET("pool", epsc[:, 0:1], EPS, [epsc_t])
    MSET("pool", epsc[:, 1:2], GN_EPS, [epsc_t])
    K.epsc, K.epsc_t = epsc, epsc_t

    scT = sb("scT", [128, 16], BF16)
    scT_t = S.tile("scT")
    ACT(scT[:], cst[:, 128:144], AF.Silu, [cst_t], [scT_t])
    mT = sb("mT", [128, 2, 48, 2], F32)
    mT_t = S.tile("mT")
    wraw = [sb(f"wa{i}", [128, 4096], BF16) for i in range(3)]
    wa = [w[:, :].rearrange("p (k n) -> p k n", k=8) for w in wraw]
    wa_t = [S.tile(f"wa{i}") for i in range(3)]
    K.wraw, K.wslot_t = wraw, wa_t
    K.wplan = []
    K.wpos = [0]
    K.wissued = [0]

    def wview(sidx, k, n):
        return wraw[sidx][:, 0:k * n].rearrange("p (k n) -> p k n", k=k)

    def wissue(q, name, idx, k, n):
        sidx = q % 3
        DMA("pool", wview(sidx, k, n), K.dram[name][idx], [], [wa_t[sidx]], wa_t[sidx])

    def wnext(name, idx, k, n, live_prev=0):
        p = K.wpos[0]
        K.wpos[0] += 1
        if plan is None:
            K.wplan.append((name, idx, k, n))
            wissue(p, name, idx, k, n)
        else:
            assert plan[p] == (name, idx, k, n), (p, plan[p], name, idx, k, n)
            upto = min(len(plan) - 1, p + 2 - live_prev)
            while K.wissued[0] <= upto:
                q = K.wissued[0]
                wissue(q, *plan[q])
                K.wissued[0] += 1
        return wview(p % 3, k, n), wa_t[p % 3]

    K.wnext = wnext
    mT_tl = [S.tile(f"mT{l}") for l in range(2)]
    mod = sb("mod", [128, 2, 6, 8, 2], F32)
    mod_t = [S.tile(f"mod{l}") for l in range(2)]
    K.mod, K.mod_t = mod, mod_t
    K.gfin = smp[:, 96 + 32:96 + 40]
    K.smp_t = smp_t
    K.ada_pending = [(1, blk) for blk in range(12)]

    def ada_block(l, blk):
        wv, wv_t = wnext("ada_w", (l, slice(None), slice(None), slice(blk * 512, (blk + 1) * 512)), 8, 512)
        pd, off, ptile = K.bank_alloc()
        for oc in range(4):
            for k in range(8):
                MM(pd[:, off + oc * 2: off + oc * 2 + 2], wv[:, k, oc * 128:(oc + 1) * 128],
                   scT[:, k * 2:(k + 1) * 2], k == 0, k == 7, [wv_t, scT_t], [ptile])
        TTO("dve", mT[:, l, blk * 4:(blk + 1) * 4, :],
            pd[:, off:off + 8].rearrange("p (c g) -> p c g", g=2),
            smp[:, l * 48 + blk * 4: l * 48 + blk * 4 + 4].unsqueeze(2).broadcast_to([128, 4, 2]),
            ALU.add, [ptile, smp_t], [mT_tl[l]])

    def ada_finish(l):
        for (dst, jscale, jshift, jgate, gi) in [(0, 1, 0, 2, l), (3, 4, 3, 5, 2 + l)]:
            gn = smp[:, 96 + gi * 8: 96 + gi * 8 + 8].unsqueeze(2).broadcast_to([128, 8, 2])
            STT("dve", mod[:, l, dst, :, :], mT[:, l, jscale * 8:(jscale + 1) * 8, :], 1.0, gn, ALU.add, ALU.mult,
                [mT_tl[l], smp_t], [mod_t[l]])
            CP("dve", mod[:, l, dst + 1, :, :], mT[:, l, jshift * 8:(jshift + 1) * 8, :], [mT_tl[l]], [mod_t[l]])
            CP("dve", mod[:, l, dst + 2, :, :], mT[:, l, jgate * 8:(jgate + 1) * 8, :], [mT_tl[l]], [mod_t[l]])

    def ada_some(n):
        for _ in range(n):
            if K.ada_pending:
                ada_block(*K.ada_pending.pop(0))

    K.ada_some = ada_some

    K.nrr = [0]
    K.t1rr = [0]
    K.nb = None

    class NormBufs:
        def __init__(self, st):
            self.sq = [K.sbs(st, f"sq{i}", [128, 8, 512], BF16) for i in range(2)]
            self.sq_t = [[S.tile(f"sq{i}a"), S.tile(f"sq{i}b")] for i in range(2)]
            self.rs = [K.sbs(st, f"rs{i}", [128, 512], F32) for i in range(2)]
            self.rs_t = [S.tile(f"rs{i}") for i in range(2)]
            self.t1 = [K.sbs(st, f"t1_{i}", [128, 512], F32) for i in range(3)]
            self.t1_t = [S.tile(f"t1_{i}") for i in range(3)]

    def rstd_block(tb):
        i = K.nrr[0] % 2
        K.nrr[0] += 1
        nb = K.nb
        sq, sq_t, rs, rs_t = nb.sq, nb.sq_t, nb.rs, nb.rs_t
        sl = slice(tb * 512, (tb + 1) * 512)
        ACT(sq[i][:, 0:4, :], xT[:, 0:4, sl], AF.Square, [K.xT_t[tb]], [sq_t[i][0]])
        TTO("pool", sq[i][:, 4:8, :], xT[:, 4:8, sl], xT[:, 4:8, sl], ALU.mult, [K.xT_t[tb]], [sq_t[i][1]])
        pd, off, ptile = next_bank()
        for c in range(8):
            MM(pd[:, off:off + 512], ones_bf[:], sq[i][:, c, :], c == 0, c == 7, [ones_t, sq_t[i][c // 4]], [ptile])
        ACT(rs[i][:], pd[:, off:off + 512], AF.Ln, [ptile, K.epsc_t], [rs_t[i]], scale=1.0 / D, bias=K.epsc[:, 0:1])
        ACT(rs[i][:], rs[i][:], AF.Exp, [rs_t[i]], [rs_t[i]], scale=-0.5)
        return rs[i], rs_t[i]

    def norm_mod(l, which, tbs):
        with contextlib.ExitStack() as st:
            K.nb = NormBufs(st)
            norm_mod_body(l, which, tbs)
        S.fence()

    def norm_mod_body(l, which, tbs):
        t1, t1_t = K.nb.t1, K.nb.t1_t

        def s2(tb, rr_):
            r, r_t = rr_
            g = 0 if tb < 2 else 1
            sl = slice(tb * 512, (tb + 1) * 512)
            for c in range(8):
                j = K.t1rr[0] % 3
                K.t1rr[0] += 1
                if c < 4:
                    STT("dve", t1[j][:], xT[:, c, sl], mod[:, l, which, c, g:g + 1], r[:], ALU.mult, ALU.mult,
                        [K.xT_t[tb], mod_t[l], r_t], [t1_t[j]])
                    ACT(hT[:, c, sl], t1[j][:], AF.Identity, [t1_t[j], mod_t[l]], [K.hT_t[tb]],
                        bias=mod[:, l, which + 1, c, g:g + 1])
                else:
                    TTO("dve", t1[j][:], xT[:, c, sl], r[:], ALU.mult, [K.xT_t[tb], r_t], [t1_t[j]])
                    TS("dve", hT[:, c, sl], t1[j][:], mod[:, l, which, c, g:g + 1], mod[:, l, which + 1, c, g:g + 1],
                       ALU.mult, ALU.add, [t1_t[j], mod_t[l]], [K.hT_t[tb]])

        pipeline(list(tbs), rstd_block, s2, 1)

    K.norm_mod = norm_mod
    K.rstd_block = rstd_block

    if cfg["ffn"]:
        convp = din("convp", [128, 2, 4, 2 * NFC])
        cvp = sb("cvp", [128, 2, 4, 2 * NFC], F32)
        cvp_t = S.tile("cvp")
        DMA("sp", cvp[:], convp, [], [cvp_t], cvp_t)
        K.ffrr = [0, 0, 0]

    def ffn(l):
        with contextlib.ExitStack() as st:
            K.nb = NormBufs(st)
            norm_mod_body(l, 3, [0, 1, 2, 3])
            ffn_body(l, st)
        S.fence()

    def ffn_body(l, st):
        actT = K.sbs(st, "actT", [128, 6, NT], BF16)
        actT_t = [[S.tile(f"actT{c}_{b}") for b in range(3)] for c in range(6)]
        acc = [[K.sbs(st, f"acc{h}{i}", [128, 1024], F32) for i in range(2)] for h in range(2)]
        acc_t = [[S.tile(f"acc{h}{i}") for i in range(2)] for h in range(2)]
        sil = [K.sbs(st, f"sil{i}", [128, 1024], F32) for i in range(2)]
        sil_t = [S.tile(f"sil{i}") for i in range(2)]
        for (c0, c1) in FFN_PHASES:
            for c in range(c0, c1):
                wuv, wuv_t = wnext("w_up", (l, slice(None), slice(None), slice(c * 256, (c + 1) * 256)), 8, 256)
                for gi, (t0, T, g, nseq, slen) in enumerate(GROUPS):
                    accs = []
                    for h in range(2):
                        pdt, ptl = next_dbank()
                        for sb_ in range(T // 512):
                            for k in range(8):
                                MM(pdt[:, sb_ * 512:(sb_ + 1) * 512], wuv[:, k, h * 128:(h + 1) * 128],
                                   hT[:, k, t0 + sb_ * 512: t0 + (sb_ + 1) * 512], k == 0, k == 7,
                                   [wuv_t, K.hT_t[(t0 // 512) + sb_]], [ptl[sb_]])
                        pts = ptl[:T // 512]
                        i = K.ffrr[1] % 2
                        if h == 1:
                            K.ffrr[1] += 1
                        a, a_t = acc[h][i], acc_t[h][i]
                        ci = c * 2 + h
                        U = pdt[:, 0:T]
                        ACT(a[:, 0:T], U, AF.Identity, pts + [cvp_t], [a_t],
                            scale=cvp[:, l, 1, ci:ci + 1], bias=cvp[:, l, 3, ci:ci + 1])
                        U3 = U.rearrange("p (s t) -> p s t", s=nseq)
                        a3 = a[:, 0:T].rearrange("p (s t) -> p s t", s=nseq)
                        STT("dve", a3[:, :, 1:slen], U3[:, :, 0:slen - 1], cvp[:, l, 0, ci:ci + 1], a3[:, :, 1:slen],
                            ALU.mult, ALU.add, pts + [cvp_t, a_t], [a_t])
                        STT("dve", a3[:, :, 0:slen - 1], U3[:, :, 1:slen], cvp[:, l, 2, ci:ci + 1], a3[:, :, 0:slen - 1],
                            ALU.mult, ALU.add, pts + [cvp_t, a_t], [a_t])
                        accs.append((a, a_t))
                    i2 = K.ffrr[2] % 2
                    K.ffrr[2] += 1
                    ACT(sil[i2][:, 0:T], accs[0][0][:, 0:T], AF.Silu, [accs[0][1]], [sil_t[i2]])
                    TTO("pool", actT[:, c - c0, t0:t0 + T], sil[i2][:, 0:T], accs[1][0][:, 0:T], ALU.mult,
                        [sil_t[i2], accs[1][1]], [actT_t[c - c0][gi]])
            npc = c1 - c0
            for dh in range(2):
                wdv, wdv_t = wnext("w_dn", (l, slice(None), slice(c0, c1), slice(dh * 512, (dh + 1) * 512)), npc, 512)
                for dc in range(4):
                    dmc = dh * 4 + dc
                    for tb in range(4):
                        g = 0 if tb < 2 else 1
                        gi = tb if tb < 2 else 2
                        pd, off, ptile = next_bank()
                        for cc in range(npc):
                            MM(pd[:, off:off + 512], wdv[:, cc, dc * 128:(dc + 1) * 128],
                               actT[:, cc, tb * 512:(tb + 1) * 512], cc == 0, cc == npc - 1,
                               [wdv_t, actT_t[cc][gi]], [ptile])
                        STT("dve", xT[:, dmc, tb * 512:(tb + 1) * 512], pd[:, off:off + 512],
                            mod[:, l, 5, dmc, g:g + 1], xT[:, dmc, tb * 512:(tb + 1) * 512], ALU.mult, ALU.add,
                            [ptile, mod_t[l], K.xT_t[tb]], [K.xT_t[tb]])

    K.ffn = ffn

    xst = contextlib.ExitStack()
    xs = [K.sbs(xst, f"xs{i}", [128, D], F32) for i in range(2)]
    xs_t = [S.tile(f"xs{i}") for i in range(2)]
    K.nb = NormBufs(xst)
    for tt in range(16):
        s = tt % 2
        DMA("sp", xs[s][:], x_tok[tt * 128:(tt + 1) * 128, :], [], [xs_t[s]], xs_t[s])
        for half in range(2):
            pd, off, ptile = next_bank()
            for j in range(4):
                c = half * 4 + j
                TR(pd[:, off + j * 128: off + (j + 1) * 128], xs[s][:, c * 128:(c + 1) * 128], K.ident,
                   [xs_t[s], cst_t], [ptile])
            eng = "act" if (tt + half) % 2 else "dve"
            CP(eng, xT[:, half * 4:(half + 1) * 4, tt * 128:(tt + 1) * 128],
               pd[:, off:off + 512].rearrange("p (c t) -> p c t", c=4), [ptile], [K.xT_t[tt // 4]])
        if tt < 12:
            ada_block(0, tt)
    ada_finish(0)
    K.first_norm = False
    if cfg["even"]:
        norm_mod_body(0, 0, [0, 1, 2, 3])
        K.first_norm = True
    xst.close()
    S.fence()

    for l in range(2):
        if l == 1:
            ada_some(12)
            ada_finish(1)
        if l == 0 and cfg["even"]:
            even_mixer(K, l)
        if l == 1 and cfg["odd"]:
            odd_mixer(K, l)
        if cfg["ffn"]:
            ffn(l)

    S.fence()
    fst = contextlib.ExitStack()
    K.nb = NormBufs(fst)
    yT = [K.sbs(fst, f"yT{i}", [128, 8, 512], F32) for i in range(2)]
    yT_t = [S.tile(f"yT{i}") for i in range(2)]
    ys = [K.sbs(fst, f"ys{i}", [128, D], F32) for i in range(2)]
    ys_t = [S.tile(f"ys{i}") for i in range(2)]

    def fin_s2(tb, rr_):
        r, r_t = rr_
        yb = tb % 2
        sl = slice(tb * 512, (tb + 1) * 512)
        for c in range(8):
            STT("dve", yT[yb][:, c, :], xT[:, c, sl], K.gfin[:, c:c + 1], r[:], ALU.mult, ALU.mult,
                [K.xT_t[tb], smp_t, r_t], [yT_t[yb]])
        for q in range(4):
            tt = tb * 4 + q
            s_ = tt % 2
            for half in range(2):
                pd, off, ptile = next_bank()
                for j in range(4):
                    c = half * 4 + j
                    TR(pd[:, off + j * 128: off + (j + 1) * 128], yT[yb][:, c, q * 128:(q + 1) * 128], K.ident,
                       [yT_t[yb], cst_t], [ptile])
                eng = "act" if half else "dve"
                CP(eng, ys[s_][:, half * 512:(half + 1) * 512], pd[:, off:off + 512], [ptile], [ys_t[s_]])
            DMA("sp", y_out[tt * 128:(tt + 1) * 128, :], ys[s_][:], [ys_t[s_]], [], ys_t[s_])
            if ys_t[s_] not in K.outs:
                K.outs.append(ys_t[s_])

    pipeline([0, 1, 2, 3], rstd_block, fin_s2, 1)

    if plan is None:
        return K.wplan
    assert K.wpos[0] == len(plan)
    S.op("sp", lambda e: e.nop(), [], K.outs)
    S.emit()
    return nc


def even_mixer(K, l):
    nc, S = K.nc, K.S
    MM, TR, ACT, TTO, STT, TS, CP, MSET, DMA = K.MM, K.TR, K.ACT, K.TTO, K.STT, K.TS, K.CP, K.MSET, K.DMA
    RCP = K.RCP
    hT, hT_t, xT, xT_t = K.hT, K.hT_t, K.xT, K.xT_t
    NCOL = 4 * 1024 + 512 + 512 + 384
    evsmall_d = K.din("evsmall", [128, 24])
    evrow_d = K.din("evrow", [128, 84])
    evtab_d = K.din("evtab", [128, 3 * 1024 + 256])
    cgk = K.din("cache_gqa_k", [2, 256, 64])
    cgv = K.din("cache_gqa_v", [2, 256, 64])
    s0f_d = K.din("state_ret_fwd", [8, 64, 128])
    s0b_d = K.din("state_ret_bwd", [8, 64, 128])
    sf_out = K.dout("sf_out", [4, 8, 64, 128])
    sb_out = K.dout("sb_out", [4, 8, 64, 128])
    gk_out = K.dout("gk_out", [4, 2, 256, 64])
    gv_out = K.dout("gv_out", [4, 2, 256, 64])

    if not K.first_norm:
        K.norm_mod(l, 0, [0, 1, 2, 3])
    st = contextlib.ExitStack()
    sbs = lambda n, sh, dt=F32: K.sbs(st, n, sh, dt)
    NH = 1024
    evs = sbs("evs", [128, 24], F32)
    evs_t = S.tile("evs")
    evr = sbs("evr", [128, 84], F32)
    evr_t = S.tile("evr")
    lgp = sbs("lgp", [128, 4, 4], F32)
    lgp_t = S.tile("lgp")
    lgr = sbs("lgr", [128, 16], F32)
    lgr_t = S.tile("lgr")
    dk = sbs("dk", [128, 2, 2, 8], F32)
    dk_t = S.tile("dk")
    scs = sbs("scs", [128, 4, 2], F32)
    scs_t = S.tile("scs")
    tab = sbs("tab", [128, 1024 + 256], F32)
    tab_t = S.tile("tab")
    POS = tab[:, 0:1024]
    RELUD = tab[:, 1024:1152]
    bones = sbs("bones", [128, 128], BF16)
    bones_t = S.tile("bones")
    rr = dict(b=0)
    K.side = []

    def side_pop(n=1):
        for _ in range(n):
            if K.side:
                K.side.pop(0)[1]()

    def side_flush(tag=None):
        if tag is None:
            n = len(K.side)
        else:
            idx = [i for i, (t_, _) in enumerate(K.side) if t_ == tag]
            n = idx[-1] + 1 if idx else 0
        for _ in range(n):
            K.side.pop(0)[1]()

    K.side_pop, K.side_flush = side_pop, side_flush

    def rot(key, n):
        i = rr.get(key, 0) % n
        rr[key] = rr.get(key, 0) + 1
        return i

    K.ev_nb = 3

    def rbank():
        return K.bank(4 + rot("b", K.ev_nb))

    K.bank_alloc = rbank

    DMA("sp", evs[:], evsmall_d, [], [evs_t], evs_t)
    DMA("sp", evr[:], evrow_d, [], [evr_t], evr_t)
    DMA("sp", tab[:], evtab_d[:, 0:1280], [], [tab_t], tab_t)
    MSET("pool", bones[:], 0.0, [bones_t])
    MSET("pool", bones[0:64, 0:64], 1.0, [bones_t])
    MSET("pool", bones[64:128, 64:128], 1.0, [bones_t])
    ACT(lgp[:, :, 0:2], evs[:, 0:8].rearrange("p (a b) -> p a b", b=2), AF.Exp, [evs_t], [lgp_t], scale=-1.0)
    ACT(lgp[:, :, 2:4], lgp[:, :, 0:2], AF.Ln, [lgp_t], [lgp_t], bias=1.0)
    TS("dve", lgp[:, :, 0:2], lgp[:, :, 2:4], -1.0, None, ALU.mult, None, [lgp_t], [lgp_t])
    ACT(lgr[:], evr[:, 0:16], AF.Exp, [evr_t], [lgr_t], scale=-1.0)
    ACT(lgr[:], lgr[:], AF.Ln, [lgr_t], [lgr_t], bias=1.0)
    TS("dve", lgr[:], lgr[:], -1.0, None, ALU.mult, None, [lgr_t], [lgr_t])
    lgs = sbs("lgs", [128, 8], F32)
    lgs_t = S.tile("lgs")
    TTO("dve", lgs[:], lgr[:, 0:8], lgr[:, 8:16], ALU.add, [lgr_t], [lgs_t])
    cdg = sbs("cdg", [128, 2, 128], F32)
    cdg_t = S.tile("cdg")
    for d_ in range(2):
        for t in range(2):
            ACT(dk[:, d_, t, :], lgr[:, d_ * 8:(d_ + 1) * 8], AF.Exp, [lgr_t, evr_t], [dk_t],
                scale=evr[:, 80 + d_ * 2 + t: 80 + d_ * 2 + t + 1])
    ACT(scs[:, :, 0:1], lgp[:, :, 0:1], AF.Exp, [lgp_t], [scs_t], scale=513.0)
    ACT(scs[:, :, 1:2], lgp[:, :, 1:2], AF.Exp, [lgp_t], [scs_t], scale=512.0)

    def load_w(c0, ncols, live_prev=0):
        return K.wnext("even_w_in", (slice(None), slice(None), slice(c0, c0 + ncols)), 8, ncols, live_prev)

    def proj_fm(w, w_t, col, tb):
        pd, off, ptile = rbank()
        for k in range(8):
            MM(pd[:, off:off + 512], w[:, k, col:col + 128], hT[:, k, tb * 512:(tb + 1) * 512], k == 0, k == 7,
               [w_t, hT_t[tb]], [ptile])
        return pd, off, ptile

    for half in range(2):
        tb0 = 2 * half
        T0 = 1024 * half
        rope = half == 1
        K.ev_nb = 3 if half == 0 else 4
        nseq, slen = (4, 256) if half == 0 else (1, 1024)
        pos0 = 384 if half == 0 else 0
        hst = contextlib.ExitStack()
        hs = lambda n, sh, dt=F32: K.sbs(hst, n, sh, dt)
        concT = hs("concT", [128, 12, NH], BF16)
        concT_t = [S.tile(f"conc{c}") for c in range(12)]
        tp = [hs(f"tp{i}", [128, 512], F32) for i in range(3)]
        tp_t = [S.tile(f"tp{i}") for i in range(3)]
        rp, rp_t, gtm, gtm_t = tp, tp_t, tp, tp_t
        pT = [hs(f"pT{i}", [128, 512], BF16) for i in range(4)]
        pT_t = [S.tile(f"pT{i}") for i in range(4)]
        if rope:
            cs = hs("cs", [128, 2048], F32)
            cs_t = S.tile("cs")
            DMA("sp", cs[:], evtab_d[:, 1280:3328], [], [cs_t], cs_t)
            COS, SIN = cs[:, 0:1024], cs[:, 1024:2048]
        else:
            cs_t = tab_t
            COS = SIN = None
        rst = contextlib.ExitStack()
        rs_ = lambda n, sh, dt=F32: K.sbs(rst, n, sh, dt)
        G = [rs_(f"G{i}", [128, 512], F32) for i in range(2)]
        G_t = [S.tile(f"G{i}") for i in range(2)]
        qk4 = [rs_(f"qk4_{i}", [128, NH], BF16) for i in range(2)]
        qk4 += [rs_(f"qk4_{i}", [128, 2, NH], BF16) for i in range(2, 4)]
        qk4_t = [S.tile(f"qk4_{i}") for i in range(4)]
        for i_ in (2, 3):
            MSET("pool", qk4[i_][64:128, 0, :], 0.0, [qk4_t[i_]])
            MSET("pool", qk4[i_][0:64, 1, :], 0.0, [qk4_t[i_]])
        vtok = rs_("vtok", [128, 8, 256], BF16)
        vtok_t = [S.tile(f"vtok{t}") for t in range(8)]
        gate = rs_("gate", [128, 2, NH], BF16)
        gate_t = [S.tile(f"gate{i}") for i in range(2)]
        obf = [rs_("obf0", [128, 2, 512], BF16)] * 2
        obf_t = [S.tile("obf0")] * 2
        osb = [rs_("osb0", [128, 512], F32)] * 2
        osb_t = [S.tile("osb0")] * 2
        gnb = rs_("gnb", [128, 512], F32)
        gnb_t = S.tile("gnb")
        if half == 0:
            kd = [rs_(f"kd{i}", [128, 8, 128], BF16) for i in range(2)]
            kd_t = [S.tile(f"kd{i}") for i in range(2)]
            stg = rs_("stg", [128, 512], F32)
            stg_t = S.tile("stg")
        else:
            s0 = rs_("s0", [128, 2, 128], F32)
            s0_t = S.tile("s0")
            s0s = rs_("s0s", [128, 2, 2, 128], BF16)
            s0s_t = S.tile("s0s")
            MSET("pool", s0s[64:128, 0, :, :], 0.0, [s0s_t])
            MSET("pool", s0s[0:64, 1, :, :], 0.0, [s0s_t])
        for hp in range(4):
            K.ada_some(1)
            wA, wA_t = load_w(hp * 1024, 512)
            if True:
                for x in range(2):
                    ACT(cdg[:, x, :], RELUD, AF.Exp, [tab_t, lgs_t], [cdg_t], scale=lgs[:, 2 * hp + x:2 * hp + x + 1])
                    TTO("pool", cdg[:, x, :], cdg[:, x, :], K.ident, ALU.add, [cdg_t, K.cst_t], [cdg_t])
            gsel = [0, 3, 2, 1]
            for bi in range(2):
                tb = tb0 + bi
                sl = slice(bi * 512, (bi + 1) * 512)
                for which in range(2):
                    K.side_pop(1)
                    pd, off, ptile = proj_fm(wA, wA_t, which * 128, tb)
                    src, src_t = pd[:, off:off + 512], ptile
                    if rope:
                        pd2, off2, ptile2 = proj_fm(wA, wA_t, 256 + which * 128, tb)
                        i1, i2 = rot("tp", 3), rot("tp", 3)
                        TTO("dve", rp[i1][:], pd[:, off:off + 512], COS[:, sl], ALU.mult, [ptile, cs_t], [rp_t[i1]])
                        TTO("dve", rp[i2][:], pd2[:, off2:off2 + 512], SIN[:, sl], ALU.mult, [ptile2, cs_t], [rp_t[i2]])
                        TTO("pool", rp[i1][:], rp[i1][:], rp[i2][:], ALU.add, [rp_t[i1], rp_t[i2]], [rp_t[i1]])
                        src, src_t = rp[i1][:], rp_t[i1]
                    for dr in range(2):
                        oi = which * 2 + dr
                        gi = rot("G", 2)
                        ps = slice(pos0 + (bi * 512 if half == 1 else 0), pos0 + (bi * 512 if half == 1 else 0) + 512)
                        if half == 0:
                            for s2 in range(2):
                                ACT(G[gi][:, s2 * 256:(s2 + 1) * 256], POS[:, 384:640], AF.Exp, [tab_t, lgp_t], [G_t[gi]],
                                    scale=lgp[:, hp, gsel[oi]:gsel[oi] + 1])
                        else:
                            ACT(G[gi][:, 0:512], POS[:, ps], AF.Exp, [tab_t, lgp_t], [G_t[gi]],
                                scale=lgp[:, hp, gsel[oi]:gsel[oi] + 1])
                        if which == 0:
                            STT("dve", qk4[oi][:, sl], src, 1.0, G[gi][:, 0:512], ALU.mult, ALU.mult,
                                [src_t, G_t[gi]], [qk4_t[oi]])
                        else:
                            for x_ in range(2):
                                ph = slice(64 * x_, 64 * x_ + 64)
                                STT("dve", qk4[oi][ph, x_, sl], src[ph, :], 0.125, G[gi][ph, 0:512], ALU.mult, ALU.mult,
                                    [src_t, G_t[gi]], [qk4_t[oi]])
            if half == 0:
                for tt in range(8):
                    K.side_pop(1)
                    tb = tb0 + tt // 4
                    tsl = slice(T0 + tt * 128, T0 + (tt + 1) * 128)
                    pd, off, ptile = rbank()
                    for k in range(8):
                        MM(pd[:, off:off + 128], hT[:, k, tsl], wA[:, k, 128:256], k == 0, k == 7, [wA_t, hT_t[tb]], [ptile])
                    for d_ in range(2):
                        STT("dve", kd[d_][:, tt, :].rearrange("p (h d) -> p h d", h=2),
                            pd[:, off:off + 128].rearrange("p (h d) -> p h d", h=2), 0.125,
                            dk[:, d_, tt % 2, 2 * hp:2 * hp + 2].unsqueeze(2).broadcast_to([128, 2, 64]),
                            ALU.mult, ALU.mult, [ptile, dk_t], [kd_t[d_]])
            wB, wB_t = load_w(hp * 1024 + 512, 512)
            def gates(x):
                for bi in range(2):
                    tb = tb0 + bi
                    sl = slice(bi * 512, (bi + 1) * 512)
                    pd, off, ptile = proj_fm(wB, wB_t, x * 128, tb)
                    ACT(gate[:, x, sl], pd[:, off:off + 512], AF.Silu, [ptile], [gate_t[x]])

            K.side_flush(0)
            gates(0)
            for tt in range(8):
                K.side_pop(1)
                tb = tb0 + tt // 4
                tsl = slice(T0 + tt * 128, T0 + (tt + 1) * 128)
                pd, off, ptile = rbank()
                for k in range(8):
                    MM(pd[:, off:off + 256], hT[:, k, tsl], wB[:, k, 256:512], k == 0, k == 7, [wB_t, hT_t[tb]], [ptile])
                CP("act", vtok[:, tt, :], pd[:, off:off + 256], [ptile], [vtok_t[tt]])
            K.side_flush()
            gates(1)
            if half == 1:
                for d_, src_d in ((0, s0f_d), (1, s0b_d)):
                    DMA("sp", s0[:, d_, :], src_d[2 * hp:2 * hp + 2].rearrange("h k v -> (h k) v"), [], [s0_t], s0_t)
                    for x_ in range(2):
                        ph = slice(64 * x_, 64 * x_ + 64)
                        TS("dve", s0s[ph, x_, d_, :], s0[ph, d_, :], scs[ph, hp, d_:d_ + 1], None, ALU.mult, None,
                           [s0_t, scs_t], [s0s_t])
            for x in range(2):
                b = 64 * x
                h = 2 * hp + x
                opd = K.pd[x]
                optl = [K.pt[2 * x], K.pt[2 * x + 1]]
                ntl = slen // 128
                blk = min(slen, 512)
                jobs = [(sq_, ib, jt) for sq_ in range(nseq) for ib in range(slen // blk) for jt in range(ntl)]
                K.side_flush(x)

                def s1(jb, b=b, x=x):
                    sq_, ib, jt = jb
                    s_off = sq_ * slen
                    c_lo = s_off + ib * blk
                    jsl = slice(s_off + jt * 128, s_off + (jt + 1) * 128)
                    pd, off, ptile = rbank()
                    its = range(ib * (blk // 128), (ib + 1) * (blk // 128))
                    fw = [it for it in its if it >= jt]
                    bw = [it for it in its if it < jt]
                    j = rot("pT", 4)
                    if bw:
                        lo, hi = s_off + bw[0] * 128, s_off + (bw[-1] + 1) * 128
                        MM(pd[:, off + lo - c_lo: off + hi - c_lo], qk4[3][:, x, jsl], qk4[1][:, lo:hi],
                           True, True, [qk4_t[3], qk4_t[1]], [ptile])
                    if fw:
                        lo, hi = s_off + fw[0] * 128, s_off + (fw[-1] + 1) * 128
                        MM(pd[:, off + lo - c_lo: off + hi - c_lo], qk4[2][:, x, jsl], qk4[0][:, lo:hi],
                           True, True, [qk4_t[2], qk4_t[0]], [ptile])
                    dlo = None
                    if jt in its:
                        dlo = s_off + jt * 128 - c_lo
                        TTO("dve", pT[j][:, dlo:dlo + 128], pd[:, off + dlo:off + dlo + 128], cdg[:, x, :], ALU.mult,
                            [ptile, cdg_t], [pT_t[j]])
                    segs = [(0, blk)] if dlo is None else [(0, dlo), (dlo + 128, blk)]
                    ceng = "act" if (jt % 2 == 0) else "dve"
                    for (lo, hi) in segs:
                        if hi > lo:
                            CP(ceng, pT[j][:, lo:hi], pd[:, off + lo:off + hi], [ptile], [pT_t[j]])
                    return j

                def s2(jb, j, b=b, x=x, opd=opd, optl=optl):
                    sq_, ib, jt = jb
                    s_off = sq_ * slen
                    c_lo = s_off + ib * blk
                    ob = c_lo // 512
                    first = jt == 0
                    if half == 1 and jt == 0:
                        for d_ in range(2):
                            MM(opd[:, c_lo:c_lo + blk], s0s[:, x, d_, :], qk4[d_][:, c_lo:c_lo + blk],
                               d_ == 0, False, [s0s_t, qk4_t[d_]], [optl[ob]])
                        first = False
                    tt = sq_ * ntl + jt
                    MM(opd[:, c_lo:c_lo + blk], vtok[:, tt, x * 128:(x + 1) * 128], pT[j][:, 0:blk],
                       first, jt == ntl - 1, [vtok_t[tt], pT_t[j]], [optl[ob]])
                    if half == 0 and jt == ntl - 1:
                        for d_ in range(2):
                            spd, soff, sptile = K.bank(7)
                            for jt2 in range(2):
                                tt2 = sq_ * 2 + jt2
                                MM(spd[64 * d_:64 * d_ + 64, soff + sq_ * 128: soff + (sq_ + 1) * 128],
                                   kd[d_][:, tt2, x * 64:(x + 1) * 64], vtok[:, tt2, x * 128:(x + 1) * 128], jt2 == 0, jt2 == 1,
                                   [kd_t[d_], vtok_t[tt2]], [sptile])

                pipeline(jobs, s1, s2, 3, K.side)
                if half == 0:
                    spd, soff, sptile = K.bank(7)
                    CP("dve", stg[:], spd[:, soff:soff + 512], [sptile], [stg_t])
                    for d_, dst in ((0, sf_out), (1, sb_out)):
                        DMA("sp", dst[:, h, :, :].rearrange("s k v -> k s v"),
                            stg[64 * d_:64 * d_ + 64, :].rearrange("k (s v) -> k s v", s=4), [stg_t], [], stg_t)
                    if stg_t not in K.outs:
                        K.outs.append(stg_t)
                def gn_steps(x=x, h=h, opd=opd, optl=optl):
                    steps = []
                    for bi in range(2):
                        sl = slice(bi * 512, (bi + 1) * 512)

                        def st1(bi=bi, sl=sl):
                            CP("dve", osb[0][:], opd[:, sl], [optl[bi]], [osb_t[0]])
                            CP("dve", obf[0][:, 0, :], opd[:, sl], [optl[bi]], [obf_t[0]])
                            ACT(obf[0][:, 1, :], opd[:, sl], AF.Square, [optl[bi]], [obf_t[0]])

                        def st2():
                            pd1, off1, pt1 = rbank()
                            MM(pd1[:, off1:off1 + 512], K.ones_bf[:], obf[0][:, 0, :], True, True, [K.ones_t, obf_t[0]], [pt1])
                            pd2, off2, pt2 = rbank()
                            MM(pd2[:, off2:off2 + 512], K.ones_bf[:], obf[0][:, 1, :], True, True, [K.ones_t, obf_t[0]], [pt2])
                            ACT(gnb[:], pd1[:, off1:off1 + 512], AF.Square, [pt1], [gnb_t], scale=1.0 / 128.0)
                            STT("dve", gnb[:], pd2[:, off2:off2 + 512], 1.0 / 128.0, gnb[:], ALU.mult, ALU.subtract,
                                [pt2, gnb_t], [gnb_t])
                            STT("dve", osb[0][:], pd1[:, off1:off1 + 512], -1.0 / 128.0, osb[0][:], ALU.mult, ALU.add,
                                [pt1, osb_t[0]], [osb_t[0]])

                        def st3():
                            ACT(gnb[:], gnb[:], AF.Ln, [gnb_t, K.epsc_t], [gnb_t], bias=K.epsc[:, 1:2])
                            ACT(gnb[:], gnb[:], AF.Exp, [gnb_t], [gnb_t], scale=-0.5)

                        def st4(sl=sl):
                            TTO("pool", gnb[:], gnb[:], gate[:, x, sl], ALU.mult, [gnb_t, gate_t[x]], [gnb_t])
                            STT("dve", concT[:, h, sl], osb[0][:], evs[:, 8 + h:9 + h], gnb[:], ALU.mult, ALU.mult,
                                [osb_t[0], evs_t, gnb_t], [concT_t[h]])

                        steps += [st1, st2, st3, st4]
                    return steps

                K.side.extend([(x, f_) for f_ in gn_steps()])
        K.side_flush()
        rst.close()
        S.fence()
        gst = contextlib.ExitStack()
        gs_ = lambda n, sh, dt=F32: K.sbs(gst, n, sh, dt)
        qa = gs_("qa", [128, 4, NH], BF16)
        qa_t = [S.tile(f"qa{c}") for c in range(4)]
        ka = gs_("ka", [128, 2, NH], BF16)
        ka_t = S.tile("ka")
        MSET("pool", ka[64:128, 0, :], 0.0, [ka_t])
        MSET("pool", ka[0:64, 1, :], 0.0, [ka_t])
        vaug = gs_("vaug", [128, 8, 256], BF16)
        vaug_t = [S.tile(f"vaug{t}") for t in range(8)]
        MSET("pool", vaug[:, :, 64:192], 1.0, vaug_t)
        rden = [gs_(f"rden{i}", [128, 512], F32) for i in range(2)]
        rden_t = [S.tile(f"rden{i}") for i in range(2)]
        if half == 0:
            kvs = [gs_(f"kvs{i}", [128, 256], F32) for i in range(2)]
            kvs_t = [S.tile(f"kvs{i}") for i in range(2)]
            sm = [gs_(f"sm{i}", [128, 4], F32) for i in range(2)]
            sm_t = [S.tile(f"sm{i}") for i in range(2)]
        else:
            vaugc = gs_("vaugc", [128, 2, 256], BF16)
            vaugc_t = S.tile("vaugc")
            MSET("pool", vaugc[:, :, 64:192], 1.0, [vaugc_t])
            kcr = gs_("kcr", [128, 2, 2, 64], F32)
            kcr_t = S.tile("kcr")
            kcT = gs_("kcT", [128, 2, 256], BF16)
            kcT_t = S.tile("kcT")
            MSET("pool", kcT[64:128, 0, :], 0.0, [kcT_t])
            MSET("pool", kcT[0:64, 1, :], 0.0, [kcT_t])
        K.ada_some(2)
        wC, wC_t = load_w(4096, 512)
        wD, wD_t = (load_w(4096 + 512, 512, 1) if rope else (None, None))

        def qk_fm(w, w_t, col, wsw, wsw_t, colsw, gcol, dst, dst_t, tb, sl, padded=False):
            def fin(fn):
                if not padded:
                    fn(dst[:, sl], slice(0, 128))
                else:
                    for x_ in range(2):
                        fn(dst[64 * x_:64 * x_ + 64, x_, sl], slice(64 * x_, 64 * x_ + 64))

            pd, off, ptile = proj_fm(w, w_t, col, tb)
            j = rot("pT", 4)
            ACT(pT[j][:], pd[:, off:off + 512], AF.Square, [ptile], [pT_t[j]])
            pds, offs, pts = rbank()
            MM(pds[:, offs:offs + 512], bones[:], pT[j][:], True, True, [bones_t, pT_t[j]], [pts])
            i0 = rot("tp", 3)
            ACT(rp[i0][:], pds[:, offs:offs + 512], AF.Ln, [pts, K.epsc_t], [rp_t[i0]], scale=1.0 / 64.0, bias=K.epsc[:, 0:1])
            ACT(rp[i0][:], rp[i0][:], AF.Exp, [rp_t[i0]], [rp_t[i0]], scale=-0.5)
            if not rope:
                fin(lambda o_, ph: STT("dve", o_, pd[ph, off:off + 512], evs[ph, gcol:gcol + 1], rp[i0][ph, :], ALU.mult, ALU.mult,
                                       [ptile, evs_t, rp_t[i0]], [dst_t]))
            else:
                pd2, off2, ptile2 = proj_fm(wsw, wsw_t, colsw, tb)
                i1, i2 = rot("tp", 3), rot("tp", 3)
                STT("dve", gtm[i1][:], pd[:, off:off + 512], evs[:, gcol:gcol + 1], COS[:, sl], ALU.mult, ALU.mult,
                    [ptile, evs_t, cs_t], [gtm_t[i1]])
                STT("dve", gtm[i2][:], pd2[:, off2:off2 + 512], evs[:, gcol + 1:gcol + 2], SIN[:, sl], ALU.mult, ALU.mult,
                    [ptile2, evs_t, cs_t], [gtm_t[i2]])
                TTO("pool", gtm[i1][:], gtm[i1][:], gtm[i2][:], ALU.add, [gtm_t[i1], gtm_t[i2]], [gtm_t[i1]])
                fin(lambda o_, ph: TTO("pool", o_, gtm[i1][ph, :], rp[i0][ph, :], ALU.mult, [gtm_t[i1], rp_t[i0]], [dst_t]))

        for bi in range(2):
            tb = tb0 + bi
            sl = slice(bi * 512, (bi + 1) * 512)
            for c in range(4):
                qk_fm(wC, wC_t, c * 128, wD, wD_t, c * 128, 16, qa[:, c, :], qa_t[c], tb, sl)
        wE, wE_t = load_w(4096 + 1024, 384)
        for bi in range(2):
            tb = tb0 + bi
            sl = slice(bi * 512, (bi + 1) * 512)
            qk_fm(wE, wE_t, 128, wE, wE_t, 0, 18, ka, ka_t, tb, sl, padded=True)
        for tt in range(8):
            tb = tb0 + tt // 4
            tsl = slice(T0 + tt * 128, T0 + (tt + 1) * 128)
            pd, off, ptile = rbank()
            for k in range(8):
                MM(pd[:, off:off + 256], hT[:, k, tsl], wE[:, k, 128:384], k == 0, k == 7, [wE_t, hT_t[tb]], [ptile])
            CP("act", vaug[:, tt, 0:64], pd[:, off + 128:off + 192], [ptile], [vaug_t[tt]])
            CP("act", vaug[:, tt, 192:256], pd[:, off + 192:off + 256], [ptile], [vaug_t[tt]])
            if half == 0:
                j = rot("kvs", 2)
                i1 = rot("tp", 3)
                si = rot("sm", 2)
                ACT(rp[i1][:, 0:128], pd[:, off:off + 128], AF.Square, [ptile], [rp_t[i1]])
                S.op("dve", lambda e, o=sm[si][:, 0:2], i_=rp[i1][:, 0:128].rearrange("p (h d) -> p h d", h=2):
                     e.tensor_reduce(out=o, in_=i_, axis=AX.X, op=ALU.add), [rp_t[i1]], [sm_t[si]])
                ACT(sm[si][:, 0:2], sm[si][:, 0:2], AF.Ln, [sm_t[si], K.epsc_t], [sm_t[si]], scale=1.0 / 64.0, bias=K.epsc[:, 0:1])
                ACT(sm[si][:, 0:2], sm[si][:, 0:2], AF.Exp, [sm_t[si]], [sm_t[si]], scale=-0.5)
                for kv in range(2):
                    STT("dve", kvs[j][:, kv * 64:(kv + 1) * 64], pd[:, off + kv * 64: off + (kv + 1) * 64],
                        sm[si][:, kv:kv + 1], evr[:, 16:80], ALU.mult, ALU.mult, [ptile, sm_t[si], evr_t], [kvs_t[j]])
                CP("dve", kvs[j][:, 128:256], pd[:, off + 128:off + 256], [ptile], [kvs_t[j]])
                sq_, t0 = tt // 2, (tt % 2) * 128
                DMA("sp", gk_out[sq_, :, t0:t0 + 128, :].rearrange("h t d -> t h d"),
                    kvs[j][:, 0:128].rearrange("p (h d) -> p h d", h=2), [kvs_t[j]], [], kvs_t[j])
                DMA("sp", gv_out[sq_, :, t0:t0 + 128, :].rearrange("h t d -> t h d"),
                    kvs[j][:, 128:256].rearrange("p (h d) -> p h d", h=2), [kvs_t[j]], [], kvs_t[j])
                if kvs_t[j] not in K.outs:
                    K.outs.append(kvs_t[j])
        if half == 1:
            for x in range(2):
                DMA("sp", kcr[:, :, x, :], cgk[x].rearrange("(t p) d -> p t d", p=128), [], [kcr_t], kcr_t)
                DMA("pool", bass.AP(tensor=vaugc, offset=192 * x, ap=[[512, 128], [256, 2], [1, 64]]),
                    cgv[x].rearrange("(t p) d -> p t d", p=128), [], [vaugc_t], vaugc_t)
            pd, off, ptile = rbank()
            for t in range(2):
                TR(pd[:, off + t * 128: off + (t + 1) * 128], kcr[:, t, :, :].rearrange("p h d -> p (h d)"), K.ident,
                   [kcr_t, K.cst_t], [ptile])
            CP("dve", kcT[0:64, 0, :], pd[0:64, off:off + 256], [ptile], [kcT_t])
            CP("dve", kcT[64:128, 1, :], pd[64:128, off:off + 256], [ptile], [kcT_t])
        blk = min(slen, 512)
        jobs = []
        for c in range(4):
            for x in range(2):
                grp = []
                for sq_ in range(nseq):
                    for ib in range(slen // blk):
                        keys = [("s", sq_ * (slen // 128) + jt) for jt in range(slen // 128)]
                        if half == 1:
                            keys += [("c", 0), ("c", 1)]
                        for ki, (kind, tt) in enumerate(keys):
                            grp.append([c, x, sq_ * slen + ib * blk, kind, tt, ki == 0, ki == len(keys) - 1, False])
                grp[-1][-1] = True
                jobs += [tuple(g) for g in grp]

        def s1(jb):
            c, x, c_lo, kind, tt, isfirst, islast, isend = jb
            b = 64 * x
            pd, off, ptile = rbank()
            if kind == "s":
                MM(pd[:, off:off + blk], ka[:, x, tt * 128:(tt + 1) * 128], qa[:, c, c_lo:c_lo + blk],
                   True, True, [ka_t, qa_t[c]], [ptile])
            else:
                MM(pd[:, off:off + blk], kcT[:, x, tt * 128:(tt + 1) * 128], qa[:, c, c_lo:c_lo + blk],
                   True, True, [kcT_t, qa_t[c]], [ptile])
            j = rot("pT", 4)
            ACT(pT[j][:, 0:blk], pd[:, off:off + blk], AF.Exp, [ptile], [pT_t[j]], scale=0.125)
            return j

        def s2(jb, j):
            c, x, c_lo, kind, tt, isfirst, islast, isend = jb
            opd = K.pd[x]
            optl = [K.pt[2 * x], K.pt[2 * x + 1]]
            ob = c_lo // 512
            if kind == "s":
                va, va_t = vaug[:, tt, x * 128:(x + 1) * 128], vaug_t[tt]
            else:
                va, va_t = vaugc[:, tt, x * 128:(x + 1) * 128], vaugc_t
            MM(opd[:, c_lo:c_lo + blk], va, pT[j][:, 0:blk], isfirst, islast, [va_t, pT_t[j]], [optl[ob]])
            if isend:
                jr = rot("rden", 2)
                dn = slice(64, 128) if x == 0 else slice(0, 64)
                obp = slice(0, 64) if x == 0 else slice(64, 128)
                for bi in range(2):
                    sl = slice(bi * 512, (bi + 1) * 512)
                    RCP(rden[jr][obp, :], opd[dn, sl], [optl[bi]], rden_t[jr])
                    TTO("dve", concT[obp, 8 + c, sl], opd[obp, sl], rden[jr][obp, :], ALU.mult, [optl[bi], rden_t[jr]],
                        [concT_t[8 + c]])

        pipeline(jobs, s1, s2, 3)
        gst.close()
        for dq in range(4):
            w, w_t = K.wnext("even_w_out", (slice(None), slice(None), slice(dq * 256, (dq + 1) * 256)), 12, 256)
            for dc in range(2):
                dmc = dq * 2 + dc
                for bi in range(2):
                    tb = tb0 + bi
                    pd, off, ptile = rbank()
                    for c in range(12):
                        MM(pd[:, off:off + 512], w[:, c, dc * 128:(dc + 1) * 128], concT[:, c, bi * 512:(bi + 1) * 512],
                           c == 0, c == 11, [w_t, concT_t[c]], [ptile])
                    STT("dve", xT[:, dmc, tb * 512:(tb + 1) * 512], pd[:, off:off + 512], K.mod[:, l, 2, dmc, half:half + 1],
                        xT[:, dmc, tb * 512:(tb + 1) * 512], ALU.mult, ALU.add, [ptile, K.mod_t[l], xT_t[tb]], [xT_t[tb]])
        hst.close()
        S.fence()
    K.bank_alloc = K.next_bank
    st.close()
    S.fence()


def pipeline(jobs, stage1, stage2, depth=2, side=None):
    pend = []
    for jb in jobs:
        pend.append((jb, stage1(jb)))
        if len(pend) > depth:
            j0, h0 = pend.pop(0)
            stage2(j0, h0)
            if side:
                side.pop(0)[1]()
    for j0, h0 in pend:
        stage2(j0, h0)
        if side:
            side.pop(0)[1]()


def na_query_rows(kr):
    rows = [r for r in range(16) if min(max(r - 4, 0), 8) <= kr <= min(max(r - 4, 0), 8) + 7]
    return rows[0], rows[-1]


def odd_mixer(K, l):
    nc, S = K.nc, K.S
    MM, TR, ACT, TTO, STT, TS, CP, MSET, DMA = K.MM, K.TR, K.ACT, K.TTO, K.STT, K.TS, K.CP, K.MSET, K.DMA
    RCP = K.RCP
    hT, hT_t, xT, xT_t = K.hT, K.hT_t, K.xT, K.xT_t
    rpbx = K.din("rpbx", [16, 128, 15, 64])
    cmask_d = K.din("colmask", [128, 64])
    ck = K.din("cache_na_k", [16, 256, 64])
    cv = K.din("cache_na_v", [16, 256, 64])
    nk_out = K.dout("nk_out", [4, 16, 256, 64])
    nv_out = K.dout("nv_out", [4, 16, 256, 64])

    K.norm_mod(l, 0, [0, 1, 2, 3])
    st = contextlib.ExitStack()
    sbs = lambda n, sh, dt=F32: K.sbs(st, n, sh, dt)
    concT = sbs("concT", [128, 8, NT], BF16)
    concT_t = [[S.tile(f"conc{c}_{b}") for b in range(5)] for c in range(8)]
    qT = sbs("qT", [128, NT], BF16)
    kT = sbs("kT", [128, 2, NT], BF16)
    qT_t = [S.tile(f"qT{b}") for b in range(4)]
    kT_t = [S.tile(f"kT{b}") for b in range(4)]
    vaug = sbs("vaug", [128, 16, 256], BF16)
    vaug_t = [S.tile(f"vaug{t}") for t in range(16)]
    vaugc = sbs("vaugc", [128, 2, 256], BF16)
    vaugc_t = S.tile("vaugc")
    kcr = sbs("kcr", [128, 2, 2, 64], F32)
    kcr_t = S.tile("kcr")
    kcT = sbs("kcT", [128, 2, 256], BF16)
    kcT_t = S.tile("kcT")
    cmask = sbs("cmask", [128, 64], F32)
    cmask_t = S.tile("cmask")
    bt = [sbs(f"bt{i}", [128, 16, 64], F32) for i in range(2)]
    bt_t = [S.tile(f"bt{i}") for i in range(2)]
    pT = [sbs(f"pT{i}", [128, 512], BF16) for i in range(4)]
    pT_t = [S.tile(f"pT{i}") for i in range(4)]
    tmp = [sbs(f"tmp{i}", [128, 512], F32) for i in range(2)]
    tmp_t = [S.tile(f"tmp{i}") for i in range(2)]
    rden = [sbs("rden0", [128, 512], F32)] * 2
    rden_t = [S.tile("rden0")] * 2
    kvs = [sbs(f"kvs{i}", [128, 256], F32) for i in range(2)]
    kvs_t = [S.tile(f"kvs{i}") for i in range(2)]
    rr = dict(p=0, t=0, r=0, k=0, b=4)

    def rot(key, n):
        i = rr.get(key, 0) % n
        rr[key] = rr.get(key, 0) + 1
        return i

    def rbank():
        i = 4 + rot("b", 4)
        return K.bank(i)

    DMA("sp", cmask[:], cmask_d, [], [cmask_t], cmask_t)
    MSET("pool", kT[64:128, 0, :], 0.0, kT_t)
    MSET("pool", kT[0:64, 1, :], 0.0, kT_t)
    MSET("pool", kcT[64:128, 0, :], 0.0, [kcT_t])
    MSET("pool", kcT[0:64, 1, :], 0.0, [kcT_t])
    MSET("pool", vaug[:, :, 64:192], 1.0, vaug_t)
    MSET("pool", vaugc[:, :, 64:192], 1.0, [vaugc_t])

    for hp in range(8):
        w, w_t = K.wnext("odd_w_in", (slice(None), slice(None), slice(hp * 384, (hp + 1) * 384)), 8, 384)
        for x in range(2):
            DMA("sp", bt[x][0:64, 0:15, :], rpbx[2 * hp + x, 0:64], [], [bt_t[x]], bt_t[x])
            DMA("sp", bt[x][64:128, 1:16, :], rpbx[2 * hp + x, 64:128], [], [bt_t[x]], bt_t[x])
            TTO("pool", bt[x][0:64, 0:15, :], bt[x][0:64, 0:15, :], cmask[0:64, :].unsqueeze(1).broadcast_to([64, 15, 64]),
                ALU.add, [bt_t[x], cmask_t], [bt_t[x]])
            TTO("pool", bt[x][64:128, 1:16, :], bt[x][64:128, 1:16, :],
                cmask[64:128, :].unsqueeze(1).broadcast_to([64, 15, 64]), ALU.add, [bt_t[x], cmask_t], [bt_t[x]])
        for x in range(2):
            DMA("sp", kcr[:, :, x, :], ck[2 * hp + x].rearrange("(t p) d -> p t d", p=128), [], [kcr_t], kcr_t)
            DMA("pool", bass.AP(tensor=vaugc, offset=192 * x, ap=[[512, 128], [256, 2], [1, 64]]),
                cv[2 * hp + x].rearrange("(t p) d -> p t d", p=128), [], [vaugc_t], vaugc_t)
        if K.cfg.get("ostop", 9) <= 1:
            continue
        for tb in range(4):
            sl = slice(tb * 512, (tb + 1) * 512)
            for which, dst, dst_t in ((0, qT, qT_t), (1, kT, kT_t)):
                pd, off, ptile = rbank()
                for k in range(8):
                    MM(pd[:, off:off + 512], w[:, k, which * 128:(which + 1) * 128], hT[:, k, sl], k == 0, k == 7,
                       [w_t, hT_t[tb]], [ptile])
                if which == 0:
                    ACT(dst[:, sl], pd[:, off:off + 512], AF.Copy, [ptile], [dst_t[tb]], scale=0.125)
                else:
                    CP("dve", dst[0:64, 0, sl], pd[0:64, off:off + 512], [ptile], [dst_t[tb]])
                    CP("dve", dst[64:128, 1, sl], pd[64:128, off:off + 512], [ptile], [dst_t[tb]])
        if K.cfg.get("ostop", 9) <= 2:
            continue
        for tt in range(16):
            tb = tt // 4
            pd, off, ptile = rbank()
            c0 = 128 if tt < 8 else 256
            ncol = 256 if tt < 8 else 128
            for k in range(8):
                MM(pd[:, off:off + ncol], hT[:, k, tt * 128:(tt + 1) * 128], w[:, k, c0:384], k == 0, k == 7,
                   [w_t, hT_t[tb]], [ptile])
            vo = off + ncol - 128
            CP("act", vaug[:, tt, 0:64], pd[:, vo:vo + 64], [ptile], [vaug_t[tt]])
            CP("act", vaug[:, tt, 192:256], pd[:, vo + 64:vo + 128], [ptile], [vaug_t[tt]])
            nd = K.cfg.get("nodma", 0)
            if tt < 8 and nd != 1:
                j = rot("k", 2)
                CP(K.cfg.get("kveng", "dve") if isinstance(K.cfg.get("kveng", "dve"), str) else ("act" if K.cfg["kveng"] else "dve"), kvs[j][:], pd[:, off:off + 256], [ptile], [kvs_t[j]])
                if nd == 2:
                    continue
                sq_, t0 = tt // 2, (tt % 2) * 128
                DMA("sp", nk_out[sq_, 2 * hp:2 * hp + 2, t0:t0 + 128, :].rearrange("h t d -> t h d"),
                    kvs[j][:, 0:128].rearrange("p (h d) -> p h d", h=2), [kvs_t[j]], [], kvs_t[j])
                DMA("sp", nv_out[sq_, 2 * hp:2 * hp + 2, t0:t0 + 128, :].rearrange("h t d -> t h d"),
                    kvs[j][:, 128:256].rearrange("p (h d) -> p h d", h=2), [kvs_t[j]], [], kvs_t[j])
                if kvs_t[j] not in K.outs:
                    K.outs.append(kvs_t[j])
        if K.cfg.get("ostop", 9) <= 3:
            continue
        pd, off, ptile = rbank()
        for t in range(2):
            TR(pd[:, off + t * 128: off + (t + 1) * 128], kcr[:, t, :, :].rearrange("p h d -> p (h d)"), K.ident,
               [kcr_t, K.cst_t], [ptile])
        CP("dve", kcT[0:64, 0, :], pd[0:64, off:off + 256], [ptile], [kcT_t])
        CP("dve", kcT[64:128, 1, :], pd[64:128, off:off + 256], [ptile], [kcT_t])
        jobs = []
        for sq_ in range(4):
            for x in range(2):
                jobs.append(("p", sq_, x))
        for x in range(2):
            nj = []
            for kt in range(2):
                for qb in range(2):
                    nj.append(["c", x, kt, qb, qb * 512, (qb + 1) * 512])
            for m in range(8):
                a0, b0 = na_query_rows(2 * m)
                a1, b1 = na_query_rows(2 * m + 1)
                a, bq = min(a0, a1), max(b0, b1)
                lo, hi = a * 64, (bq + 1) * 64
                if lo < 512:
                    nj.append(["w", x, m, 0, lo, min(hi, 512)])
                if hi > 512:
                    nj.append(["w", x, m, 1, max(lo, 512), hi])
            last, first = {}, {}
            for ji, jb in enumerate(nj):
                last[jb[3]] = ji
                first.setdefault(jb[3], ji)
            for ji, jb in enumerate(nj):
                jobs.append(tuple(jb) + (ji == first[jb[3]], ji == last[jb[3]], ji == len(nj) - 1))

        def s1(jb):
            if jb[0] == "p":
                _, sq_, x = jb
                b = 64 * x
                t0 = sq_ * 256
                tb = sq_ // 2
                pd, off, ptile = rbank()
                for kt in range(2):
                    MM(pd[:, off + kt * 256: off + (kt + 1) * 256], kT[:, x, t0 + kt * 128: t0 + (kt + 1) * 128],
                       qT[:, t0:t0 + 256], True, True, [kT_t[tb], qT_t[tb]], [ptile])
                j = rot("p", 4)
                ACT(pT[j][:], pd[:, off:off + 512], AF.Exp, [ptile], [pT_t[j]])
                return j
            kind, x, ka, qb, lo, hi = jb[:6]
            b = 64 * x
            n = hi - lo
            pd, off, ptile = rbank()
            j = rot("p", 4)
            if kind == "c":
                MM(pd[:, off:off + 512], kcT[:, x, ka * 128:(ka + 1) * 128], qT[:, 1024 + lo:1024 + hi],
                   True, True, [kcT_t, qT_t[2 + qb]], [ptile])
                ACT(pT[j][:], pd[:, off:off + 512], AF.Exp, [ptile], [pT_t[j]])
            else:
                m = ka
                s0 = 7 - 2 * m + lo // 64
                MM(pd[:, off:off + n], kT[:, x, 1024 + m * 128:1024 + (m + 1) * 128],
                   qT[:, 1024 + lo:1024 + hi], True, True, [kT_t[2 + m // 4], qT_t[2 + qb]], [ptile])
                i2 = rot("t", 2)
                TTO("dve", tmp[i2][:, 0:n], pd[:, off:off + n],
                    bt[x][:, s0:s0 + n // 64, :].rearrange("p s c -> p (s c)"), ALU.add,
                    [ptile, bt_t[x]], [tmp_t[i2]])
                for par in range(2):
                    a_, b_ = na_query_rows(2 * m + par)
                    for r in range(lo // 64, hi // 64):
                        if r < a_ or r > b_:
                            c_ = (r - lo // 64) * 64
                            MSET("dve", tmp[i2][64 * par:64 * par + 64, c_:c_ + 64], -30000.0, [tmp_t[i2]])
                ACT(pT[j][:, 0:n], tmp[i2][:, 0:n], AF.Exp, [tmp_t[i2]], [pT_t[j]])
            return j

        def s2(jb, j):
            if jb[0] == "p":
                _, sq_, x = jb
                t0 = sq_ * 256
                opd, ooff, optile = K.bank(sq_ % 4)
                for kt in range(2):
                    tt = sq_ * 2 + kt
                    MM(opd[:, ooff + x * 256: ooff + (x + 1) * 256], vaug[:, tt, x * 128:(x + 1) * 128],
                       pT[j][:, kt * 256:(kt + 1) * 256], kt == 0, kt == 1, [vaug_t[tt], pT_t[j]], [optile])
                if x == 1:
                    jr = rot("r", 2)
                    RCP(rden[jr][0:64, 0:256], opd[64:128, ooff:ooff + 256], [optile], rden_t[jr])
                    RCP(rden[jr][64:128, 0:256], opd[0:64, ooff + 256:ooff + 512], [optile], rden_t[jr])
                    TTO("dve", concT[0:64, hp, t0:t0 + 256], opd[0:64, ooff:ooff + 256], rden[jr][0:64, 0:256], ALU.mult,
                        [optile, rden_t[jr]], [concT_t[hp][sq_]])
                    TTO("dve", concT[64:128, hp, t0:t0 + 256], opd[64:128, ooff + 256:ooff + 512], rden[jr][64:128, 0:256],
                        ALU.mult, [optile, rden_t[jr]], [concT_t[hp][sq_]])
                return
            kind, x, ka, qb, lo, hi, isfirst, islast, isend = jb
            n = hi - lo
            opd = K.pd[x]
            optl = [K.pt[2 * x], K.pt[2 * x + 1]]
            if kind == "c":
                MM(opd[:, lo:hi], vaugc[:, ka, x * 128:(x + 1) * 128], pT[j][:], isfirst, islast,
                   [vaugc_t, pT_t[j]], [optl[qb]])
            else:
                tt = 8 + ka
                MM(opd[:, lo:hi], vaug[:, tt, x * 128:(x + 1) * 128], pT[j][:, 0:n],
                   isfirst, islast, [vaug_t[tt], pT_t[j]], [optl[qb]])
            if isend:
                jr = rot("r", 2)
                for qb2 in range(2):
                    sl = slice(qb2 * 512, (qb2 + 1) * 512)
                    dn = slice(64, 128) if x == 0 else slice(0, 64)
                    ob = slice(0, 64) if x == 0 else slice(64, 128)
                    RCP(rden[jr][ob, :], opd[dn, sl], [optl[qb2]], rden_t[jr])
                    TTO("dve", concT[ob, hp, 1024 + qb2 * 512:1024 + (qb2 + 1) * 512], opd[ob, sl], rden[jr][ob, :],
                        ALU.mult, [optl[qb2], rden_t[jr]], [concT_t[hp][4]])

        pipeline(jobs, s1, s2, 3)
    for dh in range(2 if K.cfg.get("ostop", 9) > 5 else 0):
        w, w_t = K.wnext("odd_w_out", (slice(None), slice(None), slice(dh * 512, (dh + 1) * 512)), 8, 512)
        for dc in range(4):
            dmc = dh * 4 + dc
            for tb in range(4):
                g = 0 if tb < 2 else 1
                pd, off, ptile = rbank()
                for c in range(8):
                    rd = [concT_t[c][2 * tb], concT_t[c][2 * tb + 1]] if tb < 2 else [concT_t[c][4]]
                    MM(pd[:, off:off + 512], w[:, c, dc * 128:(dc + 1) * 128], concT[:, c, tb * 512:(tb + 1) * 512],
                       c == 0, c == 7, [w_t] + rd, [ptile])
                STT("dve", xT[:, dmc, tb * 512:(tb + 1) * 512], pd[:, off:off + 512], K.mod[:, l, 2, dmc, g:g + 1],
                    xT[:, dmc, tb * 512:(tb + 1) * 512], ALU.mult, ALU.add, [ptile, K.mod_t[l], xT_t[tb]], [xT_t[tb]])
    st.close()
    S.fence()


def fm(v):
    v = np.asarray(v, np.float32)
    r = v.reshape(*v.shape[:-1], v.shape[-1] // 128, 128)
    r = np.moveaxis(r, -1, 0)
    return np.ascontiguousarray(r)


def wl(W):
    Kd, N = W.shape
    return np.ascontiguousarray(W.reshape(Kd // 128, 128, N).transpose(1, 0, 2))


_PROG = {}


def prep_shared(inp, cfg):
    sh = {}
    sh["ada_w"] = np.stack([wl(inp["ada_w"][l]) for l in range(2)])
    ada_b = np.stack([inp["ada_b"][l].reshape(48, 128).T for l in range(2)], 1).reshape(128, 96)
    gains = np.stack([inp["norm_mix"][0], inp["norm_mix"][1], inp["norm_ffn"][0], inp["norm_ffn"][1],
                      inp["norm_final"]])
    sh["smallp"] = np.ascontiguousarray(np.concatenate([ada_b, fm(gains).reshape(128, 40)], 1), np.float32)
    if cfg["ffn"]:
        perm = np.concatenate([np.concatenate([np.arange(c * 128, (c + 1) * 128),
                                               DFF + np.arange(c * 128, (c + 1) * 128)]) for c in range(NFC)])
        sh["w_up"] = np.stack([wl(inp["ffn_w_up"][l][:, perm]) for l in range(2)])
        sh["w_dn"] = np.stack([wl(inp["ffn_w_down"][l]) for l in range(2)])
        cp = np.zeros((128, 2, 4, 2 * NFC), np.float32)
        for l in range(2):
            for j in range(4):
                v = inp["ffn_conv_w"][l][j] if j < 3 else inp["ffn_conv_b"][l]
                vp = v[perm].reshape(2 * NFC, 128).T
                cp[:, l, j, :] = vp
        sh["convp"] = cp
    if cfg["even"]:
        Wi = np.asarray(inp["even_w_in"][0], np.float32)
        QR, KR, VR, GR, QA, KA, VA = 0, 512, 1024, 2048, 3072, 3584, 3712

        def swp(base, h):
            d = np.arange(64)
            sw = np.where((d % 32) < 16, d + 16, d - 16)
            return base + h * 64 + sw

        cols = []
        for hp in range(4):
            h0, h1 = 2 * hp, 2 * hp + 1
            cols += [QR + h0 * 64 + np.arange(64), QR + h1 * 64 + np.arange(64)]
            cols += [KR + h0 * 64 + np.arange(64), KR + h1 * 64 + np.arange(64)]
            cols += [swp(QR, h0), swp(QR, h1), swp(KR, h0), swp(KR, h1)]
            cols += [GR + h0 * 128 + np.arange(128), GR + h1 * 128 + np.arange(128)]
            cols += [VR + h0 * 128 + np.arange(128), VR + h1 * 128 + np.arange(128)]
        for c in range(4):
            cols += [QA + c * 64 + np.arange(64), QA + (c + 4) * 64 + np.arange(64)]
        for c in range(4):
            cols += [swp(QA, c), swp(QA, c + 4)]
        cols += [swp(KA, 0), swp(KA, 1), KA + np.arange(128), VA + np.arange(128)]
        cols = np.concatenate(cols)
        assert cols.shape[0] == 4 * 1024 + 512 + 512 + 384
        sh["even_w_in"] = wl(Wi[:, cols])
        Wo = np.asarray(inp["even_w_out"][0], np.float32)
        rows = [np.arange(1024)]
        for c in range(4):
            rows += [1024 + c * 64 + np.arange(64), 1024 + (c + 4) * 64 + np.arange(64)]
        sh["even_w_out"] = wl(Wo[np.concatenate(rows)])
        es = np.zeros((128, 24), np.float32)
        df, db = np.asarray(inp["ret_decay_fwd"][0]), np.asarray(inp["ret_decay_bwd"][0])
        for hp in range(4):
            for x in range(2):
                es[64 * x:64 * x + 64, hp * 2 + 0] = df[2 * hp + x]
                es[64 * x:64 * x + 64, hp * 2 + 1] = db[2 * hp + x]
        es[:, 8:16] = np.asarray(inp["ret_gn"][0]).reshape(8, 128).T
        d = np.arange(64)
        sw = np.where((d % 32) < 16, d + 16, d - 16)
        gq, gk = np.asarray(inp["gqa_q_norm"][0]), np.asarray(inp["gqa_k_norm"][0])
        es[:, 16] = np.concatenate([gq, gq]); es[:, 17] = np.concatenate([gq[sw], gq[sw]])
        es[:, 18] = np.concatenate([gk, gk]); es[:, 19] = np.concatenate([gk[sw], gk[sw]])
        sh["evsmall"] = es
        er = np.zeros((128, 84), np.float32)
        er[:, 0:8] = df[None, :]; er[:, 8:16] = db[None, :]
        er[:, 16:80] = gk[None, :]
        p = np.arange(128)
        er[:, 80] = 255 - p; er[:, 81] = 255 - (128 + p); er[:, 82] = p; er[:, 83] = 128 + p
        sh["evrow"] = er
        tabs = np.zeros((128, 3 * 1024 + 256), np.float32)
        tabs[:, 0:1024] = (np.arange(1024) - 512)[None, :]
        t = np.arange(1024)
        row = (t // 64).astype(np.float32); col = (t % 64).astype(np.float32)
        inv = (10000.0 ** (-np.arange(0, 32, 2, dtype=np.float32) / 32.0)).astype(np.float32)
        ang_r = (row[:, None] * inv[None, :]).astype(np.float32)
        ang_c = (col[:, None] * inv[None, :]).astype(np.float32)
        cosd = np.zeros((64, 1024), np.float32); sind = np.zeros((64, 1024), np.float32)
        for dd in range(64):
            ang = ang_r if dd < 32 else ang_c
            i = dd % 16
            cosd[dd] = np.cos(ang[:, i])
            sind[dd] = np.sin(ang[:, i]) * (-1.0 if (dd % 32) < 16 else 1.0)
        tabs[:, 1280:2304] = np.concatenate([cosd, cosd], 0)
        tabs[:, 2304:3328] = np.concatenate([sind, sind], 0)
        jj = np.arange(128)[:, None]; ii = np.arange(128)[None, :]
        tabs[:, 1024:1152] = np.maximum(jj - ii, 0).astype(np.float32)
        sh["evtab"] = tabs
    if cfg["odd"]:
        Wi = inp["odd_w_in"][0]
        cols = []
        for hp in range(8):
            for part in range(3):
                cols.append(part * 1024 + hp * 128 + np.arange(128))
        sh["odd_w_in"] = wl(Wi[:, np.concatenate(cols)])
        sh["odd_w_out"] = wl(inp["odd_w_out"][0])
        rpb = np.asarray(inp["na_rpb"][0], np.float32)
        kc = np.arange(64)[:, None]
        cc = np.arange(64)[None, :]
        relc = kc - cc + 15
        okc = (relc >= 0) & (relc <= 30)
        relc_c = np.clip(relc, 0, 30)
        rx = np.zeros((16, 128, 15, 64), np.float32)
        for s_ in range(15):
            g = np.where(okc[None], rpb[:, 14 - s_, :][:, relc_c], np.float32(0.0))
            rx[:, 0:64, s_, :] = g
            rx[:, 64:128, s_, :] = g
        sh["rpbx"] = rx
        cs = np.clip(np.arange(64) - 8, 0, 48)[None, :]
        win = (kc >= cs) & (kc < cs + 16)
        cm = np.where(win, 0.0, -30000.0).astype(np.float32)
        sh["colmask"] = np.ascontiguousarray(np.concatenate([cm, cm], 0))
    return sh


def prep_core(inp, i, cfg):
    m = {}
    xp = np.asarray(inp["x_prompt"][4 * i:4 * i + 4], np.float32).reshape(1024, D)
    xsm = np.asarray(inp["x_sample"][i], np.float32)
    m["x_tok"] = np.ascontiguousarray(np.concatenate([xp, xsm], 0))
    cond = np.stack([inp["c_ctx"], inp["c"][i]], -1)
    condT = cond.reshape(8, 128, 2).transpose(1, 0, 2).reshape(128, 16)
    m["consts"] = np.ascontiguousarray(np.concatenate([np.eye(128, dtype=np.float32), condT], 1), np.float32)
    if cfg["even"]:
        m["cache_gqa_k"] = np.ascontiguousarray(inp["cache_gqa_k"][i, 0], np.float32)
        m["cache_gqa_v"] = np.ascontiguousarray(inp["cache_gqa_v"][i, 0], np.float32)
        m["state_ret_fwd"] = np.ascontiguousarray(inp["state_ret_fwd"][i, 0], np.float32)
        m["state_ret_bwd"] = np.ascontiguousarray(inp["state_ret_bwd"][i, 0], np.float32)
    if cfg["odd"]:
        m["cache_na_k"] = np.ascontiguousarray(inp["cache_na_k"][i, 0], np.float32)
        m["cache_na_v"] = np.ascontiguousarray(inp["cache_na_v"][i, 0], np.float32)
    return m


def run(inp, cfg):
    key = tuple(sorted(cfg.items()))
    if key not in _PROG:
        plan = build_program(cfg, None)
        _PROG[key] = build_program(cfg, plan)
    nc = _PROG[key]
    inp = {k: np.asarray(v) for k, v in inp.items()}
    sh = prep_shared(inp, cfg)
    in_maps = []
    for i in range(8):
        m = dict(sh)
        m.update(prep_core(inp, i, cfg))
        in_maps.append(m)
    ncores = cfg.get("ncores", 8)
    res = run_bass_kernel_spmd(nc, in_maps[:ncores], core_ids=list(range(ncores)))
    return res.results


def kernel(**inputs):
    cfg = dict(CFG_DEFAULT)
    r = run(inputs, cfg)
    y = np.stack([r[i]["y_out"] for i in range(8)])
    y_prompt = np.ascontiguousarray(y[:, :1024].reshape(32, 256, D))
    y_sample = np.ascontiguousarray(y[:, 1024:])

    def cat(name):
        return np.ascontiguousarray(np.concatenate([r[i][name] for i in range(8)], 0)[:, None])

    return (y_prompt, y_sample, cat("sf_out"), cat("sb_out"), cat("gk_out"), cat("gv_out"),
            cat("nk_out"), cat("nv_out"))
```

```python
import contextlib
import numpy as np
import concourse.bass as bass
import concourse.mybir as mybir
from concourse.bass_utils import run_bass_kernel_spmd

F32 = mybir.dt.float32
BF16 = mybir.dt.bfloat16
AF = mybir.ActivationFunctionType
ALU = mybir.AluOpType
AX = mybir.AxisListType

ENGS = ("pe", "act", "dve", "pool", "sp")


class TT:
    __slots__ = ("name", "w", "r", "sem", "semcnt", "excl")

    def __init__(self, name):
        self.name = name
        self.excl = False
        self.w = None
        self.r = []
        self.sem = None
        self.semcnt = 0


class Op:
    __slots__ = ("eng", "fn", "deps", "sig", "cnt", "dma_tile", "idx")


class Sched:
    def __init__(self, nc, same_engine_sync=True):
        self.nc = nc
        self.ops = []
        self.same = same_engine_sync
        self.n_tiles = 0
        self.fence_deps = set()
        self.fence_start = 0

    def tile(self, name=None):
        self.n_tiles += 1
        return TT(name or f"t{self.n_tiles}")

    def _add(self, eng, fn, reads, writes, dma_tile=None):
        op = Op()
        op.eng = eng
        op.fn = fn
        op.idx = len(self.ops)
        op.sig = dma_tile is not None
        op.cnt = 0
        op.dma_tile = dma_tile
        deps = set()
        for t in reads:
            if t.w is not None:
                deps.add(t.w)
            if t.excl:
                for r in t.r:
                    if self.ops[r].eng != eng:
                        deps.add(r)
        for t in writes:
            if t.w is not None:
                deps.add(t.w)
            for r in t.r:
                deps.add(r)
        deps |= self.fence_deps
        deps.discard(op.idx)
        op.deps = deps
        for t in reads:
            t.r.append(op.idx)
        for t in writes:
            t.w = op.idx
            t.r = []
        self.ops.append(op)
        return op

    def fence(self):
        lasts = {}
        for op in self.ops:
            if op.dma_tile is None:
                lasts[op.eng] = op.idx
        nd = set(lasts.values())
        for op in self.ops[self.fence_start:]:
            if op.dma_tile is not None:
                nd.add(op.idx)
        self.fence_deps = self.fence_deps | nd
        self.fence_start = len(self.ops)

    def op(self, eng, fn, reads=(), writes=()):
        return self._add(eng, fn, list(reads), list(writes))

    def dma(self, q, out, in_, reads=(), writes=(), tile=None):
        assert tile is not None
        return self._add(q, lambda e: e.dma_start(out=out, in_=in_), list(reads), list(writes), dma_tile=tile)

    def emit(self):
        nc = self.nc
        ops = self.ops
        for op in ops:
            nd = set()
            for d in op.deps:
                p = ops[d]
                if p.dma_tile is None and p.eng == op.eng:
                    if p.eng == "pe" or not self.same:
                        continue
                nd.add(d)
            op.deps = nd
            for d in nd:
                ops[d].sig = True
        esem = {e: nc.alloc_semaphore(f"s_{e}") for e in ENGS}
        ecnt = {e: 0 for e in ENGS}
        for op in ops:
            if op.dma_tile is not None:
                t = op.dma_tile
                if t.sem is None:
                    self.n_dsem = getattr(self, "n_dsem", 0) + 1
                    t.sem = nc.alloc_semaphore(f"d{self.n_dsem}_{t.name}")
                t.semcnt += 16
                op.cnt = t.semcnt
            elif op.sig:
                ecnt[op.eng] += 1
                op.cnt = ecnt[op.eng]
        waited = {e: {} for e in ENGS}

        def emit_engine(ename, eobj):
            wd = waited[ename]
            for op in ops:
                if op.eng != ename:
                    continue
                need = {}
                for d in op.deps:
                    p = ops[d]
                    sem = p.dma_tile.sem if p.dma_tile is not None else esem[p.eng]
                    key = id(sem)
                    if key not in need or need[key][1] < p.cnt:
                        need[key] = (sem, p.cnt)
                for key, (sem, v) in need.items():
                    if wd.get(key, 0) >= v:
                        continue
                    eobj.wait_ge(sem, v)
                    wd[key] = v
                ins = op.fn(eobj)
                if op.dma_tile is not None:
                    ins.then_inc(op.dma_tile.sem, 16)
                elif op.sig:
                    ins.then_inc(esem[ename], 1)

        with nc.Block() as block:
            @block.tensor
            def _(e):
                emit_engine("pe", e)

            @block.scalar
            def _(e):
                emit_engine("act", e)

            @block.vector
            def _(e):
                emit_engine("dve", e)

            @block.gpsimd
            def _(e):
                emit_engine("pool", e)

            @block.sync
            def _(e):
                emit_engine("sp", e)


D = 1024
NT = 2048
DFF = 2816
NFC = 22
EPS = 1e-6
GN_EPS = 1e-5
GROUPS = [(0, 512, 0, 2, 256), (512, 512, 0, 2, 256), (1024, 1024, 1, 1, 1024)]
FFN_PHASES = [(0, 6), (6, 12), (12, 17), (17, 22)]

CFG_DEFAULT = dict(even=True, odd=True, ffn=True, dbg=False)


class Ctx:
    pass


def build_program(cfg, plan=None):
    nc = bass.Bass("TRN2", target_bir_lowering=False)
    S = Sched(nc)
    K = Ctx()
    K.nc, K.S, K.cfg = nc, S, cfg
    K.outs = []

    K.dram = {}

    def din(name, shape, dt=F32):
        a = nc.dram_tensor(name, list(shape), dt, kind="ExternalInput").ap()
        K.dram[name] = a
        return a

    def dout(name, shape):
        return nc.dram_tensor(name, list(shape), F32, kind="ExternalOutput").ap()

    def sb(name, shape, dt=F32):
        return nc.alloc_sbuf_tensor(name, list(shape), dt)

    K.uid = [0]

    def sbs(stack, name, shape, dt=F32):
        K.uid[0] += 1
        return stack.enter_context(nc.sbuf_tensor(f"{name}_{K.uid[0]}", list(shape), dt))

    K.din, K.dout, K.sb, K.sbs = din, dout, sb, sbs

    def MM(out, lhsT, rhs, start, stop, R, W):
        S.op("pe", lambda e: e.matmul(out, lhsT=lhsT, rhs=rhs, start=start, stop=stop), R, W)

    def TR(out, in_, ident, R, W):
        S.op("pe", lambda e: e.transpose(out, in_, ident), R, W)

    def ACT(out, in_, func, R, W, scale=1.0, bias=None):
        if bias is None:
            S.op("act", lambda e: e.activation(out=out, in_=in_, func=func, scale=scale), R, W)
        else:
            S.op("act", lambda e: e.activation(out=out, in_=in_, func=func, scale=scale, bias=bias), R, W)

    def TTO(eng, out, in0, in1, op, R, W):
        S.op(eng, lambda e: e.tensor_tensor(out=out, in0=in0, in1=in1, op=op), R, W)

    def STT(eng, out, in0, scalar, in1, op0, op1, R, W):
        S.op(eng, lambda e: e.scalar_tensor_tensor(out=out, in0=in0, scalar=scalar, in1=in1, op0=op0, op1=op1), R, W)

    def TS(eng, out, in0, s1, s2, op0, op1, R, W):
        if s2 is None:
            S.op(eng, lambda e: e.tensor_scalar(out=out, in0=in0, scalar1=s1, scalar2=None, op0=op0), R, W)
        else:
            S.op(eng, lambda e: e.tensor_scalar(out=out, in0=in0, scalar1=s1, scalar2=s2, op0=op0, op1=op1), R, W)

    def CP(eng, out, in_, R, W):
        if eng == "act":
            S.op("act", lambda e: e.copy(out=out, in_=in_), R, W)
        else:
            S.op(eng, lambda e: e.tensor_copy(out=out, in_=in_), R, W)

    def MSET(eng, ap, val, W):
        S.op(eng, lambda e: e.memset(ap, val), [], W)

    def DMA(q, out, in_, R, W, tile):
        S.dma(q, out, in_, R, W, tile)

    def RCP(out, in_, R, out_t):
        ACT(out, in_, AF.Ln, R, [out_t])
        ACT(out, out, AF.Exp, [out_t], [out_t], scale=-1.0)

    K.RCP = RCP
    K.MM, K.TR, K.ACT, K.TTO, K.STT, K.TS, K.CP, K.MSET, K.DMA = MM, TR, ACT, TTO, STT, TS, CP, MSET, DMA

    K.pd = [nc.alloc_psum_tensor(f"pd{i}", [128, 1024], F32) for i in range(4)]
    K.pt = [S.tile(f"pb{i}") for i in range(8)]
    for t_ in K.pt:
        t_.excl = True
    K.rr = [0]

    def bank(i):
        d, h = divmod(i, 2)
        return K.pd[d], h * 512, K.pt[i]

    def next_bank():
        i = K.rr[0] % 8
        K.rr[0] += 1
        return bank(i)

    def next_dbank():
        if K.rr[0] % 2:
            K.rr[0] += 1
        i = K.rr[0] % 8
        K.rr[0] += 2
        return K.pd[i // 2], [K.pt[i], K.pt[i + 1]]

    K.bank, K.next_bank, K.next_dbank = bank, next_bank, next_dbank
    K.bank_alloc = next_bank

    x_tok = din("x_tok", [NT, D])
    consts = din("consts", [128, 128 + 16])
    smallp = din("smallp", [128, 2 * 48 + 5 * 8])
    ada_w = din("ada_w", [2, 128, 8, 6 * D])
    if cfg["even"]:
        din("even_w_in", [128, 8, 4 * 1024 + 512 + 512 + 384])
        din("even_w_out", [128, 12, D])
    if cfg["ffn"]:
        din("w_up", [2, 128, 8, 2 * DFF])
        din("w_dn", [2, 128, NFC, D])
    if cfg["odd"]:
        din("odd_w_in", [128, 8, 3072])
        din("odd_w_out", [128, 8, D])
    y_out = dout("y_out", [NT, D])

    xT = sb("xT", [128, 8, NT], F32)
    K.xT = xT
    K.xT_t = [S.tile(f"xT{b}") for b in range(4)]
    hT = sb("hT", [128, 8, NT], BF16)
    K.hT = hT
    K.hT_t = [S.tile(f"hT{b}") for b in range(4)]
    cst = sb("cst", [128, 128 + 16], F32)
    cst_t = S.tile("cst")
    K.ident = cst[:, 0:128]
    smp = sb("smp", [128, 2 * 48 + 40], F32)
    smp_t = S.tile("smp")
    ones_bf = sb("ones_bf", [128, 128], BF16)
    ones_t = S.tile("ones")
    neghalf = sb("neghalf", [128, 512], F32)
    nh_t = S.tile("nh")
    K.ones_bf, K.ones_t, K.neghalf, K.nh_t, K.cst_t = ones_bf, ones_t, neghalf, nh_t, cst_t

    DMA("sp", cst[:], consts, [], [cst_t], cst_t)
    DMA("sp", smp[:], smallp, [], [smp_t], smp_t)
    MSET("pool", ones_bf[:], 1.0, [ones_t])
    MSET("pool", neghalf[:], -0.5, [nh_t])
    epsc = sb("epsc", [128, 2], F32)
    epsc_t = S.tile("epsc")
    MSET("pool", epsc[:, 0:1], EPS, [epsc_t])
    MSET("pool", epsc[:, 1:2], GN_EPS, [epsc_t])
    K.epsc, K.epsc_t = epsc, epsc_t

    scT = sb("scT", [128, 16], BF16)
    scT_t = S.tile("scT")
    ACT(scT[:], cst[:, 128:144], AF.Silu, [cst_t], [scT_t])
    mT = sb("mT", [128, 2, 48, 2], F32)
    mT_t = S.tile("mT")
    wraw = [sb(f"wa{i}", [128, 4096], BF16) for i in range(3)]
    wa = [w[:, :].rearrange("p (k n) -> p k n", k=8) for w in wraw]
    wa_t = [S.tile(f"wa{i}") for i in range(3)]
    K.wraw, K.wslot_t = wraw, wa_t
    K.wplan = []
    K.wpos = [0]
    K.wissued = [0]

    def wview(sidx, k, n):
        return wraw[sidx][:, 0:k * n].rearrange("p (k n) -> p k n", k=k)

    def wissue(q, name, idx, k, n):
        sidx = q % 3
        DMA("pool", wview(sidx, k, n), K.dram[name][idx], [], [wa_t[sidx]], wa_t[sidx])

    def wnext(name, idx, k, n, live_prev=0):
        p = K.wpos[0]
        K.wpos[0] += 1
        if plan is None:
            K.wplan.append((name, idx, k, n))
            wissue(p, name, idx, k, n)
        else:
            assert plan[p] == (name, idx, k, n), (p, plan[p], name, idx, k, n)
            upto = min(len(plan) - 1, p + 2 - live_prev)
            while K.wissued[0] <= upto:
                q = K.wissued[0]
                wissue(q, *plan[q])
                K.wissued[0] += 1
        return wview(p % 3, k, n), wa_t[p % 3]

    K.wnext = wnext
    mT_tl = [S.tile(f"mT{l}") for l in range(2)]
    mod = sb("mod", [128, 2, 6, 8, 2], F32)
    mod_t = [S.tile(f"mod{l}") for l in range(2)]
    K.mod, K.mod_t = mod, mod_t
    K.gfin = smp[:, 96 + 32:96 + 40]
    K.smp_t = smp_t
    K.ada_pending = [(1, blk) for blk in range(12)]

    def ada_block(l, blk):
        wv, wv_t = wnext("ada_w", (l, slice(None), slice(None), slice(blk * 512, (blk + 1) * 512)), 8, 512)
        pd, off, ptile = K.bank_alloc()
        for oc in range(4):
            for k in range(8):
                MM(pd[:, off + oc * 2: off + oc * 2 + 2], wv[:, k, oc * 128:(oc + 1) * 128],
                   scT[:, k * 2:(k + 1) * 2], k == 0, k == 7, [wv_t, scT_t], [ptile])
        TTO("dve", mT[:, l, blk * 4:(blk + 1) * 4, :],
            pd[:, off:off + 8].rearrange("p (c g) -> p c g", g=2),
            smp[:, l * 48 + blk * 4: l * 48 + blk * 4 + 4].unsqueeze(2).broadcast_to([128, 4, 2]),
            ALU.add, [ptile, smp_t], [mT_tl[l]])

    def ada_finish(l):
        for (dst, jscale, jshift, jgate, gi) in [(0, 1, 0, 2, l), (3, 4, 3, 5, 2 + l)]:
            gn = smp[:, 96 + gi * 8: 96 + gi * 8 + 8].unsqueeze(2).broadcast_to([128, 8, 2])
            STT("dve", mod[:, l, dst, :, :], mT[:, l, jscale * 8:(jscale + 1) * 8, :], 1.0, gn, ALU.add, ALU.mult,
                [mT_tl[l], smp_t], [mod_t[l]])
            CP("dve", mod[:, l, dst + 1, :, :], mT[:, l, jshift * 8:(jshift + 1) * 8, :], [mT_tl[l]], [mod_t[l]])
            CP("dve", mod[:, l, dst + 2, :, :], mT[:, l, jgate * 8:(jgate + 1) * 8, :], [mT_tl[l]], [mod_t[l]])

    def ada_some(n):
        for _ in range(n):
            if K.ada_pending:
                ada_block(*K.ada_pending.pop(0))

    K.ada_some = ada_some

    K.nrr = [0]
    K.t1rr = [0]
    K.nb = None

    class NormBufs:
        def __init__(self, st):
            self.sq = [K.sbs(st, f"sq{i}", [128, 8, 512], BF16) for i in range(2)]
            self.sq_t = [[S.tile(f"sq{i}a"), S.tile(f"sq{i}b")] for i in range(2)]
            self.rs = [K.sbs(st, f"rs{i}", [128, 512], F32) for i in range(2)]
            self.rs_t = [S.tile(f"rs{i}") for i in range(2)]
            self.t1 = [K.sbs(st, f"t1_{i}", [128, 512], F32) for i in range(3)]
            self.t1_t = [S.tile(f"t1_{i}") for i in range(3)]

    def rstd_block(tb):
        i = K.nrr[0] % 2
        K.nrr[0] += 1
        nb = K.nb
        sq, sq_t, rs, rs_t = nb.sq, nb.sq_t, nb.rs, nb.rs_t
        sl = slice(tb * 512, (tb + 1) * 512)
        for hf in range(2):
            ACT(sq[i][:, 4 * hf:4 * hf + 4, :], xT[:, 4 * hf:4 * hf + 4, sl], AF.Square, [K.xT_t[tb]], [sq_t[i][hf]])
        pd, off, ptile = next_bank()
        for c in range(8):
            MM(pd[:, off:off + 512], ones_bf[:], sq[i][:, c, :], c == 0, c == 7, [ones_t, sq_t[i][c // 4]], [ptile])
        ACT(rs[i][:], pd[:, off:off + 512], AF.Ln, [ptile, K.epsc_t], [rs_t[i]], scale=1.0 / D, bias=K.epsc[:, 0:1])
        ACT(rs[i][:], rs[i][:], AF.Exp, [rs_t[i]], [rs_t[i]], scale=-0.5)
        return rs[i], rs_t[i]

    def norm_mod(l, which, tbs):
        with contextlib.ExitStack() as st:
            K.nb = NormBufs(st)
            norm_mod_body(l, which, tbs)
        S.fence()

    def norm_mod_body(l, which, tbs):
        t1, t1_t = K.nb.t1, K.nb.t1_t

        def s2(tb, rr_):
            r, r_t = rr_
            g = 0 if tb < 2 else 1
            sl = slice(tb * 512, (tb + 1) * 512)
            for c in range(8):
                j = K.t1rr[0] % 3
                K.t1rr[0] += 1
                if c < 3:
                    STT("dve", t1[j][:], xT[:, c, sl], mod[:, l, which, c, g:g + 1], r[:], ALU.mult, ALU.mult,
                        [K.xT_t[tb], mod_t[l], r_t], [t1_t[j]])
                    ACT(hT[:, c, sl], t1[j][:], AF.Identity, [t1_t[j], mod_t[l]], [K.hT_t[tb]],
                        bias=mod[:, l, which + 1, c, g:g + 1])
                else:
                    TTO("dve", t1[j][:], xT[:, c, sl], r[:], ALU.mult, [K.xT_t[tb], r_t], [t1_t[j]])
                    TS("dve", hT[:, c, sl], t1[j][:], mod[:, l, which, c, g:g + 1], mod[:, l, which + 1, c, g:g + 1],
                       ALU.mult, ALU.add, [t1_t[j], mod_t[l]], [K.hT_t[tb]])

        pipeline(list(tbs), rstd_block, s2, 1)

    K.norm_mod = norm_mod
    K.rstd_block = rstd_block

    if cfg["ffn"]:
        convp = din("convp", [128, 2, 4, 2 * NFC])
        cvp = sb("cvp", [128, 2, 4, 2 * NFC], F32)
        cvp_t = S.tile("cvp")
        DMA("sp", cvp[:], convp, [], [cvp_t], cvp_t)
        K.ffrr = [0, 0, 0]

    def ffn(l):
        with contextlib.ExitStack() as st:
            K.nb = NormBufs(st)
            norm_mod_body(l, 3, [0, 1, 2, 3])
            ffn_body(l, st)
        S.fence()

    def ffn_body(l, st):
        actT = K.sbs(st, "actT", [128, 6, NT], BF16)
        actT_t = [[S.tile(f"actT{c}_{b}") for b in range(3)] for c in range(6)]
        acc = [[K.sbs(st, f"acc{h}{i}", [128, 1024], F32) for i in range(2)] for h in range(2)]
        acc_t = [[S.tile(f"acc{h}{i}") for i in range(2)] for h in range(2)]
        sil = [K.sbs(st, f"sil{i}", [128, 1024], F32) for i in range(2)]
        sil_t = [S.tile(f"sil{i}") for i in range(2)]
        for (c0, c1) in FFN_PHASES:
            for c in range(c0, c1):
                wuv, wuv_t = wnext("w_up", (l, slice(None), slice(None), slice(c * 256, (c + 1) * 256)), 8, 256)
                for gi, (t0, T, g, nseq, slen) in enumerate(GROUPS):
                    accs = []
                    for h in range(2):
                        pdt, ptl = next_dbank()
                        for sb_ in range(T // 512):
                            for k in range(8):
                                MM(pdt[:, sb_ * 512:(sb_ + 1) * 512], wuv[:, k, h * 128:(h + 1) * 128],
                                   hT[:, k, t0 + sb_ * 512: t0 + (sb_ + 1) * 512], k == 0, k == 7,
                                   [wuv_t, K.hT_t[(t0 // 512) + sb_]], [ptl[sb_]])
                        pts = ptl[:T // 512]
                        i = K.ffrr[1] % 2
                        if h == 1:
                            K.ffrr[1] += 1
                        a, a_t = acc[h][i], acc_t[h][i]
                        ci = c * 2 + h
                        U = pdt[:, 0:T]
                        ACT(a[:, 0:T], U, AF.Identity, pts + [cvp_t], [a_t],
                            scale=cvp[:, l, 1, ci:ci + 1], bias=cvp[:, l, 3, ci:ci + 1])
                        U3 = U.rearrange("p (s t) -> p s t", s=nseq)
                        a3 = a[:, 0:T].rearrange("p (s t) -> p s t", s=nseq)
                        STT("dve", a3[:, :, 1:slen], U3[:, :, 0:slen - 1], cvp[:, l, 0, ci:ci + 1], a3[:, :, 1:slen],
                            ALU.mult, ALU.add, pts + [cvp_t, a_t], [a_t])
                        STT("dve", a3[:, :, 0:slen - 1], U3[:, :, 1:slen], cvp[:, l, 2, ci:ci + 1], a3[:, :, 0:slen - 1],
                            ALU.mult, ALU.add, pts + [cvp_t, a_t], [a_t])
                        accs.append((a, a_t))
                    i2 = K.ffrr[2] % 2
                    K.ffrr[2] += 1
                    ACT(sil[i2][:, 0:T], accs[0][0][:, 0:T], AF.Silu, [accs[0][1]], [sil_t[i2]])
                    TTO("pool", actT[:, c - c0, t0:t0 + T], sil[i2][:, 0:T], accs[1][0][:, 0:T], ALU.mult,
                        [sil_t[i2], accs[1][1]], [actT_t[c - c0][gi]])
            npc = c1 - c0
            for dh in range(2):
                wdv, wdv_t = wnext("w_dn", (l, slice(None), slice(c0, c1), slice(dh * 512, (dh + 1) * 512)), npc, 512)
                for dc in range(4):
                    dmc = dh * 4 + dc
                    for tb in range(4):
                        g = 0 if tb < 2 else 1
                        gi = tb if tb < 2 else 2
                        pd, off, ptile = next_bank()
                        for cc in range(npc):
                            MM(pd[:, off:off + 512], wdv[:, cc, dc * 128:(dc + 1) * 128],
                               actT[:, cc, tb * 512:(tb + 1) * 512], cc == 0, cc == npc - 1,
                               [wdv_t, actT_t[cc][gi]], [ptile])
                        STT("dve", xT[:, dmc, tb * 512:(tb + 1) * 512], pd[:, off:off + 512],
                            mod[:, l, 5, dmc, g:g + 1], xT[:, dmc, tb * 512:(tb + 1) * 512], ALU.mult, ALU.add,
                            [ptile, mod_t[l], K.xT_t[tb]], [K.xT_t[tb]])

    K.ffn = ffn

    xst = contextlib.ExitStack()
    xs = [K.sbs(xst, f"xs{i}", [128, D], F32) for i in range(2)]
    xs_t = [S.tile(f"xs{i}") for i in range(2)]
    K.nb = NormBufs(xst)
    for tt in range(16):
        s = tt % 2
        DMA("sp", xs[s][:], x_tok[tt * 128:(tt + 1) * 128, :], [], [xs_t[s]], xs_t[s])
        for half in range(2):
            pd, off, ptile = next_bank()
            for j in range(4):
                c = half * 4 + j
                TR(pd[:, off + j * 128: off + (j + 1) * 128], xs[s][:, c * 128:(c + 1) * 128], K.ident,
                   [xs_t[s], cst_t], [ptile])
            eng = "act" if (tt + half) % 2 else "dve"
            CP(eng, xT[:, half * 4:(half + 1) * 4, tt * 128:(tt + 1) * 128],
               pd[:, off:off + 512].rearrange("p (c t) -> p c t", c=4), [ptile], [K.xT_t[tt // 4]])
        if tt < 12:
            ada_block(0, tt)
    ada_finish(0)
    K.first_norm = False
    if cfg["even"]:
        norm_mod_body(0, 0, [0, 1, 2, 3])
        K.first_norm = True
    xst.close()
    S.fence()

    for l in range(2):
        if l == 1:
            ada_some(12)
            ada_finish(1)
        if l == 0 and cfg["even"]:
            even_mixer(K, l)
        if l == 1 and cfg["odd"]:
            odd_mixer(K, l)
        if cfg["ffn"]:
            ffn(l)

    S.fence()
    fst = contextlib.ExitStack()
    K.nb = NormBufs(fst)
    yT = [K.sbs(fst, f"yT{i}", [128, 8, 512], F32) for i in range(2)]
    yT_t = [S.tile(f"yT{i}") for i in range(2)]
    ys = [K.sbs(fst, f"ys{i}", [128, D], F32) for i in range(4)]
    ys_t = [S.tile(f"ys{i}") for i in range(4)]

    def fin_s2(tb, rr_):
        r, r_t = rr_
        yb = tb % 2
        sl = slice(tb * 512, (tb + 1) * 512)
        for c in range(8):
            STT("dve", yT[yb][:, c, :], xT[:, c, sl], K.gfin[:, c:c + 1], r[:], ALU.mult, ALU.mult,
                [K.xT_t[tb], smp_t, r_t], [yT_t[yb]])
        for q in range(4):
            tt = tb * 4 + q
            s_ = tt % 4
            for half in range(2):
                pd, off, ptile = next_bank()
                for j in range(4):
                    c = half * 4 + j
                    TR(pd[:, off + j * 128: off + (j + 1) * 128], yT[yb][:, c, q * 128:(q + 1) * 128], K.ident,
                       [yT_t[yb], cst_t], [ptile])
                eng = "act" if half else "dve"
                CP(eng, ys[s_][:, half * 512:(half + 1) * 512], pd[:, off:off + 512], [ptile], [ys_t[s_]])
            DMA("sp" if tt % 2 == 0 else "pool", y_out[tt * 128:(tt + 1) * 128, :], ys[s_][:], [ys_t[s_]], [], ys_t[s_])
            if ys_t[s_] not in K.outs:
                K.outs.append(ys_t[s_])

    pipeline([0, 1, 2, 3], rstd_block, fin_s2, 1)

    if plan is None:
        return K.wplan
    assert K.wpos[0] == len(plan)
    S.op("sp", lambda e: e.nop(), [], K.outs)
    S.emit()
    return nc


def even_mixer(K, l):
    nc, S = K.nc, K.S
    MM, TR, ACT, TTO, STT, TS, CP, MSET, DMA = K.MM, K.TR, K.ACT, K.TTO, K.STT, K.TS, K.CP, K.MSET, K.DMA
    RCP = K.RCP
    hT, hT_t, xT, xT_t = K.hT, K.hT_t, K.xT, K.xT_t
    NCOL = 4 * 1024 + 512 + 512 + 384
    evsmall_d = K.din("evsmall", [128, 24])
    evrow_d = K.din("evrow", [128, 84])
    evtab_d = K.din("evtab", [128, 3 * 1024 + 256])
    cgk = K.din("cache_gqa_k", [2, 256, 64])
    cgv = K.din("cache_gqa_v", [2, 256, 64])
    s0f_d = K.din("state_ret_fwd", [8, 64, 128])
    s0b_d = K.din("state_ret_bwd", [8, 64, 128])
    sf_out = K.dout("sf_out", [4, 8, 64, 128])
    sb_out = K.dout("sb_out", [4, 8, 64, 128])
    gk_out = K.dout("gk_out", [4, 2, 256, 64])
    gv_out = K.dout("gv_out", [4, 2, 256, 64])

    if not K.first_norm:
        K.norm_mod(l, 0, [0, 1, 2, 3])
    st = contextlib.ExitStack()
    sbs = lambda n, sh, dt=F32: K.sbs(st, n, sh, dt)
    NH = 1024
    evs = sbs("evs", [128, 24], F32)
    evs_t = S.tile("evs")
    evr = sbs("evr", [128, 84], F32)
    evr_t = S.tile("evr")
    lgp = sbs("lgp", [128, 4, 4], F32)
    lgp_t = S.tile("lgp")
    lgr = sbs("lgr", [128, 16], F32)
    lgr_t = S.tile("lgr")
    dk = sbs("dk", [128, 2, 2, 8], F32)
    dk_t = S.tile("dk")
    scs = sbs("scs", [128, 4, 2], F32)
    scs_t = S.tile("scs")
    tab = sbs("tab", [128, 1024 + 256], F32)
    tab_t = S.tile("tab")
    POS = tab[:, 0:1024]
    RELUD = tab[:, 1024:1152]
    bones = sbs("bones", [128, 128], BF16)
    bones_t = S.tile("bones")
    rr = dict(b=0)
    K.side = []

    def side_pop(n=1):
        for _ in range(n):
            if K.side:
                K.side.pop(0)[1]()

    def side_flush(tag=None):
        if tag is None:
            n = len(K.side)
        else:
            idx = [i for i, (t_, _) in enumerate(K.side) if t_ == tag]
            n = idx[-1] + 1 if idx else 0
        for _ in range(n):
            K.side.pop(0)[1]()

    K.side_pop, K.side_flush = side_pop, side_flush

    def rot(key, n):
        i = rr.get(key, 0) % n
        rr[key] = rr.get(key, 0) + 1
        return i

    K.ev_nb = 3

    def rbank():
        return K.bank(4 + rot("b", K.ev_nb))

    K.bank_alloc = rbank

    DMA("sp", evs[:], evsmall_d, [], [evs_t], evs_t)
    DMA("sp", evr[:], evrow_d, [], [evr_t], evr_t)
    DMA("sp", tab[:], evtab_d[:, 0:1280], [], [tab_t], tab_t)
    MSET("pool", bones[:], 0.0, [bones_t])
    MSET("pool", bones[0:64, 0:64], 1.0, [bones_t])
    MSET("pool", bones[64:128, 64:128], 1.0, [bones_t])
    ACT(lgp[:, :, 0:2], evs[:, 0:8].rearrange("p (a b) -> p a b", b=2), AF.Exp, [evs_t], [lgp_t], scale=-1.0)
    ACT(lgp[:, :, 2:4], lgp[:, :, 0:2], AF.Ln, [lgp_t], [lgp_t], bias=1.0)
    TS("dve", lgp[:, :, 0:2], lgp[:, :, 2:4], -1.0, None, ALU.mult, None, [lgp_t], [lgp_t])
    ACT(lgr[:], evr[:, 0:16], AF.Exp, [evr_t], [lgr_t], scale=-1.0)
    ACT(lgr[:], lgr[:], AF.Ln, [lgr_t], [lgr_t], bias=1.0)
    TS("dve", lgr[:], lgr[:], -1.0, None, ALU.mult, None, [lgr_t], [lgr_t])
    lgs = sbs("lgs", [128, 8], F32)
    lgs_t = S.tile("lgs")
    TTO("dve", lgs[:], lgr[:, 0:8], lgr[:, 8:16], ALU.add, [lgr_t], [lgs_t])
    cdg = sbs("cdg", [128, 2, 128], F32)
    cdg_t = S.tile("cdg")
    for d_ in range(2):
        for t in range(2):
            ACT(dk[:, d_, t, :], lgr[:, d_ * 8:(d_ + 1) * 8], AF.Exp, [lgr_t, evr_t], [dk_t],
                scale=evr[:, 80 + d_ * 2 + t: 80 + d_ * 2 + t + 1])
    ACT(scs[:, :, 0:1], lgp[:, :, 0:1], AF.Exp, [lgp_t], [scs_t], scale=513.0)
    ACT(scs[:, :, 1:2], lgp[:, :, 1:2], AF.Exp, [lgp_t], [scs_t], scale=512.0)

    def load_w(c0, ncols, live_prev=0):
        return K.wnext("even_w_in", (slice(None), slice(None), slice(c0, c0 + ncols)), 8, ncols, live_prev)

    def proj_fm(w, w_t, col, tb):
        pd, off, ptile = rbank()
        for k in range(8):
            MM(pd[:, off:off + 512], w[:, k, col:col + 128], hT[:, k, tb * 512:(tb + 1) * 512], k == 0, k == 7,
               [w_t, hT_t[tb]], [ptile])
        return pd, off, ptile

    for half in range(2):
        tb0 = 2 * half
        T0 = 1024 * half
        rope = half == 1
        K.ev_nb = 3 if half == 0 else 4
        nseq, slen = (4, 256) if half == 0 else (1, 1024)
        pos0 = 384 if half == 0 else 0
        hst = contextlib.ExitStack()
        hs = lambda n, sh, dt=F32: K.sbs(hst, n, sh, dt)
        concT = hs("concT", [128, 12, NH], BF16)
        concT_t = [S.tile(f"conc{c}") for c in range(12)]
        tp = [hs(f"tp{i}", [128, 512], F32) for i in range(3)]
        tp_t = [S.tile(f"tp{i}") for i in range(3)]
        rp, rp_t, gtm, gtm_t = tp, tp_t, tp, tp_t
        pT = [hs(f"pT{i}", [128, 512], BF16) for i in range(4)]
        pT_t = [S.tile(f"pT{i}") for i in range(4)]
        if rope:
            cs = hs("cs", [128, 2048], F32)
            cs_t = S.tile("cs")
            DMA("sp", cs[:], evtab_d[:, 1280:3328], [], [cs_t], cs_t)
            COS, SIN = cs[:, 0:1024], cs[:, 1024:2048]
        else:
            cs_t = tab_t
            COS = SIN = None
        rst = contextlib.ExitStack()
        rs_ = lambda n, sh, dt=F32: K.sbs(rst, n, sh, dt)
        G = [rs_(f"G{i}", [128, 512], F32) for i in range(2)]
        G_t = [S.tile(f"G{i}") for i in range(2)]
        qk4 = [rs_(f"qk4_{i}", [128, NH], BF16) for i in range(2)]
        qk4 += [rs_(f"qk4_{i}", [128, 2, NH], BF16) for i in range(2, 4)]
        qk4_t = [S.tile(f"qk4_{i}") for i in range(4)]
        for i_ in (2, 3):
            MSET("pool", qk4[i_][64:128, 0, :], 0.0, [qk4_t[i_]])
            MSET("pool", qk4[i_][0:64, 1, :], 0.0, [qk4_t[i_]])
        vtok = rs_("vtok", [128, 8, 256], BF16)
        vtok_t = [S.tile(f"vtok{t}") for t in range(8)]
        gate = rs_("gate", [128, 2, NH], BF16)
        gate_t = [S.tile(f"gate{i}") for i in range(2)]
        obf = [rs_("obf0", [128, 2, 512], BF16)] * 2
        obf_t = [S.tile("obf0")] * 2
        osb = [rs_("osb0", [128, 512], F32)] * 2
        osb_t = [S.tile("osb0")] * 2
        gnb = rs_("gnb", [128, 512], F32)
        gnb_t = S.tile("gnb")
        if half == 0:
            kd = [rs_(f"kd{i}", [128, 8, 128], BF16) for i in range(2)]
            kd_t = [S.tile(f"kd{i}") for i in range(2)]
            stg = rs_("stg", [128, 512], F32)
            stg_t = S.tile("stg")
        else:
            s0 = rs_("s0", [128, 2, 128], F32)
            s0_t = S.tile("s0")
            s0s = rs_("s0s", [128, 2, 2, 128], BF16)
            s0s_t = S.tile("s0s")
            MSET("pool", s0s[64:128, 0, :, :], 0.0, [s0s_t])
            MSET("pool", s0s[0:64, 1, :, :], 0.0, [s0s_t])
        for hp in range(4):
            K.ada_some(1)
            wA, wA_t = load_w(hp * 1024, 512)
            if True:
                for x in range(2):
                    ACT(cdg[:, x, :], RELUD, AF.Exp, [tab_t, lgs_t], [cdg_t], scale=lgs[:, 2 * hp + x:2 * hp + x + 1])
                    TTO("pool", cdg[:, x, :], cdg[:, x, :], K.ident, ALU.add, [cdg_t, K.cst_t], [cdg_t])
            gsel = [0, 3, 2, 1]
            for bi in range(2):
                tb = tb0 + bi
                sl = slice(bi * 512, (bi + 1) * 512)
                for which in range(2):
                    K.side_pop(1)
                    pd, off, ptile = proj_fm(wA, wA_t, which * 128, tb)
                    src, src_t = pd[:, off:off + 512], ptile
                    if rope:
                        pd2, off2, ptile2 = proj_fm(wA, wA_t, 256 + which * 128, tb)
                        i1, i2 = rot("tp", 3), rot("tp", 3)
                        TTO("dve", rp[i1][:], pd[:, off:off + 512], COS[:, sl], ALU.mult, [ptile, cs_t], [rp_t[i1]])
                        TTO("dve", rp[i2][:], pd2[:, off2:off2 + 512], SIN[:, sl], ALU.mult, [ptile2, cs_t], [rp_t[i2]])
                        TTO("pool", rp[i1][:], rp[i1][:], rp[i2][:], ALU.add, [rp_t[i1], rp_t[i2]], [rp_t[i1]])
                        src, src_t = rp[i1][:], rp_t[i1]
                    for dr in range(2):
                        oi = which * 2 + dr
                        gi = rot("G", 2)
                        ps = slice(pos0 + (bi * 512 if half == 1 else 0), pos0 + (bi * 512 if half == 1 else 0) + 512)
                        if half == 0:
                            for s2 in range(2):
                                ACT(G[gi][:, s2 * 256:(s2 + 1) * 256], POS[:, 384:640], AF.Exp, [tab_t, lgp_t], [G_t[gi]],
                                    scale=lgp[:, hp, gsel[oi]:gsel[oi] + 1])
                        else:
                            ACT(G[gi][:, 0:512], POS[:, ps], AF.Exp, [tab_t, lgp_t], [G_t[gi]],
                                scale=lgp[:, hp, gsel[oi]:gsel[oi] + 1])
                        if which == 0:
                            STT("dve", qk4[oi][:, sl], src, 1.0, G[gi][:, 0:512], ALU.mult, ALU.mult,
                                [src_t, G_t[gi]], [qk4_t[oi]])
                        else:
                            for x_ in range(2):
                                ph = slice(64 * x_, 64 * x_ + 64)
                                STT("dve", qk4[oi][ph, x_, sl], src[ph, :], 0.125, G[gi][ph, 0:512], ALU.mult, ALU.mult,
                                    [src_t, G_t[gi]], [qk4_t[oi]])
            if half == 0:
                for tt in range(8):
                    K.side_pop(1)
                    tb = tb0 + tt // 4
                    tsl = slice(T0 + tt * 128, T0 + (tt + 1) * 128)
                    pd, off, ptile = rbank()
                    for k in range(8):
                        MM(pd[:, off:off + 128], hT[:, k, tsl], wA[:, k, 128:256], k == 0, k == 7, [wA_t, hT_t[tb]], [ptile])
                    for d_ in range(2):
                        STT("dve", kd[d_][:, tt, :].rearrange("p (h d) -> p h d", h=2),
                            pd[:, off:off + 128].rearrange("p (h d) -> p h d", h=2), 0.125,
                            dk[:, d_, tt % 2, 2 * hp:2 * hp + 2].unsqueeze(2).broadcast_to([128, 2, 64]),
                            ALU.mult, ALU.mult, [ptile, dk_t], [kd_t[d_]])
            wB, wB_t = load_w(hp * 1024 + 512, 512)
            def gates(x):
                for bi in range(2):
                    tb = tb0 + bi
                    sl = slice(bi * 512, (bi + 1) * 512)
                    pd, off, ptile = proj_fm(wB, wB_t, x * 128, tb)
                    ACT(gate[:, x, sl], pd[:, off:off + 512], AF.Silu, [ptile], [gate_t[x]])

            K.side_flush(0)
            gates(0)
            for tt in range(8):
                K.side_pop(1)
                tb = tb0 + tt // 4
                tsl = slice(T0 + tt * 128, T0 + (tt + 1) * 128)
                pd, off, ptile = rbank()
                for k in range(8):
                    MM(pd[:, off:off + 256], hT[:, k, tsl], wB[:, k, 256:512], k == 0, k == 7, [wB_t, hT_t[tb]], [ptile])
                CP("act", vtok[:, tt, :], pd[:, off:off + 256], [ptile], [vtok_t[tt]])
            K.side_flush()
            gates(1)
            if half == 1:
                for d_, src_d in ((0, s0f_d), (1, s0b_d)):
                    DMA("sp", s0[:, d_, :], src_d[2 * hp:2 * hp + 2].rearrange("h k v -> (h k) v"), [], [s0_t], s0_t)
                    for x_ in range(2):
                        ph = slice(64 * x_, 64 * x_ + 64)
                        TS("dve", s0s[ph, x_, d_, :], s0[ph, d_, :], scs[ph, hp, d_:d_ + 1], None, ALU.mult, None,
                           [s0_t, scs_t], [s0s_t])
            for x in range(2):
                b = 64 * x
                h = 2 * hp + x
                opd = K.pd[x]
                optl = [K.pt[2 * x], K.pt[2 * x + 1]]
                ntl = slen // 128
                blk = min(slen, 512)
                jobs = [(sq_, ib, jt) for sq_ in range(nseq) for ib in range(slen // blk) for jt in range(ntl)]
                K.side_flush(x)

                def s1(jb, b=b, x=x):
                    sq_, ib, jt = jb
                    s_off = sq_ * slen
                    c_lo = s_off + ib * blk
                    jsl = slice(s_off + jt * 128, s_off + (jt + 1) * 128)
                    pd, off, ptile = rbank()
                    its = range(ib * (blk // 128), (ib + 1) * (blk // 128))
                    fw = [it for it in its if it >= jt]
                    bw = [it for it in its if it < jt]
                    j = rot("pT", 4)
                    if bw:
                        lo, hi = s_off + bw[0] * 128, s_off + (bw[-1] + 1) * 128
                        MM(pd[:, off + lo - c_lo: off + hi - c_lo], qk4[3][:, x, jsl], qk4[1][:, lo:hi],
                           True, True, [qk4_t[3], qk4_t[1]], [ptile])
                    if fw:
                        lo, hi = s_off + fw[0] * 128, s_off + (fw[-1] + 1) * 128
                        MM(pd[:, off + lo - c_lo: off + hi - c_lo], qk4[2][:, x, jsl], qk4[0][:, lo:hi],
                           True, True, [qk4_t[2], qk4_t[0]], [ptile])
                    dlo = None
                    if jt in its:
                        dlo = s_off + jt * 128 - c_lo
                        TTO("dve", pT[j][:, dlo:dlo + 128], pd[:, off + dlo:off + dlo + 128], cdg[:, x, :], ALU.mult,
                            [ptile, cdg_t], [pT_t[j]])
                    segs = [(0, blk)] if dlo is None else [(0, dlo), (dlo + 128, blk)]
                    ceng = "act" if (jt % 2 == 0) else "dve"
                    for (lo, hi) in segs:
                        if hi > lo:
                            CP(ceng, pT[j][:, lo:hi], pd[:, off + lo:off + hi], [ptile], [pT_t[j]])
                    return j

                def s2(jb, j, b=b, x=x, opd=opd, optl=optl):
                    sq_, ib, jt = jb
                    s_off = sq_ * slen
                    c_lo = s_off + ib * blk
                    ob = c_lo // 512
                    first = jt == 0
                    if half == 1 and jt == 0:
                        for d_ in range(2):
                            MM(opd[:, c_lo:c_lo + blk], s0s[:, x, d_, :], qk4[d_][:, c_lo:c_lo + blk],
                               d_ == 0, False, [s0s_t, qk4_t[d_]], [optl[ob]])
                        first = False
                    tt = sq_ * ntl + jt
                    MM(opd[:, c_lo:c_lo + blk], vtok[:, tt, x * 128:(x + 1) * 128], pT[j][:, 0:blk],
                       first, jt == ntl - 1, [vtok_t[tt], pT_t[j]], [optl[ob]])
                    if half == 0 and jt == ntl - 1:
                        for d_ in range(2):
                            spd, soff, sptile = K.bank(7)
                            for jt2 in range(2):
                                tt2 = sq_ * 2 + jt2
                                MM(spd[64 * d_:64 * d_ + 64, soff + sq_ * 128: soff + (sq_ + 1) * 128],
                                   kd[d_][:, tt2, x * 64:(x + 1) * 64], vtok[:, tt2, x * 128:(x + 1) * 128], jt2 == 0, jt2 == 1,
                                   [kd_t[d_], vtok_t[tt2]], [sptile])

                pipeline(jobs, s1, s2, 3, K.side)
                if half == 0:
                    spd, soff, sptile = K.bank(7)
                    CP("dve", stg[:], spd[:, soff:soff + 512], [sptile], [stg_t])
                    for d_, dst in ((0, sf_out), (1, sb_out)):
                        DMA("sp", dst[:, h, :, :].rearrange("s k v -> k s v"),
                            stg[64 * d_:64 * d_ + 64, :].rearrange("k (s v) -> k s v", s=4), [stg_t], [], stg_t)
                    if stg_t not in K.outs:
                        K.outs.append(stg_t)
                def gn_steps(x=x, h=h, opd=opd, optl=optl):
                    steps = []
                    for bi in range(2):
                        sl = slice(bi * 512, (bi + 1) * 512)

                        def st1(bi=bi, sl=sl):
                            CP("dve", osb[0][:], opd[:, sl], [optl[bi]], [osb_t[0]])
                            CP("dve", obf[0][:, 0, :], opd[:, sl], [optl[bi]], [obf_t[0]])
                            ACT(obf[0][:, 1, :], opd[:, sl], AF.Square, [optl[bi]], [obf_t[0]])

                        def st2():
                            pd1, off1, pt1 = rbank()
                            MM(pd1[:, off1:off1 + 512], K.ones_bf[:], obf[0][:, 0, :], True, True, [K.ones_t, obf_t[0]], [pt1])
                            pd2, off2, pt2 = rbank()
                            MM(pd2[:, off2:off2 + 512], K.ones_bf[:], obf[0][:, 1, :], True, True, [K.ones_t, obf_t[0]], [pt2])
                            ACT(gnb[:], pd1[:, off1:off1 + 512], AF.Square, [pt1], [gnb_t], scale=1.0 / 128.0)
                            STT("dve", gnb[:], pd2[:, off2:off2 + 512], 1.0 / 128.0, gnb[:], ALU.mult, ALU.subtract,
                                [pt2, gnb_t], [gnb_t])
                            STT("dve", osb[0][:], pd1[:, off1:off1 + 512], -1.0 / 128.0, osb[0][:], ALU.mult, ALU.add,
                                [pt1, osb_t[0]], [osb_t[0]])

                        def st3():
                            ACT(gnb[:], gnb[:], AF.Ln, [gnb_t, K.epsc_t], [gnb_t], bias=K.epsc[:, 1:2])
                            ACT(gnb[:], gnb[:], AF.Exp, [gnb_t], [gnb_t], scale=-0.5)

                        def st4(sl=sl):
                            TTO("pool", gnb[:], gnb[:], gate[:, x, sl], ALU.mult, [gnb_t, gate_t[x]], [gnb_t])
                            STT("dve", concT[:, h, sl], osb[0][:], evs[:, 8 + h:9 + h], gnb[:], ALU.mult, ALU.mult,
                                [osb_t[0], evs_t, gnb_t], [concT_t[h]])

                        steps += [st1, st2, st3, st4]
                    return steps

                K.side.extend([(x, f_) for f_ in gn_steps()])
        K.side_flush()
        rst.close()
        S.fence()
        gst = contextlib.ExitStack()
        gs_ = lambda n, sh, dt=F32: K.sbs(gst, n, sh, dt)
        qa = gs_("qa", [128, 4, NH], BF16)
        qa_t = [S.tile(f"qa{c}") for c in range(4)]
        ka = gs_("ka", [128, 2, NH], BF16)
        ka_t = S.tile("ka")
        MSET("pool", ka[64:128, 0, :], 0.0, [ka_t])
        MSET("pool", ka[0:64, 1, :], 0.0, [ka_t])
        vaug = gs_("vaug", [128, 8, 256], BF16)
        vaug_t = [S.tile(f"vaug{t}") for t in range(8)]
        MSET("pool", vaug[:, :, 64:192], 1.0, vaug_t)
        rden = [gs_(f"rden{i}", [128, 512], F32) for i in range(2)]
        rden_t = [S.tile(f"rden{i}") for i in range(2)]
        if half == 0:
            kvs = [gs_(f"kvs{i}", [128, 256], F32) for i in range(2)]
            kvs_t = [S.tile(f"kvs{i}") for i in range(2)]
            sm = [gs_(f"sm{i}", [128, 4], F32) for i in range(2)]
            sm_t = [S.tile(f"sm{i}") for i in range(2)]
        else:
            vaugc = gs_("vaugc", [128, 2, 256], BF16)
            vaugc_t = S.tile("vaugc")
            MSET("pool", vaugc[:, :, 64:192], 1.0, [vaugc_t])
            kcr = gs_("kcr", [128, 2, 2, 64], F32)
            kcr_t = S.tile("kcr")
            kcT = gs_("kcT", [128, 2, 256], BF16)
            kcT_t = S.tile("kcT")
            MSET("pool", kcT[64:128, 0, :], 0.0, [kcT_t])
            MSET("pool", kcT[0:64, 1, :], 0.0, [kcT_t])
        K.ada_some(2)
        wC, wC_t = load_w(4096, 512)
        wD, wD_t = (load_w(4096 + 512, 512, 1) if rope else (None, None))

        def qk_fm(w, w_t, col, wsw, wsw_t, colsw, gcol, dst, dst_t, tb, sl, padded=False):
            def fin(fn):
                if not padded:
                    fn(dst[:, sl], slice(0, 128))
                else:
                    for x_ in range(2):
                        fn(dst[64 * x_:64 * x_ + 64, x_, sl], slice(64 * x_, 64 * x_ + 64))

            pd, off, ptile = proj_fm(w, w_t, col, tb)
            j = rot("pT", 4)
            ACT(pT[j][:], pd[:, off:off + 512], AF.Square, [ptile], [pT_t[j]])
            pds, offs, pts = rbank()
            MM(pds[:, offs:offs + 512], bones[:], pT[j][:], True, True, [bones_t, pT_t[j]], [pts])
            i0 = rot("tp", 3)
            ACT(rp[i0][:], pds[:, offs:offs + 512], AF.Ln, [pts, K.epsc_t], [rp_t[i0]], scale=1.0 / 64.0, bias=K.epsc[:, 0:1])
            ACT(rp[i0][:], rp[i0][:], AF.Exp, [rp_t[i0]], [rp_t[i0]], scale=-0.5)
            if not rope:
                fin(lambda o_, ph: STT("dve", o_, pd[ph, off:off + 512], evs[ph, gcol:gcol + 1], rp[i0][ph, :], ALU.mult, ALU.mult,
                                       [ptile, evs_t, rp_t[i0]], [dst_t]))
            else:
                pd2, off2, ptile2 = proj_fm(wsw, wsw_t, colsw, tb)
                i1, i2 = rot("tp", 3), rot("tp", 3)
                STT("dve", gtm[i1][:], pd[:, off:off + 512], evs[:, gcol:gcol + 1], COS[:, sl], ALU.mult, ALU.mult,
                    [ptile, evs_t, cs_t], [gtm_t[i1]])
                STT("dve", gtm[i2][:], pd2[:, off2:off2 + 512], evs[:, gcol + 1:gcol + 2], SIN[:, sl], ALU.mult, ALU.mult,
                    [ptile2, evs_t, cs_t], [gtm_t[i2]])
                TTO("pool", gtm[i1][:], gtm[i1][:], gtm[i2][:], ALU.add, [gtm_t[i1], gtm_t[i2]], [gtm_t[i1]])
                fin(lambda o_, ph: TTO("pool", o_, gtm[i1][ph, :], rp[i0][ph, :], ALU.mult, [gtm_t[i1], rp_t[i0]], [dst_t]))

        for bi in range(2):
            tb = tb0 + bi
            sl = slice(bi * 512, (bi + 1) * 512)
            for c in range(4):
                qk_fm(wC, wC_t, c * 128, wD, wD_t, c * 128, 16, qa[:, c, :], qa_t[c], tb, sl)
        wE, wE_t = load_w(4096 + 1024, 384)
        for bi in range(2):
            tb = tb0 + bi
            sl = slice(bi * 512, (bi + 1) * 512)
            qk_fm(wE, wE_t, 128, wE, wE_t, 0, 18, ka, ka_t, tb, sl, padded=True)
        for tt in range(8):
            tb = tb0 + tt // 4
            tsl = slice(T0 + tt * 128, T0 + (tt + 1) * 128)
            pd, off, ptile = rbank()
            for k in range(8):
                MM(pd[:, off:off + 256], hT[:, k, tsl], wE[:, k, 128:384], k == 0, k == 7, [wE_t, hT_t[tb]], [ptile])
            CP("act", vaug[:, tt, 0:64], pd[:, off + 128:off + 192], [ptile], [vaug_t[tt]])
            CP("act", vaug[:, tt, 192:256], pd[:, off + 192:off + 256], [ptile], [vaug_t[tt]])
            if half == 0:
                j = rot("kvs", 2)
                i1 = rot("tp", 3)
                si = rot("sm", 2)
                ACT(rp[i1][:, 0:128], pd[:, off:off + 128], AF.Square, [ptile], [rp_t[i1]])
                S.op("dve", lambda e, o=sm[si][:, 0:2], i_=rp[i1][:, 0:128].rearrange("p (h d) -> p h d", h=2):
                     e.tensor_reduce(out=o, in_=i_, axis=AX.X, op=ALU.add), [rp_t[i1]], [sm_t[si]])
                ACT(sm[si][:, 0:2], sm[si][:, 0:2], AF.Ln, [sm_t[si], K.epsc_t], [sm_t[si]], scale=1.0 / 64.0, bias=K.epsc[:, 0:1])
                ACT(sm[si][:, 0:2], sm[si][:, 0:2], AF.Exp, [sm_t[si]], [sm_t[si]], scale=-0.5)
                for kv in range(2):
                    STT("dve", kvs[j][:, kv * 64:(kv + 1) * 64], pd[:, off + kv * 64: off + (kv + 1) * 64],
                        sm[si][:, kv:kv + 1], evr[:, 16:80], ALU.mult, ALU.mult, [ptile, sm_t[si], evr_t], [kvs_t[j]])
                CP("dve", kvs[j][:, 128:256], pd[:, off + 128:off + 256], [ptile], [kvs_t[j]])
                sq_, t0 = tt // 2, (tt % 2) * 128
                DMA("sp", gk_out[sq_, :, t0:t0 + 128, :].rearrange("h t d -> t h d"),
                    kvs[j][:, 0:128].rearrange("p (h d) -> p h d", h=2), [kvs_t[j]], [], kvs_t[j])
                DMA("sp", gv_out[sq_, :, t0:t0 + 128, :].rearrange("h t d -> t h d"),
                    kvs[j][:, 128:256].rearrange("p (h d) -> p h d", h=2), [kvs_t[j]], [], kvs_t[j])
                if kvs_t[j] not in K.outs:
                    K.outs.append(kvs_t[j])
        if half == 1:
            for x in range(2):
                DMA("sp", kcr[:, :, x, :], cgk[x].rearrange("(t p) d -> p t d", p=128), [], [kcr_t], kcr_t)
                DMA("pool", bass.AP(tensor=vaugc, offset=192 * x, ap=[[512, 128], [256, 2], [1, 64]]),
                    cgv[x].rearrange("(t p) d -> p t d", p=128), [], [vaugc_t], vaugc_t)
            pd, off, ptile = rbank()
            for t in range(2):
                TR(pd[:, off + t * 128: off + (t + 1) * 128], kcr[:, t, :, :].rearrange("p h d -> p (h d)"), K.ident,
                   [kcr_t, K.cst_t], [ptile])
            CP("dve", kcT[0:64, 0, :], pd[0:64, off:off + 256], [ptile], [kcT_t])
            CP("dve", kcT[64:128, 1, :], pd[64:128, off:off + 256], [ptile], [kcT_t])
        blk = min(slen, 512)
        jobs = []
        for c in range(4):
            for x in range(2):
                grp = []
                for sq_ in range(nseq):
                    for ib in range(slen // blk):
                        keys = [("s", sq_ * (slen // 128) + jt) for jt in range(slen // 128)]
                        if half == 1:
                            keys += [("c", 0), ("c", 1)]
                        for ki, (kind, tt) in enumerate(keys):
                            grp.append([c, x, sq_ * slen + ib * blk, kind, tt, ki == 0, ki == len(keys) - 1, False])
                grp[-1][-1] = True
                jobs += [tuple(g) for g in grp]

        def s1(jb):
            c, x, c_lo, kind, tt, isfirst, islast, isend = jb
            b = 64 * x
            pd, off, ptile = rbank()
            if kind == "s":
                MM(pd[:, off:off + blk], ka[:, x, tt * 128:(tt + 1) * 128], qa[:, c, c_lo:c_lo + blk],
                   True, True, [ka_t, qa_t[c]], [ptile])
            else:
                MM(pd[:, off:off + blk], kcT[:, x, tt * 128:(tt + 1) * 128], qa[:, c, c_lo:c_lo + blk],
                   True, True, [kcT_t, qa_t[c]], [ptile])
            j = rot("pT", 4)
            ACT(pT[j][:, 0:blk], pd[:, off:off + blk], AF.Exp, [ptile], [pT_t[j]], scale=0.125)
            return j

        def s2(jb, j):
            c, x, c_lo, kind, tt, isfirst, islast, isend = jb
            opd = K.pd[x]
            optl = [K.pt[2 * x], K.pt[2 * x + 1]]
            ob = c_lo // 512
            if kind == "s":
                va, va_t = vaug[:, tt, x * 128:(x + 1) * 128], vaug_t[tt]
            else:
                va, va_t = vaugc[:, tt, x * 128:(x + 1) * 128], vaugc_t
            MM(opd[:, c_lo:c_lo + blk], va, pT[j][:, 0:blk], isfirst, islast, [va_t, pT_t[j]], [optl[ob]])
            if isend:
                jr = rot("rden", 2)
                dn = slice(64, 128) if x == 0 else slice(0, 64)
                obp = slice(0, 64) if x == 0 else slice(64, 128)
                for bi in range(2):
                    sl = slice(bi * 512, (bi + 1) * 512)
                    RCP(rden[jr][obp, :], opd[dn, sl], [optl[bi]], rden_t[jr])
                    TTO("dve", concT[obp, 8 + c, sl], opd[obp, sl], rden[jr][obp, :], ALU.mult, [optl[bi], rden_t[jr]],
                        [concT_t[8 + c]])

        pipeline(jobs, s1, s2, 3)
        gst.close()
        for dq in range(4):
            w, w_t = K.wnext("even_w_out", (slice(None), slice(None), slice(dq * 256, (dq + 1) * 256)), 12, 256)
            for dc in range(2):
                dmc = dq * 2 + dc
                for bi in range(2):
                    tb = tb0 + bi
                    pd, off, ptile = rbank()
                    for c in range(12):
                        MM(pd[:, off:off + 512], w[:, c, dc * 128:(dc + 1) * 128], concT[:, c, bi * 512:(bi + 1) * 512],
                           c == 0, c == 11, [w_t, concT_t[c]], [ptile])
                    STT("dve", xT[:, dmc, tb * 512:(tb + 1) * 512], pd[:, off:off + 512], K.mod[:, l, 2, dmc, half:half + 1],
                        xT[:, dmc, tb * 512:(tb + 1) * 512], ALU.mult, ALU.add, [ptile, K.mod_t[l], xT_t[tb]], [xT_t[tb]])
        hst.close()
        S.fence()
    K.bank_alloc = K.next_bank
    st.close()
    S.fence()


def pipeline(jobs, stage1, stage2, depth=2, side=None):
    pend = []
    for jb in jobs:
        pend.append((jb, stage1(jb)))
        if len(pend) > depth:
            j0, h0 = pend.pop(0)
            stage2(j0, h0)
            if side:
                side.pop(0)[1]()
    for j0, h0 in pend:
        stage2(j0, h0)
        if side:
            side.pop(0)[1]()


def na_query_rows(kr):
    rows = [r for r in range(16) if min(max(r - 4, 0), 8) <= kr <= min(max(r - 4, 0), 8) + 7]
    return rows[0], rows[-1]


def odd_mixer(K, l):
    nc, S = K.nc, K.S
    MM, TR, ACT, TTO, STT, TS, CP, MSET, DMA = K.MM, K.TR, K.ACT, K.TTO, K.STT, K.TS, K.CP, K.MSET, K.DMA
    RCP = K.RCP
    hT, hT_t, xT, xT_t = K.hT, K.hT_t, K.xT, K.xT_t
    rpbx = K.din("rpbx", [16, 128, 15, 64])
    cmask_d = K.din("colmask", [128, 64])
    ck = K.din("cache_na_k", [16, 256, 64])
    cv = K.din("cache_na_v", [16, 256, 64])
    nk_out = K.dout("nk_out", [4, 16, 256, 64])
    nv_out = K.dout("nv_out", [4, 16, 256, 64])

    K.norm_mod(l, 0, [0, 1, 2, 3])
    st = contextlib.ExitStack()
    sbs = lambda n, sh, dt=F32: K.sbs(st, n, sh, dt)
    concT = sbs("concT", [128, 8, NT], BF16)
    concT_t = [[S.tile(f"conc{c}_{b}") for b in range(5)] for c in range(8)]
    qT = sbs("qT", [128, NT], BF16)
    kT = sbs("kT", [128, 2, NT], BF16)
    qT_t = [S.tile(f"qT{b}") for b in range(4)]
    kT_t = [S.tile(f"kT{b}") for b in range(4)]
    vaug = sbs("vaug", [128, 16, 256], BF16)
    vaug_t = [S.tile(f"vaug{t}") for t in range(16)]
    vaugc = sbs("vaugc", [128, 2, 256], BF16)
    vaugc_t = S.tile("vaugc")
    kcr = sbs("kcr", [128, 2, 2, 64], F32)
    kcr_t = S.tile("kcr")
    kcT = sbs("kcT", [128, 2, 256], BF16)
    kcT_t = S.tile("kcT")
    cmask = sbs("cmask", [128, 64], F32)
    cmask_t = S.tile("cmask")
    bt = [sbs(f"bt{i}", [128, 16, 64], F32) for i in range(2)]
    bt_t = [S.tile(f"bt{i}") for i in range(2)]
    pT = [sbs(f"pT{i}", [128, 512], BF16) for i in range(4)]
    pT_t = [S.tile(f"pT{i}") for i in range(4)]
    tmp = [sbs(f"tmp{i}", [128, 512], F32) for i in range(2)]
    tmp_t = [S.tile(f"tmp{i}") for i in range(2)]
    rden = [sbs("rden0", [128, 512], F32)] * 2
    rden_t = [S.tile("rden0")] * 2
    kvs = [sbs(f"kvs{i}", [128, 256], F32) for i in range(2)]
    kvs_t = [S.tile(f"kvs{i}") for i in range(2)]
    rr = dict(p=0, t=0, r=0, k=0, b=4)

    def rot(key, n):
        i = rr.get(key, 0) % n
        rr[key] = rr.get(key, 0) + 1
        return i

    def rbank():
        i = 4 + rot("b", 4)
        return K.bank(i)

    DMA("sp", cmask[:], cmask_d, [], [cmask_t], cmask_t)
    MSET("pool", kT[64:128, 0, :], 0.0, kT_t)
    MSET("pool", kT[0:64, 1, :], 0.0, kT_t)
    MSET("pool", kcT[64:128, 0, :], 0.0, [kcT_t])
    MSET("pool", kcT[0:64, 1, :], 0.0, [kcT_t])
    MSET("pool", vaug[:, :, 64:192], 1.0, vaug_t)
    MSET("pool", vaugc[:, :, 64:192], 1.0, [vaugc_t])

    for hp in range(8):
        w, w_t = K.wnext("odd_w_in", (slice(None), slice(None), slice(hp * 384, (hp + 1) * 384)), 8, 384)
        for x in range(2):
            DMA("sp", bt[x][0:64, 0:15, :], rpbx[2 * hp + x, 0:64], [], [bt_t[x]], bt_t[x])
            DMA("sp", bt[x][64:128, 1:16, :], rpbx[2 * hp + x, 64:128], [], [bt_t[x]], bt_t[x])
            TTO("pool", bt[x][0:64, 0:15, :], bt[x][0:64, 0:15, :], cmask[0:64, :].unsqueeze(1).broadcast_to([64, 15, 64]),
                ALU.add, [bt_t[x], cmask_t], [bt_t[x]])
            TTO("pool", bt[x][64:128, 1:16, :], bt[x][64:128, 1:16, :],
                cmask[64:128, :].unsqueeze(1).broadcast_to([64, 15, 64]), ALU.add, [bt_t[x], cmask_t], [bt_t[x]])
        for x in range(2):
            DMA("sp", kcr[:, :, x, :], ck[2 * hp + x].rearrange("(t p) d -> p t d", p=128), [], [kcr_t], kcr_t)
            DMA("pool", bass.AP(tensor=vaugc, offset=192 * x, ap=[[512, 128], [256, 2], [1, 64]]),
                cv[2 * hp + x].rearrange("(t p) d -> p t d", p=128), [], [vaugc_t], vaugc_t)
        if K.cfg.get("ostop", 9) <= 1:
            continue
        for tb in range(4):
            sl = slice(tb * 512, (tb + 1) * 512)
            for which, dst, dst_t in ((0, qT, qT_t), (1, kT, kT_t)):
                pd, off, ptile = rbank()
                for k in range(8):
                    MM(pd[:, off:off + 512], w[:, k, which * 128:(which + 1) * 128], hT[:, k, sl], k == 0, k == 7,
                       [w_t, hT_t[tb]], [ptile])
                if which == 0:
                    ACT(dst[:, sl], pd[:, off:off + 512], AF.Copy, [ptile], [dst_t[tb]], scale=0.125)
                else:
                    CP("dve", dst[0:64, 0, sl], pd[0:64, off:off + 512], [ptile], [dst_t[tb]])
                    CP("dve", dst[64:128, 1, sl], pd[64:128, off:off + 512], [ptile], [dst_t[tb]])
        if K.cfg.get("ostop", 9) <= 2:
            continue
        for tt in range(16):
            tb = tt // 4
            pd, off, ptile = rbank()
            c0 = 128 if tt < 8 else 256
            ncol = 256 if tt < 8 else 128
            for k in range(8):
                MM(pd[:, off:off + ncol], hT[:, k, tt * 128:(tt + 1) * 128], w[:, k, c0:384], k == 0, k == 7,
                   [w_t, hT_t[tb]], [ptile])
            vo = off + ncol - 128
            CP("act", vaug[:, tt, 0:64], pd[:, vo:vo + 64], [ptile], [vaug_t[tt]])
            CP("act", vaug[:, tt, 192:256], pd[:, vo + 64:vo + 128], [ptile], [vaug_t[tt]])
            nd = K.cfg.get("nodma", 0)
            if tt < 8 and nd != 1:
                j = rot("k", 2)
                CP(K.cfg.get("kveng", "dve") if isinstance(K.cfg.get("kveng", "dve"), str) else ("act" if K.cfg["kveng"] else "dve"), kvs[j][:], pd[:, off:off + 256], [ptile], [kvs_t[j]])
                if nd == 2:
                    continue
                sq_, t0 = tt // 2, (tt % 2) * 128
                DMA("sp", nk_out[sq_, 2 * hp:2 * hp + 2, t0:t0 + 128, :].rearrange("h t d -> t h d"),
                    kvs[j][:, 0:128].rearrange("p (h d) -> p h d", h=2), [kvs_t[j]], [], kvs_t[j])
                DMA("sp", nv_out[sq_, 2 * hp:2 * hp + 2, t0:t0 + 128, :].rearrange("h t d -> t h d"),
                    kvs[j][:, 128:256].rearrange("p (h d) -> p h d", h=2), [kvs_t[j]], [], kvs_t[j])
                if kvs_t[j] not in K.outs:
                    K.outs.append(kvs_t[j])
        if K.cfg.get("ostop", 9) <= 3:
            continue
        pd, off, ptile = rbank()
        for t in range(2):
            TR(pd[:, off + t * 128: off + (t + 1) * 128], kcr[:, t, :, :].rearrange("p h d -> p (h d)"), K.ident,
               [kcr_t, K.cst_t], [ptile])
        CP("dve", kcT[0:64, 0, :], pd[0:64, off:off + 256], [ptile], [kcT_t])
        CP("dve", kcT[64:128, 1, :], pd[64:128, off:off + 256], [ptile], [kcT_t])
        jobs = []
        for sq_ in range(4):
            for x in range(2):
                jobs.append(("p", sq_, x))
        for x in range(2):
            nj = []
            for kt in range(2):
                for qb in range(2):
                    nj.append(["c", x, kt, qb, qb * 512, (qb + 1) * 512])
            for m in range(8):
                a0, b0 = na_query_rows(2 * m)
                a1, b1 = na_query_rows(2 * m + 1)
                a, bq = min(a0, a1), max(b0, b1)
                lo, hi = a * 64, (bq + 1) * 64
                if lo < 512:
                    nj.append(["w", x, m, 0, lo, min(hi, 512)])
                if hi > 512:
                    nj.append(["w", x, m, 1, max(lo, 512), hi])
            last, first = {}, {}
            for ji, jb in enumerate(nj):
                last[jb[3]] = ji
                first.setdefault(jb[3], ji)
            for ji, jb in enumerate(nj):
                jobs.append(tuple(jb) + (ji == first[jb[3]], ji == last[jb[3]], ji == len(nj) - 1))

        def s1(jb):
            if jb[0] == "p":
                _, sq_, x = jb
                b = 64 * x
                t0 = sq_ * 256
                tb = sq_ // 2
                pd, off, ptile = rbank()
                for kt in range(2):
                    MM(pd[:, off + kt * 256: off + (kt + 1) * 256], kT[:, x, t0 + kt * 128: t0 + (kt + 1) * 128],
                       qT[:, t0:t0 + 256], True, True, [kT_t[tb], qT_t[tb]], [ptile])
                j = rot("p", 4)
                ACT(pT[j][:], pd[:, off:off + 512], AF.Exp, [ptile], [pT_t[j]])
                return j
            kind, x, ka, qb, lo, hi = jb[:6]
            b = 64 * x
            n = hi - lo
            pd, off, ptile = rbank()
            j = rot("p", 4)
            if kind == "c":
                MM(pd[:, off:off + 512], kcT[:, x, ka * 128:(ka + 1) * 128], qT[:, 1024 + lo:1024 + hi],
                   True, True, [kcT_t, qT_t[2 + qb]], [ptile])
                ACT(pT[j][:], pd[:, off:off + 512], AF.Exp, [ptile], [pT_t[j]])
            else:
                m = ka
                s0 = 7 - 2 * m + lo // 64
                MM(pd[:, off:off + n], kT[:, x, 1024 + m * 128:1024 + (m + 1) * 128],
                   qT[:, 1024 + lo:1024 + hi], True, True, [kT_t[2 + m // 4], qT_t[2 + qb]], [ptile])
                i2 = rot("t", 2)
                TTO("dve", tmp[i2][:, 0:n], pd[:, off:off + n],
                    bt[x][:, s0:s0 + n // 64, :].rearrange("p s c -> p (s c)"), ALU.add,
                    [ptile, bt_t[x]], [tmp_t[i2]])
                for par in range(2):
                    a_, b_ = na_query_rows(2 * m + par)
                    for r in range(lo // 64, hi // 64):
                        if r < a_ or r > b_:
                            c_ = (r - lo // 64) * 64
                            MSET("dve", tmp[i2][64 * par:64 * par + 64, c_:c_ + 64], -30000.0, [tmp_t[i2]])
                ACT(pT[j][:, 0:n], tmp[i2][:, 0:n], AF.Exp, [tmp_t[i2]], [pT_t[j]])
            return j

        def s2(jb, j):
            if jb[0] == "p":
                _, sq_, x = jb
                t0 = sq_ * 256
                opd, ooff, optile = K.bank(sq_ % 4)
                for kt in range(2):
                    tt = sq_ * 2 + kt
                    MM(opd[:, ooff + x * 256: ooff + (x + 1) * 256], vaug[:, tt, x * 128:(x + 1) * 128],
                       pT[j][:, kt * 256:(kt + 1) * 256], kt == 0, kt == 1, [vaug_t[tt], pT_t[j]], [optile])
                if x == 1:
                    jr = rot("r", 2)
                    RCP(rden[jr][0:64, 0:256], opd[64:128, ooff:ooff + 256], [optile], rden_t[jr])
                    RCP(rden[jr][64:128, 0:256], opd[0:64, ooff + 256:ooff + 512], [optile], rden_t[jr])
                    TTO("dve", concT[0:64, hp, t0:t0 + 256], opd[0:64, ooff:ooff + 256], rden[jr][0:64, 0:256], ALU.mult,
                        [optile, rden_t[jr]], [concT_t[hp][sq_]])
                    TTO("dve", concT[64:128, hp, t0:t0 + 256], opd[64:128, ooff + 256:ooff + 512], rden[jr][64:128, 0:256],
                        ALU.mult, [optile, rden_t[jr]], [concT_t[hp][sq_]])
                return
            kind, x, ka, qb, lo, hi, isfirst, islast, isend = jb
            n = hi - lo
            opd = K.pd[x]
            optl = [K.pt[2 * x], K.pt[2 * x + 1]]
            if kind == "c":
                MM(opd[:, lo:hi], vaugc[:, ka, x * 128:(x + 1) * 128], pT[j][:], isfirst, islast,
                   [vaugc_t, pT_t[j]], [optl[qb]])
            else:
                tt = 8 + ka
                MM(opd[:, lo:hi], vaug[:, tt, x * 128:(x + 1) * 128], pT[j][:, 0:n],
                   isfirst, islast, [vaug_t[tt], pT_t[j]], [optl[qb]])
            if isend:
                jr = rot("r", 2)
                for qb2 in range(2):
                    sl = slice(qb2 * 512, (qb2 + 1) * 512)
                    dn = slice(64, 128) if x == 0 else slice(0, 64)
                    ob = slice(0, 64) if x == 0 else slice(64, 128)
                    RCP(rden[jr][ob, :], opd[dn, sl], [optl[qb2]], rden_t[jr])
                    TTO("dve", concT[ob, hp, 1024 + qb2 * 512:1024 + (qb2 + 1) * 512], opd[ob, sl], rden[jr][ob, :],
                        ALU.mult, [optl[qb2], rden_t[jr]], [concT_t[hp][4]])

        pipeline(jobs, s1, s2, 3)
    for dh in range(2 if K.cfg.get("ostop", 9) > 5 else 0):
        w, w_t = K.wnext("odd_w_out", (slice(None), slice(None), slice(dh * 512, (dh + 1) * 512)), 8, 512)
        for dc in range(4):
            dmc = dh * 4 + dc
            for tb in range(4):
                g = 0 if tb < 2 else 1
                pd, off, ptile = rbank()
                for c in range(8):
                    rd = [concT_t[c][2 * tb], concT_t[c][2 * tb + 1]] if tb < 2 else [concT_t[c][4]]
                    MM(pd[:, off:off + 512], w[:, c, dc * 128:(dc + 1) * 128], concT[:, c, tb * 512:(tb + 1) * 512],
                       c == 0, c == 7, [w_t] + rd, [ptile])
                STT("dve", xT[:, dmc, tb * 512:(tb + 1) * 512], pd[:, off:off + 512], K.mod[:, l, 2, dmc, g:g + 1],
                    xT[:, dmc, tb * 512:(tb + 1) * 512], ALU.mult, ALU.add, [ptile, K.mod_t[l], xT_t[tb]], [xT_t[tb]])
    st.close()
    S.fence()


def fm(v):
    v = np.asarray(v, np.float32)
    r = v.reshape(*v.shape[:-1], v.shape[-1] // 128, 128)
    r = np.moveaxis(r, -1, 0)
    return np.ascontiguousarray(r)


def wl(W):
    Kd, N = W.shape
    return np.ascontiguousarray(W.reshape(Kd // 128, 128, N).transpose(1, 0, 2))


_PROG = {}


def prep_shared(inp, cfg):
    sh = {}
    sh["ada_w"] = np.stack([wl(inp["ada_w"][l]) for l in range(2)])
    ada_b = np.stack([inp["ada_b"][l].reshape(48, 128).T for l in range(2)], 1).reshape(128, 96)
    gains = np.stack([inp["norm_mix"][0], inp["norm_mix"][1], inp["norm_ffn"][0], inp["norm_ffn"][1],
                      inp["norm_final"]])
    sh["smallp"] = np.ascontiguousarray(np.concatenate([ada_b, fm(gains).reshape(128, 40)], 1), np.float32)
    if cfg["ffn"]:
        perm = np.concatenate([np.concatenate([np.arange(c * 128, (c + 1) * 128),
                                               DFF + np.arange(c * 128, (c + 1) * 128)]) for c in range(NFC)])
        sh["w_up"] = np.stack([wl(inp["ffn_w_up"][l][:, perm]) for l in range(2)])
        sh["w_dn"] = np.stack([wl(inp["ffn_w_down"][l]) for l in range(2)])
        cp = np.zeros((128, 2, 4, 2 * NFC), np.float32)
        for l in range(2):
            for j in range(4):
                v = inp["ffn_conv_w"][l][j] if j < 3 else inp["ffn_conv_b"][l]
                vp = v[perm].reshape(2 * NFC, 128).T
                cp[:, l, j, :] = vp
        sh["convp"] = cp
    if cfg["even"]:
        Wi = np.asarray(inp["even_w_in"][0], np.float32)
        QR, KR, VR, GR, QA, KA, VA = 0, 512, 1024, 2048, 3072, 3584, 3712

        def swp(base, h):
            d = np.arange(64)
            sw = np.where((d % 32) < 16, d + 16, d - 16)
            return base + h * 64 + sw

        cols = []
        for hp in range(4):
            h0, h1 = 2 * hp, 2 * hp + 1
            cols += [QR + h0 * 64 + np.arange(64), QR + h1 * 64 + np.arange(64)]
            cols += [KR + h0 * 64 + np.arange(64), KR + h1 * 64 + np.arange(64)]
            cols += [swp(QR, h0), swp(QR, h1), swp(KR, h0), swp(KR, h1)]
            cols += [GR + h0 * 128 + np.arange(128), GR + h1 * 128 + np.arange(128)]
            cols += [VR + h0 * 128 + np.arange(128), VR + h1 * 128 + np.arange(128)]
        for c in range(4):
            cols += [QA + c * 64 + np.arange(64), QA + (c + 4) * 64 + np.arange(64)]
        for c in range(4):
            cols += [swp(QA, c), swp(QA, c + 4)]
        cols += [swp(KA, 0), swp(KA, 1), KA + np.arange(128), VA + np.arange(128)]
        cols = np.concatenate(cols)
        assert cols.shape[0] == 4 * 1024 + 512 + 512 + 384
        sh["even_w_in"] = wl(Wi[:, cols])
        Wo = np.asarray(inp["even_w_out"][0], np.float32)
        rows = [np.arange(1024)]
        for c in range(4):
            rows += [1024 + c * 64 + np.arange(64), 1024 + (c + 4) * 64 + np.arange(64)]
        sh["even_w_out"] = wl(Wo[np.concatenate(rows)])
        es = np.zeros((128, 24), np.float32)
        df, db = np.asarray(inp["ret_decay_fwd"][0]), np.asarray(inp["ret_decay_bwd"][0])
        for hp in range(4):
            for x in range(2):
                es[64 * x:64 * x + 64, hp * 2 + 0] = df[2 * hp + x]
                es[64 * x:64 * x + 64, hp * 2 + 1] = db[2 * hp + x]
        es[:, 8:16] = np.asarray(inp["ret_gn"][0]).reshape(8, 128).T
        d = np.arange(64)
        sw = np.where((d % 32) < 16, d + 16, d - 16)
        gq, gk = np.asarray(inp["gqa_q_norm"][0]), np.asarray(inp["gqa_k_norm"][0])
        es[:, 16] = np.concatenate([gq, gq]); es[:, 17] = np.concatenate([gq[sw], gq[sw]])
        es[:, 18] = np.concatenate([gk, gk]); es[:, 19] = np.concatenate([gk[sw], gk[sw]])
        sh["evsmall"] = es
        er = np.zeros((128, 84), np.float32)
        er[:, 0:8] = df[None, :]; er[:, 8:16] = db[None, :]
        er[:, 16:80] = gk[None, :]
        p = np.arange(128)
        er[:, 80] = 255 - p; er[:, 81] = 255 - (128 + p); er[:, 82] = p; er[:, 83] = 128 + p
        sh["evrow"] = er
        tabs = np.zeros((128, 3 * 1024 + 256), np.float32)
        tabs[:, 0:1024] = (np.arange(1024) - 512)[None, :]
        t = np.arange(1024)
        row = (t // 64).astype(np.float32); col = (t % 64).astype(np.float32)
        inv = (10000.0 ** (-np.arange(0, 32, 2, dtype=np.float32) / 32.0)).astype(np.float32)
        ang_r = (row[:, None] * inv[None, :]).astype(np.float32)
        ang_c = (col[:, None] * inv[None, :]).astype(np.float32)
        cosd = np.zeros((64, 1024), np.float32); sind = np.zeros((64, 1024), np.float32)
        for dd in range(64):
            ang = ang_r if dd < 32 else ang_c
            i = dd % 16
            cosd[dd] = np.cos(ang[:, i])
            sind[dd] = np.sin(ang[:, i]) * (-1.0 if (dd % 32) < 16 else 1.0)
        tabs[:, 1280:2304] = np.concatenate([cosd, cosd], 0)
        tabs[:, 2304:3328] = np.concatenate([sind, sind], 0)
        jj = np.arange(128)[:, None]; ii = np.arange(128)[None, :]
        tabs[:, 1024:1152] = np.maximum(jj - ii, 0).astype(np.float32)
        sh["evtab"] = tabs
    if cfg["odd"]:
        Wi = inp["odd_w_in"][0]
        cols = []
        for hp in range(8):
            for part in range(3):
                cols.append(part * 1024 + hp * 128 + np.arange(128))
        sh["odd_w_in"] = wl(Wi[:, np.concatenate(cols)])
        sh["odd_w_out"] = wl(inp["odd_w_out"][0])
        rpb = np.asarray(inp["na_rpb"][0], np.float32)
        kc = np.arange(64)[:, None]
        cc = np.arange(64)[None, :]
        relc = kc - cc + 15
        okc = (relc >= 0) & (relc <= 30)
        relc_c = np.clip(relc, 0, 30)
        rx = np.zeros((16, 128, 15, 64), np.float32)
        for s_ in range(15):
            g = np.where(okc[None], rpb[:, 14 - s_, :][:, relc_c], np.float32(0.0))
            rx[:, 0:64, s_, :] = g
            rx[:, 64:128, s_, :] = g
        sh["rpbx"] = rx
        cs = np.clip(np.arange(64) - 8, 0, 48)[None, :]
        win = (kc >= cs) & (kc < cs + 16)
        cm = np.where(win, 0.0, -30000.0).astype(np.float32)
        sh["colmask"] = np.ascontiguousarray(np.concatenate([cm, cm], 0))
    return sh


def prep_core(inp, i, cfg):
    m = {}
    xp = np.asarray(inp["x_prompt"][4 * i:4 * i + 4], np.float32).reshape(1024, D)
    xsm = np.asarray(inp["x_sample"][i], np.float32)
    m["x_tok"] = np.ascontiguousarray(np.concatenate([xp, xsm], 0))
    cond = np.stack([inp["c_ctx"], inp["c"][i]], -1)
    condT = cond.reshape(8, 128, 2).transpose(1, 0, 2).reshape(128, 16)
    m["consts"] = np.ascontiguousarray(np.concatenate([np.eye(128, dtype=np.float32), condT], 1), np.float32)
    if cfg["even"]:
        m["cache_gqa_k"] = np.ascontiguousarray(inp["cache_gqa_k"][i, 0], np.float32)
        m["cache_gqa_v"] = np.ascontiguousarray(inp["cache_gqa_v"][i, 0], np.float32)
        m["state_ret_fwd"] = np.ascontiguousarray(inp["state_ret_fwd"][i, 0], np.float32)
        m["state_ret_bwd"] = np.ascontiguousarray(inp["state_ret_bwd"][i, 0], np.float32)
    if cfg["odd"]:
        m["cache_na_k"] = np.ascontiguousarray(inp["cache_na_k"][i, 0], np.float32)
        m["cache_na_v"] = np.ascontiguousarray(inp["cache_na_v"][i, 0], np.float32)
    return m


def run(inp, cfg):
    key = tuple(sorted(cfg.items()))
    if key not in _PROG:
        plan = build_program(cfg, None)
        _PROG[key] = build_program(cfg, plan)
    nc = _PROG[key]
    inp = {k: np.asarray(v) for k, v in inp.items()}
    sh = prep_shared(inp, cfg)
    in_maps = []
    for i in range(8):
        m = dict(sh)
        m.update(prep_core(inp, i, cfg))
        in_maps.append(m)
    ncores = cfg.get("ncores", 8)
    res = run_bass_kernel_spmd(nc, in_maps[:ncores], core_ids=list(range(ncores)))
    return res.results


def kernel(**inputs):
    cfg = dict(CFG_DEFAULT)
    r = run(inputs, cfg)
    y = np.stack([r[i]["y_out"] for i in range(8)])
    y_prompt = np.ascontiguousarray(y[:, :1024].reshape(32, 256, D))
    y_sample = np.ascontiguousarray(y[:, 1024:])

    def cat(name):
        return np.ascontiguousarray(np.concatenate([r[i][name] for i in range(8)], 0)[:, None])

    return (y_prompt, y_sample, cat("sf_out"), cat("sb_out"), cat("gk_out"), cat("gv_out"),
            cat("nk_out"), cat("nv_out"))
```
